# Optimizing a Trainium2 kernel written in Bass

```python
import math
import jax
import jax.numpy as jnp
from jax import lax
import numpy as np

D_MODEL = 1024
BATCH = 16
SEQ = 2048
DEPTH = 2

A_HEADS = 4
A_HEAD_DIM = D_MODEL // A_HEADS
A_WIDTH = A_HEADS * A_HEAD_DIM
A_CHUNK = 128
B_WIDTH = D_MODEL
B_CONV = 31
AB_IN = 4 * A_WIDTH + 2 * A_HEADS + 2 * B_WIDTH
AB_OUT = A_WIDTH + B_WIDTH
C_PATTERNS = ((128, 1), (512, 4), (2048, 16))
C_HEADS = 4
C_HEAD_DIM = 128
C_BLOCK = 128
ROPE_THETA = 10000.0
C_IN = len(C_PATTERNS) * 3 * C_HEADS * C_HEAD_DIM
C_OUT = C_HEADS * C_HEAD_DIM
D_FF = 2816
FFN_CONV = 3
EPS = 1e-6
N_EVEN = (DEPTH + 1) // 2
N_ODD = DEPTH // 2

kernel_name = "hybrid_mlstm_conformer_dilated_attn_trunk"


def _rms_norm(x, g):
    x32 = x.astype(jnp.float32)
    y = x32 * lax.rsqrt(jnp.mean(jnp.square(x32), axis=-1, keepdims=True) + EPS)
    return (y * g.astype(jnp.float32)).astype(x.dtype)


def _layer_norm(x, g, b):
    x32 = x.astype(jnp.float32)
    mu = jnp.mean(x32, axis=-1, keepdims=True)
    var = jnp.mean(jnp.square(x32 - mu), axis=-1, keepdims=True)
    y = (x32 - mu) * lax.rsqrt(var + EPS) * g.astype(jnp.float32) + b.astype(jnp.float32)
    return y.astype(x.dtype)


def _causal_dwconv(x, w, b):
    k_width = w.shape[0]
    y = lax.conv_general_dilated(
        x, w[:, None, :].astype(x.dtype), window_strides=(1,), padding=[(k_width - 1, 0)],
        dimension_numbers=("NWC", "WIO", "NWC"), feature_group_count=x.shape[-1])
    return y + b.astype(x.dtype)


def _rope(x, cos, sin):
    x32 = x.astype(jnp.float32)
    x1, x2 = jnp.split(x32, 2, axis=-1)
    c = cos[:, None, :]
    s = sin[:, None, :]
    return jnp.concatenate([x1 * c - x2 * s, x1 * s + x2 * c], axis=-1).astype(x.dtype)


def _mlstm_chunkwise(q, k, v, i_pre, f_pre):
    bsz, seq, nh, hd = q.shape
    nc = seq // A_CHUNK

    def to_chunks(t):
        return t.astype(jnp.float32).reshape(bsz, nc, A_CHUNK, nh, hd).transpose(1, 0, 3, 2, 4)

    def gate_chunks(t):
        return t.reshape(bsz, nc, A_CHUNK, nh).transpose(1, 0, 3, 2)

    qc = to_chunks(q)
    kc = to_chunks(k) * (hd ** -0.5)
    vc = to_chunks(v)
    log_f = gate_chunks(jax.nn.log_sigmoid(f_pre))
    b_cum = jnp.cumsum(log_f, axis=-1)
    i_log = gate_chunks(i_pre)
    causal = jnp.tril(jnp.ones((A_CHUNK, A_CHUNK), dtype=bool))

    def step(carry, xs):
        c_st, n_st, m_st = carry
        qx, kx, vx, bx, ix = xs
        d_log = jnp.where(causal, bx[..., :, None] - bx[..., None, :] + ix[..., None, :], -jnp.inf)
        inter = bx + m_st[..., None]
        m_t = jnp.maximum(inter, jnp.max(d_log, axis=-1))
        w_intra = jnp.exp(d_log - m_t[..., None]) * jnp.einsum("bhtd,bhsd->bhts", qx, kx)
        e_inter = jnp.exp(inter - m_t)
        num = e_inter[..., None] * jnp.einsum("bhvd,bhtd->bhtv", c_st, qx) + jnp.einsum("bhts,bhsv->bhtv", w_intra, vx)
        den = e_inter * jnp.einsum("bhd,bhtd->bht", n_st, qx) + jnp.sum(w_intra, axis=-1)
        h = num / jnp.maximum(jnp.abs(den), jnp.exp(-m_t))[..., None]
        b_last = bx[..., -1]
        w_log = b_last[..., None] - bx + ix
        m_new = jnp.maximum(b_last + m_st, jnp.max(w_log, axis=-1))
        decay = jnp.exp(b_last + m_st - m_new)
        w_state = jnp.exp(w_log - m_new[..., None])
        c_new = decay[..., None, None] * c_st + jnp.einsum("bhsv,bhsd->bhvd", w_state[..., None] * vx, kx)
        n_new = decay[..., None] * n_st + jnp.einsum("bhs,bhsd->bhd", w_state, kx)
        return (c_new, n_new, m_new), h

    init = (jnp.zeros((bsz, nh, hd, hd), jnp.float32),
            jnp.zeros((bsz, nh, hd), jnp.float32),
            jnp.zeros((bsz, nh), jnp.float32))
    _, h = lax.scan(step, init, (qc, kc, vc, b_cum, i_log))
    return h.transpose(1, 0, 3, 2, 4).reshape(bsz, seq, nh, hd)


def _mlstm_conv_mixer(h, w_in, i_bias, f_bias, head_norm, conv_w, conv_b, ln_g, ln_b, w_out):
    bsz, seq, _ = h.shape
    proj = h @ w_in
    cuts = [A_WIDTH, 2 * A_WIDTH, 3 * A_WIDTH, 4 * A_WIDTH, 4 * A_WIDTH + A_HEADS,
            4 * A_WIDTH + 2 * A_HEADS, 4 * A_WIDTH + 2 * A_HEADS + B_WIDTH]
    q, k, v, o, ig, fg, glu_a, glu_g = jnp.split(proj, cuts, axis=-1)
    heads = lambda t: t.reshape(bsz, seq, A_HEADS, A_HEAD_DIM)
    ig = ig.astype(jnp.float32) + i_bias.astype(jnp.float32)
    fg = fg.astype(jnp.float32) + f_bias.astype(jnp.float32)
    hm = _mlstm_chunkwise(heads(q), heads(k), heads(v), ig, fg)
    hm = hm * lax.rsqrt(jnp.mean(jnp.square(hm), axis=-1, keepdims=True) + EPS)
    hm = hm * head_norm.astype(jnp.float32).reshape(A_HEADS, A_HEAD_DIM)
    hm = (hm.reshape(bsz, seq, A_WIDTH) * jax.nn.sigmoid(o.astype(jnp.float32))).astype(h.dtype)
    u = glu_a * jax.nn.sigmoid(glu_g)
    u = _causal_dwconv(u, conv_w, conv_b)
    u = jax.nn.silu(_layer_norm(u, ln_g, ln_b))
    return jnp.concatenate([hm, u], axis=-1) @ w_out


def _fold(t, d):
    bsz, seq = t.shape[:2]
    rest = t.shape[2:]
    t = jnp.moveaxis(t.reshape((bsz, seq // d, d) + rest), 2, 1)
    return t.reshape((bsz * d, seq // d) + rest)


def _unfold(t, bsz, d):
    n, length = t.shape[:2]
    rest = t.shape[2:]
    t = jnp.moveaxis(t.reshape((bsz, d, length) + rest), 1, 2)
    return t.reshape((bsz, length * d) + rest)


def _window_attention(q, k, v, span):
    n, length, nh, hd = q.shape
    bq = math.gcd(length, C_BLOCK)
    nb = length // bq
    kw = bq + span
    pad = ((0, 0), (span, 0), (0, 0), (0, 0))
    idx = jnp.arange(nb)[:, None] * bq + jnp.arange(kw)[None, :]
    kb = jnp.pad(k, pad)[:, idx]
    vb = jnp.pad(v, pad)[:, idx]
    qb = q.reshape(n, nb, bq, nh, hd)
    s = jnp.einsum("nbqhd,nbkhd->nbhqk", qb, kb).astype(jnp.float32)
    qi = jnp.arange(bq)[None, :, None]
    kj = jnp.arange(kw)[None, None, :]
    blk = jnp.arange(nb)[:, None, None]
    dist = qi - kj + span
    valid = (dist >= 0) & (dist <= span) & (blk * bq + kj >= span)
    s = jnp.where(valid[None, :, None], s, -jnp.inf)
    m = jnp.max(s, axis=-1)
    p = jnp.exp(s - m[..., None])
    den = jnp.sum(p, axis=-1)
    num = jnp.einsum("nbhqk,nbkhd->nbqhd", p, vb.astype(jnp.float32))
    m = m.transpose(0, 1, 3, 2).reshape(n, length, nh)
    den = den.transpose(0, 1, 3, 2).reshape(n, length, nh)
    return num.reshape(n, length, nh, hd), den, m


def _dilated_attention(h, w_qkv, w_out):
    bsz, seq, _ = h.shape
    proj = (h @ w_qkv).reshape(bsz, seq, len(C_PATTERNS), 3, C_HEADS, C_HEAD_DIM)
    pos = jnp.arange(seq, dtype=jnp.float32)
    inv_freq = ROPE_THETA ** (-jnp.arange(0, C_HEAD_DIM, 2, dtype=jnp.float32) / C_HEAD_DIM)
    ang = pos[:, None] * inv_freq[None, :]
    cos, sin = jnp.cos(ang), jnp.sin(ang)
    nums, dens, maxs = [], [], []
    for g, (window, dil) in enumerate(C_PATTERNS):
        q = _rope(proj[:, :, g, 0], cos, sin) * (C_HEAD_DIM ** -0.5)
        k = _rope(proj[:, :, g, 1], cos, sin)
        v = proj[:, :, g, 2]
        num, den, m = _window_attention(_fold(q, dil), _fold(k, dil), _fold(v, dil), window // dil)
        nums.append(_unfold(num, bsz, dil))
        dens.append(_unfold(den, bsz, dil))
        maxs.append(_unfold(m, bsz, dil))
    m_all = jnp.stack(maxs)
    wts = jnp.exp(m_all - jnp.max(m_all, axis=0, keepdims=True))
    o = jnp.sum(wts[..., None] * jnp.stack(nums), axis=0) / jnp.sum(wts * jnp.stack(dens), axis=0)[..., None]
    return o.reshape(bsz, seq, C_OUT).astype(h.dtype) @ w_out


def _conv_ffn(h, w_gu, conv_w, conv_b, w_down):
    gate, up = jnp.split(h @ w_gu, 2, axis=-1)
    gate = _causal_dwconv(gate, conv_w, conv_b)
    return (jax.nn.silu(gate) * up) @ w_down


def setup_inputs(seed: int = 0) -> dict:
    key = jax.random.key(seed)
    ks = jax.random.split(key, 24)
    nrm = lambda k, shape, scale: jax.random.normal(k, shape, jnp.float32) * scale
    f_bias = jnp.linspace(3.0, 6.0, A_HEADS, dtype=jnp.float32)[None, :] + nrm(ks[4], (N_EVEN, A_HEADS), 0.1)
    return {
        "x": nrm(ks[0], (BATCH, SEQ, D_MODEL), 1.0),
        "mix_norm": 1.0 + nrm(ks[1], (DEPTH, D_MODEL), 0.02),
        "ffn_norm": 1.0 + nrm(ks[2], (DEPTH, D_MODEL), 0.02),
        "ab_w_in": nrm(ks[3], (N_EVEN, D_MODEL, AB_IN), D_MODEL ** -0.5),
        "ab_i_bias": nrm(ks[5], (N_EVEN, A_HEADS), 0.1),
        "ab_f_bias": f_bias,
        "ab_head_norm": 1.0 + nrm(ks[6], (N_EVEN, A_WIDTH), 0.02),
        "ab_conv_w": nrm(ks[7], (N_EVEN, B_CONV, B_WIDTH), B_CONV ** -0.5),
        "ab_conv_b": nrm(ks[8], (N_EVEN, B_WIDTH), 0.02),
        "ab_ln_g": 1.0 + nrm(ks[9], (N_EVEN, B_WIDTH), 0.02),
        "ab_ln_b": nrm(ks[10], (N_EVEN, B_WIDTH), 0.02),
        "ab_w_out": nrm(ks[11], (N_EVEN, AB_OUT, D_MODEL), AB_OUT ** -0.5),
        "c_w_qkv": nrm(ks[12], (N_ODD, D_MODEL, C_IN), D_MODEL ** -0.5),
        "c_w_out": nrm(ks[13], (N_ODD, C_OUT, D_MODEL), C_OUT ** -0.5),
        "ffn_w_gu": nrm(ks[14], (DEPTH, D_MODEL, 2 * D_FF), D_MODEL ** -0.5),
        "ffn_conv_w": nrm(ks[15], (DEPTH, FFN_CONV, D_FF), FFN_CONV ** -0.5),
        "ffn_conv_b": nrm(ks[16], (DEPTH, D_FF), 0.02),
        "ffn_w_down": nrm(ks[17], (DEPTH, D_FF, D_MODEL), D_FF ** -0.5),
        "final_norm": 1.0 + nrm(ks[18], (D_MODEL,), 0.02),
    }


def reference(x, mix_norm, ffn_norm, ab_w_in, ab_i_bias, ab_f_bias, ab_head_norm, ab_conv_w, ab_conv_b,
              ab_ln_g, ab_ln_b, ab_w_out, c_w_qkv, c_w_out, ffn_w_gu, ffn_conv_w, ffn_conv_b, ffn_w_down,
              final_norm):
    for layer in range(DEPTH):
        j = layer // 2
        hn = _rms_norm(x, mix_norm[layer])
        if layer % 2 == 0:
            mix = _mlstm_conv_mixer(hn, ab_w_in[j], ab_i_bias[j], ab_f_bias[j], ab_head_norm[j], ab_conv_w[j],
                                    ab_conv_b[j], ab_ln_g[j], ab_ln_b[j], ab_w_out[j])
        else:
            mix = _dilated_attention(hn, c_w_qkv[j], c_w_out[j])
        x = x + mix.astype(x.dtype)
        hn = _rms_norm(x, ffn_norm[layer])
        x = x + _conv_ffn(hn, ffn_w_gu[layer], ffn_conv_w[layer], ffn_conv_b[layer], ffn_w_down[layer]).astype(x.dtype)
    return _rms_norm(x, final_norm)
```

```python
import contextlib
import numpy as np
import concourse.bass as bass
import concourse.mybir as mybir
from concourse.bass_utils import run_bass_kernel_spmd

F32 = mybir.dt.float32
BF16 = mybir.dt.bfloat16
ALU = mybir.AluOpType
AF = mybir.ActivationFunctionType
AX = mybir.AxisListType

NCORES = 8
T = 4096
S = 2048
D = 1024
DFF = 2816
N_DMA_SEMS = 24
EPS = 1e-6


class Tk:
    __slots__ = ("w", "r")

    def __init__(self):
        self.w = None
        self.r = {}


def tks(*shape):
    if len(shape) == 1:
        return [Tk() for _ in range(shape[0])]
    return [tks(*shape[1:]) for _ in range(shape[0])]


class Prog:
    ENGS = ("pe", "act", "dve", "pool", "sp")

    def __init__(self, nc, stack):
        self.nc = nc
        self.ops = []
        self.base = 0
        self.seq = {e: 0 for e in self.ENGS}
        self.dma_i = 0
        self.slot_last = {}
        self.known = {e: {} for e in self.ENGS}
        self.sems = {}
        self.stack = stack
        self.stage_i = 0
        for k in range(N_DMA_SEMS):
            self.sems[("dma", k)] = stack.enter_context(nc.semaphore("s_dma%d" % k))
        self.n_instr = 0

    def op(self, eng, fn, reads=(), writes=(), dma=False):
        deps = set()
        for t in reads:
            if t.w is not None:
                deps.add(t.w)
        for t in writes:
            if t.w is not None:
                deps.add(t.w)
            deps.update(t.r.values())
        gid = self.base + len(self.ops)
        self.ops.append([eng, fn, deps, dma, False])
        key = ("dma", gid) if dma else eng
        for t in reads:
            t.r[key] = gid
        for t in writes:
            t.w = gid
            t.r = {}
        return gid

    def dma(self, q, out, in_, reads=(), writes=()):
        return self.op(q, lambda e: e.dma_start(out=out, in_=in_), reads, writes, dma=True)

    def flush(self):
        nc = self.nc
        ops = self.ops
        base = self.base
        n = len(ops)
        if n == 0:
            return
        self.stage_i += 1
        for e in self.ENGS:
            if e != "pe" and e in self.sems:
                continue
            self.sems[e] = self.stack.enter_context(nc.semaphore("s_%s_%d" % (e, self.stage_i)))
            self.seq[e] = 0
            for e2 in self.ENGS:
                self.known[e2].pop(e, None)
        self.stage_counts = getattr(self, "stage_counts", [])
        for o in ops:
            for d in o[2]:
                if d >= base:
                    ops[d - base][4] = True
        last = {}
        for i, o in enumerate(ops):
            if not o[3]:
                last[o[0]] = i
        for e, i in last.items():
            ops[i][4] = True
        sig = {}
        dma_prev = {}
        for i, o in enumerate(ops):
            eng, fn, deps, is_dma, signals = o
            if is_dma:
                slot = self.dma_i % N_DMA_SEMS
                val = 16 * (self.dma_i // N_DMA_SEMS + 1)
                sig[i] = (("dma", slot), val)
                if slot in self.slot_last:
                    dma_prev[i] = self.slot_last[slot]
                self.slot_last[slot] = (("dma", slot), val)
                self.dma_i += 1
            elif signals:
                self.seq[eng] += 1
                sig[i] = (eng, self.seq[eng])
        final = {}
        for e in self.ENGS:
            if self.seq[e] > 0:
                final[e] = self.seq[e]
        for slot, kv in self.slot_last.items():
            final[kv[0]] = kv[1]
        sems = self.sems

        def run_engine(ename, e):
            known = self.known[ename]
            for i, o in enumerate(ops):
                eng, fn, deps, is_dma, signals = o
                if eng != ename:
                    continue
                need = {}
                for d in deps:
                    if d < base:
                        continue
                    dd = ops[d - base]
                    if (not dd[3]) and dd[0] == ename and ename == "pe":
                        continue
                    k, v = sig[d - base]
                    if need.get(k, 0) < v:
                        need[k] = v
                if i in dma_prev:
                    k, v = dma_prev[i]
                    if need.get(k, 0) < v:
                        need[k] = v
                for k, v in need.items():
                    if known.get(k, 0) < v:
                        e.wait_ge(sems[k], v)
                        known[k] = v
                        self.n_instr += 1
                ins = fn(e)
                self.n_instr += 1
                if i in sig:
                    k, v = sig[i]
                    ins.then_inc(sems[k], 16 if is_dma else 1)
            for k, v in final.items():
                if known.get(k, 0) < v:
                    e.wait_ge(sems[k], v)
                    known[k] = v

        with nc.Block() as block:
            @block.tensor
            def _(e):
                run_engine("pe", e)

            @block.scalar
            def _(e):
                run_engine("act", e)

            @block.vector
            def _(e):
                run_engine("dve", e)

            @block.gpsimd
            def _(e):
                run_engine("pool", e)

            @block.sync
            def _(e):
                run_engine("sp", e)
        self.stage_counts.append(dict(self.seq))
        self.base += n
        self.ops = []


class Ctx:
    def __init__(self, nc, P, ext_in=(), ext_out=()):
        self.nc = nc
        self.P = P
        self.ext_in = set(ext_in)
        self.ext_out = set(ext_out)
        self.d = {}
        self.dt = {}

    def dram(self, name, shape, dtype):
        if name in self.d:
            return self.d[name]
        kind = "Internal"
        if name in self.ext_in:
            kind = "ExternalInput"
        elif name in self.ext_out:
            kind = "ExternalOutput"
        self.d[name] = self.nc.dram_tensor(name, list(shape), dtype, kind=kind).ap()
        self.dt[name] = Tk()
        return self.d[name]


_UID = [0]


class Stage:
    def __init__(self, C):
        self.C = C
        self.st = contextlib.ExitStack()
        self.n = 0

    def __enter__(self):
        self.st.__enter__()
        return self

    def __exit__(self, *a):
        if a[0] is None:
            self.C.P.flush()
        return self.st.__exit__(*a)

    def sb(self, shape, dtype, name=None):
        _UID[0] += 1
        return self.st.enter_context(self.C.nc.sbuf_tensor("sb%d" % _UID[0], list(shape), dtype))

    def ps(self, shape, dtype=F32):
        _UID[0] += 1
        return self.st.enter_context(self.C.nc.psum_tensor("ps%d" % _UID[0], list(shape), dtype))

    def psbanks(self, n=8):
        return [self.ps([128, 512]) for _ in range(n)], tks(n)


def make_ident(P, t, tk, dtype_is_bf=False):
    P.op("pool", lambda e: e.memset(t[:], 0.0), writes=[tk])
    P.op("pool", lambda e: e.affine_select(out=t[:], in_=t[:], pattern=[[-1, 128]], compare_op=ALU.not_equal,
                                           fill=1.0, base=0, channel_multiplier=1), reads=[tk], writes=[tk])


def st_norm(C, src, srct, g_dram, dst, dstt, final=False):
    P = C.P
    srcv = src.rearrange("(kc p) t -> p kc t", p=128)
    dstv = dst.rearrange("(kc p) t -> p kc t", p=128)
    odt = F32 if final else BF16
    NT = T // 512
    LA = 2
    with Stage(C) as st:
        NR = LA + 1
        R = st.sb([128, NR, 8, 512], F32)
        Rt = tks(NR, 8)
        g = st.sb([128, 8], F32)
        gt = Tk()
        ones = st.sb([128, 128], BF16)
        onest = Tk()
        sq = st.sb([128, 2, 8, 512], BF16)
        sqt = tks(2, 8)
        rs = st.sb([128, 2, 512], F32)
        rst = tks(2)
        ho = st.sb([128, 2, 8, 512], odt)
        hot = tks(2, 8)
        pb, pbt = st.psbanks(2)
        P.dma("sp", g[:], g_dram, writes=[gt])
        P.op("pool", lambda e: e.memset(ones[:], 1.0), writes=[onest])

        def load(i):
            rb = i % NR
            P.dma("sp", R[:, rb, :, :], srcv[:, :, i * 512:(i + 1) * 512], reads=[srct], writes=Rt[rb])
        for i in range(min(LA, NT)):
            load(i)
        for i in range(NT):
            if i + LA < NT:
                load(i + LA)
            par = i % 2
            rb = i % NR
            for kc in range(8):
                P.op("act", lambda e, kc=kc, par=par, rb=rb: e.activation(out=sq[:, par, kc, :], in_=R[:, rb, kc, :], func=AF.Square),
                     reads=[Rt[rb][kc]], writes=[sqt[par][kc]])
            for kc in range(8):
                P.op("pe", lambda e, kc=kc, par=par: e.matmul(pb[par][:], lhsT=ones[:], rhs=sq[:, par, kc, :], start=(kc == 0), stop=(kc == 7)),
                     reads=[onest, sqt[par][kc]], writes=[pbt[par]])
            P.op("dve", lambda e, par=par: e.tensor_scalar(out=rs[:, par, :], in0=pb[par][:], scalar1=1.0 / D, scalar2=EPS, op0=ALU.mult, op1=ALU.add),
                 reads=[pbt[par]], writes=[rst[par]])
            P.op("act", lambda e, par=par: e.activation(out=rs[:, par, :], in_=rs[:, par, :], func=AF.Sqrt), reads=[rst[par]], writes=[rst[par]])
            P.op("dve", lambda e, par=par: e.reciprocal(out=rs[:, par, :], in_=rs[:, par, :]), reads=[rst[par]], writes=[rst[par]])
            for kc in range(8):
                P.op("dve", lambda e, kc=kc, par=par, rb=rb: e.scalar_tensor_tensor(out=ho[:, par, kc, :], in0=R[:, rb, kc, :], scalar=g[:, kc:kc + 1], in1=rs[:, par, :],
                                                                                   op0=ALU.mult, op1=ALU.mult),
                     reads=[Rt[rb][kc], gt, rst[par]], writes=[hot[par][kc]])
            P.dma("sp", dstv[:, :, i * 512:(i + 1) * 512], ho[:, par, :, :], reads=hot[par], writes=[dstt])


def norm_into(C, st, src, srct, g_dram, xin, xint, pb, pbt, banks, hn=None, hnt=None, outproj=None):
    P = C.P
    srcv = src.rearrange("(kc p) t -> p kc t", p=128)
    NT = T // 512
    LA = 2
    NR = LA + 1
    R = st.sb([128, NR, 8, 512], F32)
    Rt = tks(NR, 8)
    g = st.sb([128, 8], F32)
    gt = Tk()
    ones = st.sb([128, 128], BF16)
    onest = Tk()
    sq = st.sb([128, 2, 8, 512], BF16)
    sqt = tks(2, 8)
    rs = st.sb([128, 2, 512], F32)
    rst = tks(2)
    P.dma("sp", g[:], g_dram, writes=[gt])
    P.op("pool", lambda e: e.memset(ones[:], 1.0), writes=[onest])
    if hn is not None:
        hnv = hn.rearrange("(kc p) t -> p kc t", p=128)

    if outproj is not None:
        oT_d, oTt_d, wo_d, rout, routt = outproj
        wo = st.sb([128, 4, D], BF16)
        wot = Tk()
        P.dma("pool", wo[:], wo_d.rearrange("(kc p) n -> p kc n", p=128), writes=[wot])
        oTs = st.sb([128, NR, 4, 512], BF16)
        oTst = tks(NR)
        oTv = oT_d.rearrange("(kc p) t -> p kc t", p=128)
        routv = rout.rearrange("(kc p) t -> p kc t", p=128)

    def load(i):
        rb = i % NR
        P.dma("sp", R[:, rb, :, :], srcv[:, :, i * 512:(i + 1) * 512], reads=[srct], writes=Rt[rb])
        if outproj is not None:
            P.dma("sp", oTs[:, rb, :, :], oTv[:, :, i * 512:(i + 1) * 512], reads=[oTt_d], writes=[oTst[rb]])

    def produce(i):
        rb = i % NR
        if outproj is not None:
            for gi in range(8):
                bk = 4 + gi % 2
                for kc in range(4):
                    P.op("pe", lambda e, bk=bk, kc=kc, gi=gi, rb=rb: e.matmul(pb[bk][:], lhsT=wo[:, kc, gi * 128:(gi + 1) * 128], rhs=oTs[:, rb, kc, :], start=(kc == 0), stop=(kc == 3)),
                         reads=[wot, oTst[rb]], writes=[pbt[bk]])
                P.op("dve", lambda e, bk=bk, gi=gi, rb=rb: e.tensor_tensor(out=R[:, rb, gi, :], in0=R[:, rb, gi, :], in1=pb[bk][:], op=ALU.add),
                     reads=[pbt[bk], Rt[rb][gi]], writes=[Rt[rb][gi]])
            P.dma("sp", routv[:, :, i * 512:(i + 1) * 512], R[:, rb, :, :], reads=Rt[rb], writes=[routt])
    for i in range(min(LA, NT)):
        load(i)

    def tile(i):
        if i + LA < NT:
            load(i + LA)
        produce(i)
        par = i % 2
        rb = i % NR
        s, tb = i // 4, i % 4
        bk = banks[par]
        for kc in range(8):
            P.op("act", lambda e, kc=kc, par=par, rb=rb: e.activation(out=sq[:, par, kc, :], in_=R[:, rb, kc, :], func=AF.Square),
                 reads=[Rt[rb][kc]], writes=[sqt[par][kc]])
        for kc in range(8):
            P.op("pe", lambda e, kc=kc, par=par, bk=bk: e.matmul(pb[bk][:], lhsT=ones[:], rhs=sq[:, par, kc, :], start=(kc == 0), stop=(kc == 7)),
                 reads=[onest, sqt[par][kc]], writes=[pbt[bk]])
        P.op("dve", lambda e, par=par, bk=bk: e.tensor_scalar(out=rs[:, par, :], in0=pb[bk][:], scalar1=1.0 / D, scalar2=EPS, op0=ALU.mult, op1=ALU.add),
             reads=[pbt[bk]], writes=[rst[par]])
        P.op("act", lambda e, par=par: e.activation(out=rs[:, par, :], in_=rs[:, par, :], func=AF.Sqrt), reads=[rst[par]], writes=[rst[par]])
        P.op("dve", lambda e, par=par: e.reciprocal(out=rs[:, par, :], in_=rs[:, par, :]), reads=[rst[par]], writes=[rst[par]])
        for kc in range(8):
            P.op("dve", lambda e, kc=kc, par=par, rb=rb, s=s, tb=tb: e.scalar_tensor_tensor(out=xin[:, s, kc, tb * 512:(tb + 1) * 512], in0=R[:, rb, kc, :], scalar=g[:, kc:kc + 1], in1=rs[:, par, :],
                                                                                     op0=ALU.mult, op1=ALU.mult),
                 reads=[Rt[rb][kc], gt, rst[par]], writes=[xint[s][kc]])
        if hn is not None:
            P.dma("sp", hnv[:, :, i * 512:(i + 1) * 512], xin[:, s, :, tb * 512:(tb + 1) * 512], reads=xint[s], writes=[hnt])
    for i in range(4):
        tile(i)
    return [lambda i=i: tile(i) for i in range(4, NT)]


def alloc_fused(st):
    xin = st.sb([128, 2, 8, S], BF16)
    xint = tks(2, 8)
    pb, pbt = st.psbanks(8)
    return dict(xin=xin, xint=xint, pb=pb, pbt=pbt)


def proj_fm(C, st, src, srct, W, groups, epi, KC, wload=None, nslot=2, pre=None):
    P = C.P
    srcv = src.rearrange("(kc p) t -> p kc t", p=128)
    Wv = W.rearrange("(kc p) n -> p kc n", p=128)
    big = KC > 8
    if big:
        NH = 3
        xin = st.sb([128, NH, KC, 1024], BF16)
        xint = tks(NH, KC)
    elif pre is not None:
        xin, xint = pre["xin"], pre["xint"]
    else:
        xin = st.sb([128, 2, KC, S], BF16)
        xint = tks(2, KC)
    NW = 3
    wt = st.sb([128, NW, nslot, KC, 128], BF16)
    wtt = tks(NW, nslot)
    if pre is not None:
        pb, pbt = pre["pb"], pre["pbt"]
    else:
        pb, pbt = st.psbanks(8)
    bank_i = [0]

    def alloc():
        b = bank_i[0] % 8
        bank_i[0] += 1
        return b
    order = [(s, gi) for s in range(2) for gi in range(len(groups))]

    def load_w(idx):
        s, gi = order[idx]
        grp = groups[gi]
        wb = idx % NW
        if wload is None:
            for j, col in enumerate(grp):
                P.dma("pool", wt[:, wb, j, :, :], Wv[:, :, col:col + 128], writes=[wtt[wb][j]])
        else:
            wload(gi, grp, wt, wtt, wb, Wv)

    def load_x(s):
        for kc in range(KC):
            P.dma("sp", xin[:, s, kc, :], srcv[:, kc, s * S:(s + 1) * S], reads=[srct], writes=[xint[s][kc]])

    def load_xh(hi):
        s, half = hi // 2, hi % 2
        hb = hi % NH
        t0 = s * S + half * 1024
        for kc in range(KC):
            P.dma("sp", xin[:, hb, kc, :], srcv[:, kc, t0:t0 + 1024], reads=[srct], writes=[xint[hb][kc]])
    if big:
        load_xh(0)
        load_w(0)
        load_xh(1)
        if len(order) > 1:
            load_w(1)
        load_xh(2)
    else:
        if pre is None:
            load_x(0)
        load_w(0)
        if len(order) > 1:
            load_w(1)
        if pre is None:
            load_x(1)
    for idx, (s, gi) in enumerate(order):
        grp = groups[gi]
        wb = idx % NW
        if idx + 2 < len(order):
            load_w(idx + 2)
        if pre is not None and pre.get("hooks"):
            pre["hooks"].pop(0)()
        for half in range(2):
            if big:
                xb = (2 * s + half) % NH
                xoff = 0
            else:
                xb = s
                xoff = half * 1024
            banks = []
            for j in range(nslot if wload is not None else len(grp)):
                b0 = alloc()
                b1 = alloc()
                for kc in range(KC):
                    for tb, b in enumerate((b0, b1)):
                        t0 = xoff + tb * 512
                        P.op("pe", lambda e, b=b, wb=wb, j=j, kc=kc, t0=t0, xb=xb: e.matmul(pb[b][:], lhsT=wt[:, wb, j, kc, :], rhs=xin[:, xb, kc, t0:t0 + 512],
                                                                                             start=(kc == 0), stop=(kc == KC - 1)),
                             reads=[wtt[wb][j], xint[xb][kc]], writes=[pbt[b]])
                banks.append((b0, b1))
            if big and s == 0 and gi == len(groups) - 1 and half == 0:
                load_xh(3)
            epi(s, half, gi, grp, banks, pb, pbt, alloc)


class Rot:
    def __init__(self, st, n, shape, dtype):
        self.t = st.sb([128, n] + list(shape), dtype)
        self.tk = tks(n)
        self.n = n
        self.i = 0

    def next(self):
        k = self.i % self.n
        self.i += 1
        return k, self.tk[k]


def st_proj_tok(C, src, srct, W, col0, ncols, cgroups, epi, wranges=None):
    P = C.P
    srcv = src.rearrange("(kc p) t -> p kc t", p=128)
    Wv = W.rearrange("(kc p) n -> p kc n", p=128)
    if wranges is None:
        wranges = [(col0, ncols)]
    ncols = sum(n for _, n in wranges)
    with Stage(C) as st:
        xin = st.sb([128, 2, 8, S], BF16)
        xint = tks(2, 8)
        wt = st.sb([128, 8, ncols], BF16)
        wtt = tks(8)
        pb, pbt = st.psbanks(8)
        env = epi(st)
        for kc in range(8):
            o = 0
            for (c0, n) in wranges:
                P.dma("pool", wt[:, kc, o:o + n], Wv[:, kc, c0:c0 + n], writes=[wtt[kc]])
                o += n
        for s in range(2):
            for kc in range(8):
                P.dma("sp", xin[:, s, kc, :], srcv[:, kc, s * S:(s + 1) * S], reads=[srct], writes=[xint[s][kc]])
        bi = 0
        for s in range(2):
            for tt in range(16):
                for si, cgs in enumerate(cgroups):
                    banks = []
                    for _ in cgs:
                        banks.append(bi % 8)
                        bi += 1
                    for kc in range(8):
                        for (c0, n), b in zip(cgs, banks):
                            P.op("pe", lambda e, b=b, kc=kc, tt=tt, c0=c0, n=n, s=s: e.matmul(pb[b][:, 0:n], lhsT=xin[:, s, kc, tt * 128:(tt + 1) * 128], rhs=wt[:, kc, c0:c0 + n],
                                                                                               start=(kc == 0), stop=(kc == 7)),
                                 reads=[xint[s][kc], wtt[kc]], writes=[pbt[b]])
                    env(s, tt, si, cgs, banks, pb, pbt)


def st_inproj_fm(C, hn, w_in, qT0, kT0, uT, norm=None):
    P = C.P
    groups = [(c * 128,) for c in range(8)] + [(1024 + c * 128,) for c in range(8)] + [(4104 + c * 128, 5128 + c * 128) for c in range(8)]
    with Stage(C) as st:
        pre = None
        if norm is not None:
            pre = alloc_fused(st)
            pre["hooks"] = norm_into(C, st, norm[0], norm[1], norm[2], pre["xin"], pre["xint"], pre["pb"], pre["pbt"], (6, 7), hn=hn, hnt=C.dt["hn"])
        ob = Rot(st, 4, [512], BF16)
        sg = Rot(st, 3, [512], F32)

        def epi(s, half, gi, grp, banks, pb, pbt, alloc):
            for tb in range(2):
                t0 = s * S + half * 1024 + tb * 512
                k, kt = ob.next()
                if gi < 16:
                    b = banks[0][tb]
                    sc = 1.0 if gi < 8 else 1.0 / 16
                    P.op("act", lambda e, b=b, k=k, sc=sc: e.activation(out=ob.t[:, k, :], in_=pb[b][:], func=AF.Identity, scale=sc), reads=[pbt[b]], writes=[kt])
                    dst, dn = (qT0, "qT0") if gi < 8 else (kT0, "kT0")
                    r0 = (gi % 8) * 128
                else:
                    ba, bg = banks[0][tb], banks[1][tb]
                    k2, k2t = sg.next()
                    P.op("act", lambda e, bg=bg, k2=k2: e.activation(out=sg.t[:, k2, :], in_=pb[bg][:], func=AF.Sigmoid), reads=[pbt[bg]], writes=[k2t])
                    P.op("dve", lambda e, ba=ba, k=k, k2=k2: e.tensor_tensor(out=ob.t[:, k, :], in0=pb[ba][:], in1=sg.t[:, k2, :], op=ALU.mult), reads=[pbt[ba], k2t], writes=[kt])
                    dst, dn = uT, "uT"
                    r0 = (gi - 16) * 128
                P.dma("sp", dst[r0:r0 + 128, t0:t0 + 512], ob.t[:, k, :], reads=[kt], writes=[C.dt[dn]])
        proj_fm(C, st, hn, C.dt["hn"], w_in, groups, epi, 8, pre=pre)


def st_inproj_tok(C, hn, w_in, ktok, vtok, so, gates, hnb_d):
    P = C.P
    cg = [[(0, 512), (512, 512), (1024, 512), (1536, 512)], [(2048, 512), (2560, 512), (3072, 8)]]

    def mk(st):
        hnb = st.sb([128, 1024], F32); hnbt = Tk()
        P.dma("sp", hnb[:], hnb_d, writes=[hnbt])
        kst = Rot(st, 2, [1024], BF16)
        vst = Rot(st, 2, [1024], F32)
        ost = Rot(st, 2, [1024], F32)
        gst = Rot(st, 2, [8], F32)

        def env(s, tt, si, cgs, banks, pb, pbt):
            r0 = s * S + tt * 128
            if si == 0:
                k, kt = kst.next()
                for i in range(2):
                    b = banks[i]
                    P.op("act", lambda e, b=b, k=k, i=i: e.activation(out=kst.t[:, k, i * 512:(i + 1) * 512], in_=pb[b][:], func=AF.Identity, scale=1.0 / 16), reads=[pbt[b]], writes=[kt])
                P.dma("sp", ktok[r0:r0 + 128, :], kst.t[:, k, :], reads=[kt], writes=[C.dt["ktok"]])
                k, kt = vst.next()
                for i in range(2):
                    b = banks[2 + i]
                    P.op("dve", lambda e, b=b, k=k, i=i: e.tensor_copy(out=vst.t[:, k, i * 512:(i + 1) * 512], in_=pb[b][:]), reads=[pbt[b]], writes=[kt])
                P.dma("sp", vtok[r0:r0 + 128, :], vst.t[:, k, :], reads=[kt], writes=[C.dt["vtok"]])
            else:
                k, kt = ost.next()
                for i in range(2):
                    b = banks[i]
                    P.op("act", lambda e, b=b, k=k, i=i: e.activation(out=ost.t[:, k, i * 512:(i + 1) * 512], in_=pb[b][:], func=AF.Sigmoid), reads=[pbt[b]], writes=[kt])
                P.op("dve", lambda e, k=k: e.tensor_tensor(out=ost.t[:, k, :], in0=ost.t[:, k, :], in1=hnb[:], op=ALU.mult), reads=[kt, hnbt], writes=[kt])
                P.dma("sp", so[r0:r0 + 128, :], ost.t[:, k, :], reads=[kt], writes=[C.dt["so"]])
                k, kt = gst.next()
                b = banks[2]
                P.op("dve", lambda e, b=b, k=k: e.tensor_copy(out=gst.t[:, k, :], in_=pb[b][:, 0:8]), reads=[pbt[b]], writes=[kt])
                P.dma("sp", gates[r0:r0 + 128, :], gst.t[:, k, :], reads=[kt], writes=[C.dt["gates"]])
        return env
    st_proj_tok(C, hn, C.dt["hn"], w_in, 1024, 3080, cg, mk)


def st_mlstm(C, qT0, kT0, ktok, vtok, so, gates, catT, fb_d, ib_d, hnb_d):
    P = C.P
    dt = C.dt
    with Stage(C) as st:
        tri = st.sb([128, 128], F32); trit = Tk()
        onesf = st.sb([128, 128], F32); onest = Tk()
        ident = st.sb([128, 128], F32); identt = Tk()
        identb = st.sb([128, 128], BF16); identbt = Tk()
        maskT = st.sb([128, 128], F32); maskt = Tk()
        fb = st.sb([128, 64], F32); ib = st.sb([128, 64], F32); cbt = Tk()
        P.op("pool", lambda e: e.memset(onesf[:], 1.0), writes=[onest])
        make_ident(P, ident, identt)
        P.op("pool", lambda e: e.tensor_copy(out=identb[:], in_=ident[:]), reads=[identt], writes=[identbt])
        P.op("pool", lambda e: e.memset(tri[:], 1.0), writes=[trit])
        P.op("pool", lambda e: e.affine_select(out=tri[:], in_=tri[:], pattern=[[1, 128]], compare_op=ALU.is_ge, fill=0.0, base=0, channel_multiplier=-1), reads=[trit], writes=[trit])
        P.op("pool", lambda e: e.tensor_copy(out=maskT[:], in_=tri[:]), reads=[trit], writes=[maskt])
        P.dma("sp", fb[:], fb_d, writes=[cbt])
        P.dma("sp", ib[:], ib_d, writes=[cbt])
        GF = st.sb([128, 16, 8], F32); gft = Tk()
        IGb = st.sb([128, 16, 4], F32); FGb = st.sb([128, 16, 4], F32); Lt = st.sb([128, 16, 4], F32)
        A = st.sb([128, 16, 4], F32); Bn = st.sb([128, 16, 4], F32); BLn = st.sb([128, 16, 4], F32)
        Amax = st.sb([64, 1], F32); Arep = st.sb([64, 128], F32); Abc = st.sb([128, 16, 4], F32)
        M = st.sb([128, 17, 4], F32); MU = st.sb([128, 16, 4], F32)
        Wg = st.sb([128, 16, 4], F32); Gg = st.sb([128, 16, 4], F32); EN = st.sb([128, 16, 4], F32)
        tmp = st.sb([128, 16, 4], F32)
        gt = {n: Tk() for n in "IGb FGb L A Bn BLn Amax Arep Abc M MU W G EN tmp".split()}
        bk0 = st.ps([128, 512])
        pg = [bk0[:, 0:64], bk0[:, 64:128]]
        _pgt = Tk(); pgt = [_pgt, _pgt]
        qTh2 = st.sb([128, 2, 2, S], BF16); qTht2 = tks(2, 2)
        kTh2 = st.sb([128, 2, 2, S], BF16); kTht2 = tks(2, 2)
        kth2 = st.sb([128, 2, 16, 256], BF16); ktht2 = tks(2)
        vh2 = st.sb([128, 2, 16, 256], F32); vht2 = tks(2)
        soh2 = st.sb([128, 2, 16, 256], F32); soht2 = tks(2)
        vwa = st.sb([128, 16, 257], BF16); vwat = tks(16); vwaot = Tk()
        Cst2 = st.sb([128, 2, 2, 257], F32); Cstt2 = tks(2, 2)
        Cbf2 = st.sb([128, 2, 2, 257], BF16); Cbft2 = tks(2)
        WmT = Rot(st, 3, [128], BF16)
        numsb2 = st.sb([128, 2, 16, 257], F32); numt2 = tks(2, 16)
        sqs_t = st.sb([128, 16, 256], BF16); sqst_t = Tk()
        sm2 = [{n: (st.sb([128, 16], F32), Tk()) for n in "ssn d1 r t1 sc".split()} for _ in range(2)]
        hmt2 = st.sb([128, 2, 16, 256], BF16); hmtt2 = tks(2, 16)
        hmT = st.sb([128, 2, S], BF16); hmTt = tks(2, 4)
        pS = [st.ps([128, 512])[:, 0:128] for _ in range(2)]; pSt = tks(2)
        pN = [st.ps([128, 512])[:, 0:257] for _ in range(2)]; pNt = tks(2)
        pU = [st.ps([128, 512])[:, 0:257] for _ in range(2)]; pUt = tks(2)
        pX = st.ps([128, 1024], BF16)[:, 0:512]; pXt = Tk()
        gv = gates.rearrange("(s c p) g -> s p c g", p=128, c=16)

        def load_head(idx):
            s, h = idx // 4, idx % 4
            hb = idx % 2
            fm = lambda a: a[h * 256:(h + 1) * 256, s * S:(s + 1) * S].rearrange("(dc p) t -> p dc t", p=128)
            tokv = lambda a: a[s * S:(s + 1) * S, h * 256:(h + 1) * 256].rearrange("(c p) d -> p c d", p=128)
            P.dma("sp", qTh2[:, hb], fm(qT0), reads=[dt["qT0"]], writes=qTht2[hb])
            P.dma("sp", kTh2[:, hb], fm(kT0), reads=[dt["kT0"]], writes=kTht2[hb])
            P.dma("sp", kth2[:, hb], tokv(ktok), reads=[dt["ktok"]], writes=[ktht2[hb]])
            P.dma("sp", vh2[:, hb], tokv(vtok), reads=[dt["vtok"]], writes=[vht2[hb]])

        def load_so(idx):
            s, h = idx // 4, idx % 4
            hb = idx % 2
            tokv = lambda a: a[s * S:(s + 1) * S, h * 256:(h + 1) * 256].rearrange("(c p) d -> p c d", p=128)
            P.dma("sp", soh2[:, hb], tokv(so), reads=[dt["so"]], writes=[soht2[hb]])
        load_head(0)
        load_so(0)
        pending = []
        for s in range(2):
            while pending:
                pending.pop(0)()
            P.dma("sp", GF[:], gv[s], reads=[dt["gates"]], writes=[gft])
            fb3 = fb[:].rearrange("p (c h) -> p c h", h=4)
            ib3 = ib[:].rearrange("p (c h) -> p c h", h=4)
            P.op("dve", lambda e: e.tensor_tensor(out=FGb[:], in0=GF[:, :, 4:8], in1=fb3, op=ALU.add), reads=[gft, cbt], writes=[gt["FGb"]])
            P.op("dve", lambda e: e.tensor_tensor(out=IGb[:], in0=GF[:, :, 0:4], in1=ib3, op=ALU.add), reads=[gft, cbt], writes=[gt["IGb"]])
            P.op("act", lambda e: e.activation(out=Lt[:], in_=FGb[:], func=AF.Exp, scale=-1.0), reads=[gt["FGb"]], writes=[gt["L"]])
            P.op("dve", lambda e: e.tensor_scalar(out=Lt[:], in0=Lt[:], scalar1=1.0, scalar2=1.0, op0=ALU.add, op1=ALU.mult), reads=[gt["L"]], writes=[gt["L"]])
            P.op("act", lambda e: e.activation(out=Lt[:], in_=Lt[:], func=AF.Ln), reads=[gt["L"]], writes=[gt["L"]])
            L2 = Lt[:].rearrange("p c h -> p (c h)")
            P.op("pe", lambda e: e.matmul(pg[0][:], lhsT=tri[:], rhs=L2, start=True, stop=True), reads=[trit, gt["L"]], writes=[pgt[0]])
            P.op("pe", lambda e: e.matmul(pg[1][:], lhsT=onesf[:], rhs=L2, start=True, stop=True), reads=[onest, gt["L"]], writes=[pgt[1]])
            f2 = lambda t: t[:].rearrange("p c h -> p (c h)")
            P.op("dve", lambda e: e.tensor_copy(out=f2(Bn), in_=pg[0][:]), reads=[pgt[0]], writes=[gt["Bn"], pgt[0]])
            P.op("dve", lambda e: e.tensor_copy(out=f2(BLn), in_=pg[1][:]), reads=[pgt[1]], writes=[gt["BLn"], pgt[1]])
            P.op("dve", lambda e: e.tensor_tensor(out=A[:], in0=IGb[:], in1=Bn[:], op=ALU.add), reads=[gt["IGb"], gt["Bn"]], writes=[gt["A"]])
            P.op("act", lambda e: e.activation(out=tmp[:], in_=A[:], func=AF.Exp), reads=[gt["A"]], writes=[gt["tmp"]])
            P.op("pe", lambda e: e.matmul(pg[0][:], lhsT=onesf[:], rhs=f2(tmp), start=True, stop=True), reads=[onest, gt["tmp"]], writes=[pgt[0]])
            P.op("act", lambda e: e.activation(out=f2(Abc), in_=pg[0][:], func=AF.Ln), reads=[pgt[0]], writes=[gt["Abc"], pgt[0]])
            P.op("pool", lambda e: e.memset(M[:, 0, :], 0.0), writes=[gt["M"]])
            for c in range(16):
                P.op("dve", lambda e, c=c: e.tensor_tensor(out=MU[:, c, :], in0=M[:, c, :], in1=Abc[:, c, :], op=ALU.max), reads=[gt["M"], gt["Abc"]], writes=[gt["MU"]])
                P.op("dve", lambda e, c=c: e.tensor_tensor(out=M[:, c + 1, :], in0=MU[:, c, :], in1=BLn[:, c, :], op=ALU.subtract), reads=[gt["MU"], gt["BLn"]], writes=[gt["M"]])
            P.op("dve", lambda e: e.tensor_tensor(out=tmp[:], in0=A[:], in1=MU[:], op=ALU.subtract), reads=[gt["A"], gt["MU"]], writes=[gt["tmp"]])
            P.op("act", lambda e: e.activation(out=Wg[:], in_=tmp[:], func=AF.Exp), reads=[gt["tmp"]], writes=[gt["W"]])
            P.op("dve", lambda e: e.tensor_tensor(out=tmp[:], in0=M[:, 0:16, :], in1=MU[:], op=ALU.subtract), reads=[gt["M"], gt["MU"], gt["W"]], writes=[gt["tmp"]])
            P.op("act", lambda e: e.activation(out=Gg[:], in_=tmp[:], func=AF.Exp), reads=[gt["tmp"]], writes=[gt["G"]])
            P.op("dve", lambda e: e.tensor_tensor(out=tmp[:], in0=Bn[:], in1=MU[:], op=ALU.subtract), reads=[gt["Bn"], gt["MU"], gt["G"]], writes=[gt["tmp"]])
            P.op("act", lambda e: e.activation(out=EN[:], in_=tmp[:], func=AF.Exp), reads=[gt["tmp"]], writes=[gt["EN"]])
            def do_head(s, h, pending):
                idx = s * 4 + h
                hb = idx % 2
                if idx + 1 < 8:
                    load_head(idx + 1)
                qTh = qTh2[:, hb]; kTh = kTh2[:, hb]; kth = kth2[:, hb]; vh = vh2[:, hb]; soh = soh2[:, hb]
                qTht = qTht2[hb]; kTht = kTht2[hb]; ktht = ktht2[hb]; vht = vht2[hb]; soht = soht2[hb]
                numsb = numsb2[:, hb]; numt = numt2[hb]; hmt = hmt2[:, hb]; hmtt = hmtt2[hb]; sm = sm2[hb]
                so_done = []
                sqs = sqs_t; sqst = sqst_t
                for c in range(16):
                    P.op("act", lambda e, c=c, h=h: e.activation(out=vwa[:, c, 0:256], in_=vh[:, c, :], func=AF.Identity, scale=Wg[:, c, h:h + 1]),
                         reads=[vht, gt["W"]], writes=[vwat[c]])
                P.op("dve", lambda e, h=h: e.tensor_copy(out=vwa[:, :, 256:257], in_=Wg[:, :, h:h + 1]), reads=[gt["W"]], writes=[vwaot])
                for c in range(16):
                    cs = slice(c * 128, (c + 1) * 128)
                    pp = c % 2
                    sp_ = c % 2
                    for dc in range(2):
                        P.op("pe", lambda e, dc=dc, cs=cs, sp_=sp_: e.matmul(pS[sp_][:], lhsT=kTh[:, dc, cs], rhs=qTh[:, dc, cs], start=(dc == 0), stop=(dc == 1)),
                             reads=[kTht[dc], qTht[dc]], writes=[pSt[sp_]])
                    k, kt = WmT.next()
                    P.op("dve", lambda e, k=k, sp_=sp_: e.tensor_tensor(out=WmT.t[:, k, :], in0=pS[sp_][:], in1=maskT[:], op=ALU.mult), reads=[pSt[sp_], maskt], writes=[kt])
                    if c < 15:
                        for dc in range(2):
                            P.op("pe", lambda e, dc=dc, c=c: e.matmul(pU[dc][:], lhsT=kth[:, c, dc * 128:(dc + 1) * 128], rhs=vwa[:, c, :], start=True, stop=True),
                                 reads=[ktht, vwat[c], vwaot], writes=[pUt[dc]])
                            if c == 0:
                                P.op("dve", lambda e, dc=dc, pp=pp: e.tensor_copy(out=Cst2[:, pp, dc, :], in_=pU[dc][:]), reads=[pUt[dc]], writes=[Cstt2[pp][dc]])
                            else:
                                P.op("dve", lambda e, dc=dc, c=c, h=h, pp=pp: e.scalar_tensor_tensor(out=Cst2[:, pp, dc, :], in0=Cst2[:, 1 - pp, dc, :], scalar=Gg[:, c, h:h + 1], in1=pU[dc][:], op0=ALU.mult, op1=ALU.add),
                                     reads=[pUt[dc], Cstt2[1 - pp][dc], gt["G"]], writes=[Cstt2[pp][dc]])
                        P.op("act", lambda e, c=c, h=h, pp=pp: e.activation(out=Cbf2[:, 1 - pp], in_=Cst2[:, pp], func=AF.Identity, scale=Gg[:, c + 1, h:h + 1]),
                             reads=[gt["G"]] + Cstt2[pp], writes=[Cbft2[1 - pp]])
                    if c > 0:
                        for dc in range(2):
                            P.op("pe", lambda e, dc=dc, cs=cs, sp_=sp_, pp=pp: e.matmul(pN[sp_][:], lhsT=qTh[:, dc, cs], rhs=Cbf2[:, pp, dc, :], start=(dc == 0), stop=False),
                                 reads=[qTht[dc], Cbft2[pp]], writes=[pNt[sp_]])
                    P.op("pe", lambda e, k=k, c=c, sp_=sp_: e.matmul(pN[sp_][:], lhsT=WmT.t[:, k, :], rhs=vwa[:, c, :], start=(c == 0), stop=True),
                         reads=[kt, vwat[c], vwaot], writes=[pNt[sp_]])
                    P.op("act", lambda e, c=c, sp_=sp_: e.activation(out=numsb[:, c, :], in_=pN[sp_][:], func=AF.Copy), reads=[pNt[sp_]], writes=[numt[c]])
                    if c % 2 == 1 and pending:
                        pending.pop(0)()
                        if not pending and idx + 1 < 8 and not so_done:
                            load_so(idx + 1)
                            so_done.append(1)
                ssn, d1, rr, t1, sc = (sm[n][0] for n in "ssn d1 r t1 sc".split())
                ssnt, d1t, rt_, t1t, sct = (sm[n][1] for n in "ssn d1 r t1 sc".split())
                steps = []

                def s0():
                    P.op("act", lambda e: e.activation(out=sqs[:], in_=numsb[:, :, 0:256], func=AF.Square), reads=numt, writes=[sqst])
                    P.op("act", lambda e: e.activation(out=d1[:], in_=numsb[:, :, 256], func=AF.Abs), reads=numt, writes=[d1t])
                steps.append(s0)

                def s1():
                    P.op("dve", lambda e: e.tensor_reduce(out=ssn[:], in_=sqs[:], axis=AX.X, op=ALU.add), reads=[sqst], writes=[ssnt])
                    P.op("dve", lambda e: e.tensor_tensor(out=d1[:], in0=d1[:], in1=EN[:, :, h], op=ALU.max), reads=[d1t, gt["EN"]], writes=[d1t])
                    P.op("dve", lambda e: e.reciprocal(out=rr[:], in_=d1[:]), reads=[d1t], writes=[rt_])
                    P.op("dve", lambda e: e.tensor_tensor(out=t1[:], in0=ssn[:], in1=rr[:], op=ALU.mult), reads=[ssnt, rt_], writes=[t1t])
                    P.op("dve", lambda e: e.tensor_tensor(out=t1[:], in0=t1[:], in1=rr[:], op=ALU.mult), reads=[t1t, rt_], writes=[t1t])
                    P.op("dve", lambda e: e.tensor_scalar(out=t1[:], in0=t1[:], scalar1=1.0 / 256, scalar2=EPS, op0=ALU.mult, op1=ALU.add), reads=[t1t], writes=[t1t])
                    P.op("act", lambda e: e.activation(out=t1[:], in_=t1[:], func=AF.Sqrt), reads=[t1t], writes=[t1t])
                    P.op("dve", lambda e: e.reciprocal(out=t1[:], in_=t1[:]), reads=[t1t], writes=[t1t])
                    P.op("dve", lambda e: e.tensor_tensor(out=sc[:], in0=t1[:], in1=rr[:], op=ALU.mult), reads=[t1t, rt_], writes=[sct])
                steps.append(s1)

                def mk_stt(c0):
                    def f():
                        for c in range(c0, c0 + 4):
                            P.op("dve", lambda e, c=c: e.scalar_tensor_tensor(out=hmt[:, c, :], in0=numsb[:, c, 0:256], scalar=sc[:, c:c + 1], in1=soh[:, c, :], op0=ALU.mult, op1=ALU.mult),
                                 reads=[numt[c], sct, soht], writes=[hmtt[c]])
                    return f
                for c0 in range(0, 16, 4):
                    steps.append(mk_stt(c0))

                def mk_tr(vc):
                    def f():
                        for c4 in range(4):
                            for ci in range(4):
                                c = c4 * 4 + ci
                                P.op("pe", lambda e, c=c, ci=ci, vc=vc: e.transpose(pX[:, ci * 128:(ci + 1) * 128], hmt[:, c, vc * 128:(vc + 1) * 128], identb[:]),
                                     reads=[hmtt[c], identbt], writes=[pXt])
                            P.op("act", lambda e, c4=c4, vc=vc: e.activation(out=hmT[:, vc, c4 * 512:(c4 + 1) * 512], in_=pX[:], func=AF.Copy), reads=[pXt], writes=[hmTt[vc][c4]])
                        r0 = h * 256 + vc * 128
                        P.dma("sp", catT[r0:r0 + 128, s * S:(s + 1) * S], hmT[:, vc, :], reads=hmTt[vc], writes=[dt["catT"]])
                    return f
                steps.append(mk_tr(0))
                steps.append(mk_tr(1))
                while pending:
                    pending.pop(0)()
                if idx + 1 < 8 and not so_done:
                    load_so(idx + 1)
                return steps
            for h in range(4):
                pending = do_head(s, h, pending)
        while pending:
            pending.pop(0)()


def st_conf(C, uT, catT, cw_d, cb_d, lg_d, lb_d):
    P = C.P
    dt = C.dt
    uv = uT.rearrange("(kc p) t -> p kc t", p=128)
    with Stage(C) as st:
        ident = st.sb([128, 128], F32); identt = Tk()
        onesf = st.sb([128, 128], F32); onest = Tk()
        make_ident(P, ident, identt)
        P.op("pool", lambda e: e.memset(onesf[:], 1.0), writes=[onest])
        cw = st.sb([128, 8, 31], F32); cb = st.sb([128, 8], F32); lg = st.sb([128, 8], F32); lb = st.sb([128, 8], F32); ct = Tk()
        P.dma("sp", cw[:], cw_d, writes=[ct]); P.dma("sp", cb[:], cb_d, writes=[ct]); P.dma("sp", lg[:], lg_d, writes=[ct]); P.dma("sp", lb[:], lb_d, writes=[ct])
        U = st.sb([128, 2, 8, 30 + S], BF16); Ut = tks(2, 8); Upt = Tk()
        P.op("pool", lambda e: e.memset(U[:, :, :, 0:30], 0.0), writes=[Upt])
        Dg = st.sb([128, 8, 31, 128], BF16); Dgt = tks(8)
        Y = st.sb([128, 2, 8, 512], F32); Yt = tks(2, 8)
        Ysq = st.sb([128, 8, 512], F32); Ysqt = tks(8)
        mean = st.sb([128, 512], F32); meant = Tk()
        rstd = st.sb([128, 512], F32); rstdt = Tk()
        ysum = st.sb([128, 2, 512], F32); ysumt = tks(2)
        t1 = Rot(st, 2, [512], F32)
        ob = Rot(st, 3, [512], BF16)
        pb, pbt = st.psbanks(6)
        for s in range(2):
            for cc in range(8):
                P.dma("sp", U[:, s, cc, 30:], uv[:, cc, s * S:(s + 1) * S], reads=[dt["uT"]], writes=[Ut[s][cc]])
        for cc in range(8):
            for k in range(31):
                eng = ("act", "dve")[k % 2]
                if eng == "act":
                    P.op(eng, lambda e, k=k, cc=cc: e.activation(out=Dg[:, cc, k, :], in_=ident[:], func=AF.Identity, scale=cw[:, cc, k:k + 1]),
                         reads=[identt, ct], writes=[Dgt[cc]])
                else:
                    P.op(eng, lambda e, k=k, cc=cc: e.tensor_scalar(out=Dg[:, cc, k, :], in0=ident[:], scalar1=cw[:, cc, k:k + 1], scalar2=None, op0=ALU.mult),
                         reads=[identt, ct], writes=[Dgt[cc]])
        bi_ = [0]

        def conv_cc(it, cc):
            s, tb = it // 4, it % 4
            yp = it % 2
            b = bi_[0] % 4
            bi_[0] += 1
            for k in range(31):
                P.op("pe", lambda e, k=k, cc=cc, tb=tb, b=b, s=s: e.matmul(pb[b][:], lhsT=Dg[:, cc, k, :], rhs=U[:, s, cc, tb * 512 + k: tb * 512 + k + 512], start=(k == 0), stop=(k == 30)),
                     reads=[Dgt[cc], Ut[s][cc], Upt], writes=[pbt[b]])
            P.op("act", lambda e, cc=cc, b=b, yp=yp: e.activation(out=Y[:, yp, cc, :], in_=pb[b][:], func=AF.Identity, bias=cb[:, cc:cc + 1]),
                 reads=[pbt[b], ct], writes=[Yt[yp][cc]])

        def ln_a(it):
            yp = it % 2
            for cc in range(8):
                P.op("act", lambda e, cc=cc, yp=yp: e.activation(out=Ysq[:, cc, :], in_=Y[:, yp, cc, :], func=AF.Square), reads=[Yt[yp][cc]], writes=[Ysqt[cc]])
            P.op("dve", lambda e, yp=yp: e.tensor_tensor(out=ysum[:, 0, :], in0=Y[:, yp, 0, :], in1=Y[:, yp, 1, :], op=ALU.add), reads=[Yt[yp][0], Yt[yp][1]], writes=[ysumt[0]])
            for cc in range(2, 8):
                P.op("dve", lambda e, yp=yp, cc=cc: e.tensor_tensor(out=ysum[:, 0, :], in0=ysum[:, 0, :], in1=Y[:, yp, cc, :], op=ALU.add), reads=[Yt[yp][cc], ysumt[0]], writes=[ysumt[0]])
            P.op("dve", lambda e: e.tensor_tensor(out=ysum[:, 1, :], in0=Ysq[:, 0, :], in1=Ysq[:, 1, :], op=ALU.add), reads=[Ysqt[0], Ysqt[1]], writes=[ysumt[1]])
            for cc in range(2, 8):
                P.op("dve", lambda e, cc=cc: e.tensor_tensor(out=ysum[:, 1, :], in0=ysum[:, 1, :], in1=Ysq[:, cc, :], op=ALU.add), reads=[Ysqt[cc], ysumt[1]], writes=[ysumt[1]])

        def ln_b(it):
            P.op("pe", lambda e: e.matmul(pb[4][:], lhsT=onesf[:], rhs=ysum[:, 0, :], start=True, stop=True), reads=[onest, ysumt[0]], writes=[pbt[4]])
            P.op("pe", lambda e: e.matmul(pb[5][:], lhsT=onesf[:], rhs=ysum[:, 1, :], start=True, stop=True), reads=[onest, ysumt[1]], writes=[pbt[5]])
            P.op("dve", lambda e: e.tensor_scalar(out=mean[:], in0=pb[4][:], scalar1=1.0 / D, scalar2=None, op0=ALU.mult), reads=[pbt[4]], writes=[meant])
            P.op("dve", lambda e: e.tensor_tensor(out=rstd[:], in0=mean[:], in1=mean[:], op=ALU.mult), reads=[meant], writes=[rstdt])
            P.op("dve", lambda e: e.scalar_tensor_tensor(out=rstd[:], in0=pb[5][:], scalar=1.0 / D, in1=rstd[:], op0=ALU.mult, op1=ALU.subtract), reads=[pbt[5], rstdt], writes=[rstdt])
            P.op("dve", lambda e: e.tensor_scalar(out=rstd[:], in0=rstd[:], scalar1=EPS, scalar2=None, op0=ALU.add), reads=[rstdt], writes=[rstdt])
            P.op("act", lambda e: e.activation(out=rstd[:], in_=rstd[:], func=AF.Sqrt), reads=[rstdt], writes=[rstdt])
            P.op("dve", lambda e: e.reciprocal(out=rstd[:], in_=rstd[:]), reads=[rstdt], writes=[rstdt])

        def ln_c(it, ccs):
            s, tb = it // 4, it % 4
            yp = it % 2
            for cc in ccs:
                k, kt = t1.next()
                P.op("dve", lambda e, cc=cc, k=k, yp=yp: e.tensor_tensor(out=t1.t[:, k, :], in0=Y[:, yp, cc, :], in1=mean[:], op=ALU.subtract), reads=[Yt[yp][cc], meant], writes=[kt])
                P.op("dve", lambda e, k=k: e.tensor_tensor(out=t1.t[:, k, :], in0=t1.t[:, k, :], in1=rstd[:], op=ALU.mult), reads=[kt, rstdt], writes=[kt])
                k2, k2t = ob.next()
                P.op("act", lambda e, cc=cc, k=k, k2=k2: e.activation(out=ob.t[:, k2, :], in_=t1.t[:, k, :], func=AF.Silu, scale=lg[:, cc:cc + 1], bias=lb[:, cc:cc + 1]),
                     reads=[kt, ct], writes=[k2t])
                r0 = 1024 + cc * 128
                P.dma("sp", catT[r0:r0 + 128, s * S + tb * 512: s * S + (tb + 1) * 512], ob.t[:, k2, :], reads=[k2t], writes=[dt["catT"]])

        for cc in range(8):
            conv_cc(0, cc)
        for it in range(8):
            nxt = it + 1 if it + 1 < 8 else None
            parts = [lambda it=it: ln_a(it), lambda it=it: ln_b(it), lambda it=it: ln_c(it, range(0, 4)), lambda it=it: ln_c(it, range(4, 8))]
            for pi, part in enumerate(parts):
                if nxt is not None:
                    conv_cc(nxt, 2 * pi)
                    conv_cc(nxt, 2 * pi + 1)
                part()


def st_projres(C, src, srcn, W, KC, Rin, rinn, Rout, routn):
    P = C.P
    dt = C.dt
    groups = [(c * 128,) for c in range(8)]
    with Stage(C) as st:
        LA = 5
        rt = Rot(st, LA + 2, [512], F32)
        ot = Rot(st, 4, [512], F32)
        tiles = [(s, gi, half, tb) for s in range(2) for gi in range(8) for half in range(2) for tb in range(2)]
        issued = {}
        ptr = [0]

        def issue(i):
            s, gi, half, tb = tiles[i]
            t0 = s * S + half * 1024 + tb * 512
            k, kt = rt.next()
            P.dma("sp", rt.t[:, k, :], Rin[gi * 128:(gi + 1) * 128, t0:t0 + 512], reads=[dt[rinn]], writes=[kt])
            issued[i] = (k, kt)

        def epi(s, half, gi, grp, banks, pb, pbt, alloc):
            for tb in range(2):
                i = ptr[0]
                ptr[0] += 1
                assert tiles[i] == (s, gi, half, tb)
                if i == 0:
                    for j in range(min(LA, len(tiles))):
                        issue(j)
                if i + LA < len(tiles):
                    issue(i + LA)
                k, kt = issued.pop(i)
                t0 = s * S + half * 1024 + tb * 512
                b = banks[0][tb]
                k2, k2t = ot.next()
                P.op("dve", lambda e, b=b, k=k, k2=k2: e.tensor_tensor(out=ot.t[:, k2, :], in0=pb[b][:], in1=rt.t[:, k, :], op=ALU.add), reads=[pbt[b], kt], writes=[k2t])
                P.dma("sp", Rout[gi * 128:(gi + 1) * 128, t0:t0 + 512], ot.t[:, k2, :], reads=[k2t], writes=[dt[routn]])
        proj_fm(C, st, src, dt[srcn], W, groups, epi, KC)


def st_ffnup(C, hn, w_gu, aT, fcw_d, fcb_d, norm=None, outproj=None):
    P = C.P
    dt = C.dt
    NJ = DFF // 128
    groups = [(j * 128, DFF + j * 128) for j in range(NJ)]
    with Stage(C) as st:
        pre = None
        if norm is not None:
            pre = alloc_fused(st)
            pre["hooks"] = norm_into(C, st, norm[0], norm[1], norm[2], pre["xin"], pre["xint"], pre["pb"], pre["pbt"], (6, 7), outproj=outproj)
        ident = st.sb([128, 128], F32); identt = Tk()
        make_ident(P, ident, identt)
        fcw = st.sb([128, NJ, 3], F32); fcb = st.sb([128, NJ], F32); ct = Tk()
        P.dma("sp", fcw[:], fcw_d, writes=[ct]); P.dma("sp", fcb[:], fcb_d, writes=[ct])
        Gsb = st.sb([128, 2, 2 + S], BF16); Gt = tks(2, 4); Gpt = Tk()
        P.op("pool", lambda e: e.memset(Gsb[:, :, 0:2], 0.0), writes=[Gpt])
        cv = Rot(st, 3, [512], F32)
        sg = Rot(st, 3, [512], F32)
        ob = Rot(st, 3, [512], BF16)
        state = {"n": 0}

        def epi(s, half, gi, grp, banks, pb, pbt, alloc):
            if half == 0:
                state["n"] += 1
            gp = state["n"] % 2
            for tb in range(2):
                q = half * 2 + tb
                b = banks[0][tb]
                P.op("act", lambda e, b=b, gp=gp, q=q: e.activation(out=Gsb[:, gp, 2 + q * 512: 2 + (q + 1) * 512], in_=pb[b][:], func=AF.Copy), reads=[pbt[b]], writes=[Gt[gp][q]])
            tl = []
            for tb in range(2):
                q = half * 2 + tb
                rd = [Gt[gp][q], Gpt, ct] + ([Gt[gp][q - 1]] if q > 0 else [])
                kc_, kct = cv.next()
                k1, k1t = sg.next()
                k2, k2t = ob.next()
                tl.append((tb, q, rd, kc_, kct, k1, k1t, k2, k2t))
            for (tb, q, rd, kc_, kct, k1, k1t, k2, k2t) in tl:
                P.op("act", lambda e, kc_=kc_, gp=gp, q=q, gi=gi: e.activation(out=cv.t[:, kc_, :], in_=Gsb[:, gp, q * 512: q * 512 + 512], func=AF.Identity,
                                                                               scale=fcw[:, gi, 0:1], bias=fcb[:, gi:gi + 1]), reads=rd, writes=[kct])
            for (tb, q, rd, kc_, kct, k1, k1t, k2, k2t) in tl:
                for k in (1, 2):
                    P.op("dve", lambda e, kc_=kc_, gp=gp, q=q, gi=gi, k=k: e.scalar_tensor_tensor(out=cv.t[:, kc_, :], in0=Gsb[:, gp, q * 512 + k: q * 512 + k + 512], scalar=fcw[:, gi, k:k + 1],
                                                                                                  in1=cv.t[:, kc_, :], op0=ALU.mult, op1=ALU.add), reads=rd + [kct], writes=[kct])
            for (tb, q, rd, kc_, kct, k1, k1t, k2, k2t) in tl:
                P.op("act", lambda e, kc_=kc_, k1=k1: e.activation(out=sg.t[:, k1, :], in_=cv.t[:, kc_, :], func=AF.Silu), reads=[kct], writes=[k1t])
            for (tb, q, rd, kc_, kct, k1, k1t, k2, k2t) in tl:
                bu = banks[1][tb]
                t0 = s * S + q * 512
                P.op("dve", lambda e, bu=bu, k1=k1, k2=k2: e.tensor_tensor(out=ob.t[:, k2, :], in0=pb[bu][:], in1=sg.t[:, k1, :], op=ALU.mult), reads=[pbt[bu], k1t], writes=[k2t])
                P.dma("sp", aT[gi * 128:(gi + 1) * 128, t0:t0 + 512], ob.t[:, k2, :], reads=[k2t], writes=[dt["aT"]])
        proj_fm(C, st, hn, dt["hn"], w_gu, groups, epi, 8, pre=pre)


def st_qk(C, hn, w_qkv, qT1, kT1, cos_d, sin_d, norm=None, v1=None):
    P = C.P
    dt = C.dt
    blocks = [(g, j, h) for g in range(3) for j in range(2) for h in range(4)]
    groups = [(g * 1536 + j * 512 + h * 128,) for (g, j, h) in blocks]
    with Stage(C) as st:
        pre = None
        if norm is not None:
            pre = alloc_fused(st)
            if v1 is None:
                pre["hooks"] = norm_into(C, st, norm[0], norm[1], norm[2], pre["xin"], pre["xint"], pre["pb"], pre["pbt"], (6, 7), hn=hn, hnt=dt["hn"])
            else:
                pre["hooks"] = norm_into(C, st, norm[0], norm[1], norm[2], pre["xin"], pre["xint"], pre["pb"], pre["pbt"], (6, 7))
        if v1 is not None:
            Wv_ = w_qkv.rearrange("(kc p) n -> p kc n", p=128)
            wv = st.sb([128, 8, 1536], BF16); wvt = tks(8)

            def load_wv():
                for kc in range(8):
                    for g in range(3):
                        P.dma("pool", wv[:, kc, g * 512:(g + 1) * 512], Wv_[:, kc, g * 1536 + 1024: g * 1536 + 1536], writes=[wvt[kc]])
            pre["hooks"].append(load_wv)
            vst = Rot(st, 2, [1536], BF16)
        cos2 = st.sb([128, S], F32); sin2 = st.sb([128, S], F32); ct = Tk()
        P.dma("sp", cos2[:], cos_d, writes=[ct]); P.dma("sp", sin2[:], sin_d, writes=[ct])
        pif = st.sb([128, 128], F32); pib = st.sb([128, 128], BF16); pit = Tk()
        P.op("pool", lambda e: e.memset(pif[:], 0.0), writes=[pit])
        P.op("pool", lambda e: e.affine_select(out=pif[:], in_=pif[:], pattern=[[-1, 128]], compare_op=ALU.not_equal, fill=-1.0, base=-64, channel_multiplier=1), reads=[pit], writes=[pit])
        P.op("pool", lambda e: e.affine_select(out=pif[:], in_=pif[:], pattern=[[-1, 128]], compare_op=ALU.not_equal, fill=1.0, base=64, channel_multiplier=1), reads=[pit], writes=[pit])
        P.op("pool", lambda e: e.tensor_copy(out=pib[:], in_=pif[:]), reads=[pit], writes=[pit])
        qb = Rot(st, 4, [512], BF16)
        ta = Rot(st, 2, [512], F32)
        tb_ = Rot(st, 2, [512], F32)
        ob = Rot(st, 3, [512], BF16)

        def epi(s, half, gi, grp, banks, pb, pbt, alloc):
            g, j, h = blocks[gi]
            dst, dn = (qT1, "qT1") if j == 0 else (kT1, "kT1")
            r0 = (g * 4 + h) * 128
            for tb in range(2):
                p0 = half * 1024 + tb * 512
                b0 = banks[0][tb]
                k0, k0t = qb.next()
                P.op("act", lambda e, b0=b0, k0=k0: e.activation(out=qb.t[:, k0, :], in_=pb[b0][:], func=AF.Copy), reads=[pbt[b0]], writes=[k0t])
                b1 = alloc()
                P.op("pe", lambda e, b1=b1, k0=k0: e.matmul(pb[b1][:], lhsT=pib[:], rhs=qb.t[:, k0, :], start=True, stop=True), reads=[pit, k0t], writes=[pbt[b1]])
                k1, k1t = ta.next()
                k2, k2t = tb_.next()
                k3, k3t = ob.next()
                P.op("dve", lambda e, b0=b0, k1=k1, p0=p0: e.tensor_tensor(out=ta.t[:, k1, :], in0=pb[b0][:], in1=cos2[:, p0:p0 + 512], op=ALU.mult), reads=[pbt[b0], ct, k0t], writes=[k1t])
                P.op("dve", lambda e, b1=b1, k2=k2, p0=p0: e.tensor_tensor(out=tb_.t[:, k2, :], in0=pb[b1][:], in1=sin2[:, p0:p0 + 512], op=ALU.mult), reads=[pbt[b1], ct], writes=[k2t])
                P.op("dve", lambda e, k1=k1, k2=k2, k3=k3: e.tensor_tensor(out=ob.t[:, k3, :], in0=ta.t[:, k1, :], in1=tb_.t[:, k2, :], op=ALU.add), reads=[k1t, k2t], writes=[k3t])
                P.dma("sp", dst[r0:r0 + 128, s * S + p0: s * S + p0 + 512], ob.t[:, k3, :], reads=[k3t], writes=[dt[dn]])
        proj_fm(C, st, hn, dt["hn"], w_qkv, groups, epi, 8, nslot=1, pre=pre)
        if v1 is not None:
            xin, xint, pb, pbt = pre["xin"], pre["xint"], pre["pb"], pre["pbt"]
            bi = 0
            for s in range(2):
                for tt in range(16):
                    banks = [(bi + g) % 8 for g in range(3)]
                    bi += 3
                    for kc in range(8):
                        for g in range(3):
                            b = banks[g]
                            P.op("pe", lambda e, b=b, kc=kc, tt=tt, g=g, s=s: e.matmul(pb[b][:], lhsT=xin[:, s, kc, tt * 128:(tt + 1) * 128], rhs=wv[:, kc, g * 512:(g + 1) * 512],
                                                                                       start=(kc == 0), stop=(kc == 7)),
                                 reads=[xint[s][kc], wvt[kc]], writes=[pbt[b]])
                    r0 = s * S + tt * 128
                    k, kt = vst.next()
                    for g in range(3):
                        b = banks[g]
                        if g == 1:
                            P.op("dve", lambda e, b=b, k=k, g=g: e.tensor_copy(out=vst.t[:, k, g * 512:(g + 1) * 512], in_=pb[b][:]), reads=[pbt[b]], writes=[kt])
                        else:
                            P.op("act", lambda e, b=b, k=k, g=g: e.activation(out=vst.t[:, k, g * 512:(g + 1) * 512], in_=pb[b][:], func=AF.Copy), reads=[pbt[b]], writes=[kt])
                    P.dma("sp", v1[r0:r0 + 128, :], vst.t[:, k, :], reads=[kt], writes=[dt["v1"]])


def st_v1(C, hn, w_qkv, v1):
    P = C.P

    def mk(st):
        vst = Rot(st, 3, [1536], BF16)

        def env(s, tt, si, cgs, banks, pb, pbt):
            r0 = s * S + tt * 128
            k, kt = vst.next()
            for g in range(3):
                b = banks[g]
                if g == 1:
                    P.op("dve", lambda e, b=b, k=k, g=g: e.tensor_copy(out=vst.t[:, k, g * 512:(g + 1) * 512], in_=pb[b][:]), reads=[pbt[b]], writes=[kt])
                else:
                    P.op("act", lambda e, b=b, k=k, g=g: e.activation(out=vst.t[:, k, g * 512:(g + 1) * 512], in_=pb[b][:], func=AF.Copy), reads=[pbt[b]], writes=[kt])
            P.dma("sp", v1[r0:r0 + 128, :], vst.t[:, k, :], reads=[kt], writes=[C.dt["v1"]])
        return env
    st_proj_tok(C, hn, C.dt["hn"], w_qkv, 0, 0, [[(0, 512), (512, 512), (1024, 512)]], mk,
                wranges=[(g * 1536 + 1024, 512) for g in range(3)])


PATTERNS = ((128, 1), (512, 4), (2048, 16))


def st_attn(C, qT1, kT1, v1, numg):
    P = C.P
    dt = C.dt
    with Stage(C) as st:
        mask2 = st.sb([128, 256], BF16); mt = Tk()
        P.op("pool", lambda e: e.memset(mask2[:], 1.0), writes=[mt])
        P.op("pool", lambda e: e.affine_select(out=mask2[:, 0:128], in_=mask2[:, 0:128], pattern=[[1, 128]], compare_op=ALU.is_ge, fill=0.0, base=0, channel_multiplier=-1), reads=[mt], writes=[mt])
        P.op("pool", lambda e: e.affine_select(out=mask2[:, 128:256], in_=mask2[:, 128:256], pattern=[[-1, 128]], compare_op=ALU.is_ge, fill=0.0, base=0, channel_multiplier=1), reads=[mt], writes=[mt])
        qTg2 = st.sb([128, 2, 4, S], BF16); qt2 = tks(2, 4)
        kTg2 = st.sb([128, 2, 4, S], BF16); kt2 = tks(2, 4)
        vaug2 = st.sb([128, 2, 16, 4, 129], BF16); vt2 = tks(2, 16); vot = Tk()
        for bb_ in range(2):
            P.op("pool", lambda e, bb_=bb_: e.memset(vaug2[:, bb_, :, :, 128:129], 1.0), writes=[vot])
        Eb = st.sb([128, 4, 3, 256], BF16); Ebt = tks(4, 3)
        E3 = [0]
        si_ = [0]
        osb = Rot(st, 3, [516], F32)
        pS = [st.ps([128, 512])[:, 0:256] for _ in range(4)]; pSt = tks(4)
        pO = [[st.ps([128, 512])[:, 0:258] for _ in range(2)] for _ in range(2)]; pOt = tks(2, 4)
        si = 0
        sg_list = [(s, g) for s in range(2) for g in range(3)]

        def load_sg(idx):
            s, g = sg_list[idx]
            dil = PATTERNS[g][1]
            nb = 16 // dil
            bb = idx % 2
            qv = qT1[g * 512:(g + 1) * 512, s * S:(s + 1) * S].rearrange("(h p) t -> p h t", p=128)
            kv = kT1[g * 512:(g + 1) * 512, s * S:(s + 1) * S].rearrange("(h p) t -> p h t", p=128)
            P.dma("sp", qTg2[:, bb, :, :], qv, reads=[dt["qT1"]], writes=qt2[bb])
            P.dma("sp", kTg2[:, bb, :, :], kv, reads=[dt["kT1"]], writes=kt2[bb])
            vv = v1[s * S:(s + 1) * S, g * 512:(g + 1) * 512].rearrange("(b i r) (h d) -> r b i h d", i=128, r=dil, h=4)
            for r in range(dil):
                for b in range(nb):
                    P.dma("sp", vaug2[:, bb, r * nb + b, :, 0:128], vv[r, b], reads=[dt["v1"]], writes=[vt2[bb][r * nb + b]])
        load_sg(0)
        for idx, (s, g) in enumerate(sg_list):
            if True:
                window, dil = PATTERNS[g]
                nb = 16 // dil
                if idx + 1 < len(sg_list):
                    load_sg(idx + 1)
                bb = idx % 2
                qTg = qTg2[:, bb]; kTg = kTg2[:, bb]; vaug = vaug2[:, bb]
                qt = qt2[bb]; kt_ = kt2[bb]; vt = vt2[bb]
                nv = numg[g, s * S:(s + 1) * S, :].rearrange("(b i r) f -> r b i f", i=128, r=dil)
                blist = [(r, b) for r in range(dil) for b in range(nb)]

                def phase1(fi, kTg=kTg, qTg=qTg, qt=qt, kt_=kt_, dil=dil, nb=nb, blist=blist):
                    r, b = blist[fi]
                    nq = 256 if b < nb - 1 else 128
                    k0 = r + dil * b * 128
                    e3 = E3[0] % 3
                    E3[0] += 1
                    for h in range(4):
                        sp_ = si_[0] % 4
                        si_[0] += 1
                        ks = slice(k0, k0 + dil * 127 + 1, dil)
                        qs = slice(k0, k0 + dil * (nq - 1) + 1, dil)
                        P.op("pe", lambda e, sp_=sp_, h=h, ks=ks, qs=qs, nq=nq: e.matmul(pS[sp_][:, 0:nq], lhsT=kTg[:, h, ks], rhs=qTg[:, h, qs], start=True, stop=True),
                             reads=[kt_[h], qt[h]], writes=[pSt[sp_]])
                        kt = Ebt[h][e3]
                        P.op("act", lambda e, sp_=sp_, h=h, e3=e3, nq=nq: e.activation(out=Eb[:, h, e3, 0:nq], in_=pS[sp_][:, 0:nq], func=AF.Exp, scale=float(128 ** -0.5)), reads=[pSt[sp_]], writes=[kt])
                        P.op("dve", lambda e, h=h, e3=e3, nq=nq: e.tensor_tensor(out=Eb[:, h, e3, 0:nq], in0=Eb[:, h, e3, 0:nq], in1=mask2[:, 0:nq], op=ALU.mult), reads=[kt, mt], writes=[kt])
                    return e3

                def phase2(fi, e3, e3prev, vaug=vaug, vt=vt, nv=nv, nb=nb, blist=blist):
                    r, b = blist[fi]
                    rb = r * nb + b
                    for h in range(4):
                        hp, hh = h // 2, h % 2
                        if b > 0:
                            P.op("pe", lambda e, b=b, hp=hp, hh=hh, rb=rb, h=h: e.matmul(pO[b % 2][hp][:, hh * 129:(hh + 1) * 129], lhsT=Eb[:, h, e3prev, 128:256], rhs=vaug[:, rb - 1, h, :], start=True, stop=False),
                                 reads=[Ebt[h][e3prev], vt[rb - 1], vot], writes=[pOt[b % 2][h]])
                        P.op("pe", lambda e, b=b, hp=hp, hh=hh, rb=rb, h=h: e.matmul(pO[b % 2][hp][:, hh * 129:(hh + 1) * 129], lhsT=Eb[:, h, e3, 0:128], rhs=vaug[:, rb, h, :], start=(b == 0), stop=True),
                             reads=[Ebt[h][e3], vt[rb], vot], writes=[pOt[b % 2][h]])
                    k, kt = osb.next()
                    for hp in range(2):
                        if hp == 0:
                            P.op("act", lambda e, k=k, b=b, hp=hp: e.activation(out=osb.t[:, k, hp * 258:(hp + 1) * 258], in_=pO[b % 2][hp][:], func=AF.Copy),
                                 reads=[pOt[b % 2][2 * hp], pOt[b % 2][2 * hp + 1]], writes=[kt])
                        else:
                            P.op("dve", lambda e, k=k, b=b, hp=hp: e.tensor_copy(out=osb.t[:, k, hp * 258:(hp + 1) * 258], in_=pO[b % 2][hp][:]),
                                 reads=[pOt[b % 2][2 * hp], pOt[b % 2][2 * hp + 1]], writes=[kt])
                    P.dma("sp", nv[r, b], osb.t[:, k, :], reads=[kt], writes=[dt["numg"]])

                es = {0: phase1(0)}
                for fi in range(len(blist)):
                    if fi + 1 < len(blist):
                        es[fi + 1] = phase1(fi + 1)
                    phase2(fi, es[fi], es.get(fi - 1, 0))


def st_merge(C, numg, oT):
    P = C.P
    dt = C.dt
    with Stage(C) as st:
        identb = st.sb([128, 128], BF16); identbt = Tk()
        identf = st.sb([128, 128], F32); identft = Tk()
        make_ident(P, identf, identft)
        P.op("pool", lambda e: e.tensor_copy(out=identb[:], in_=identf[:]), reads=[identft], writes=[identbt])
        n3 = Rot(st, 4, [3, 516], F32)
        acc = Rot(st, 3, [516], F32)
        rr = Rot(st, 3, [4], F32)
        otok = Rot(st, 3, [512], BF16)
        oTs = st.sb([128, 4, S], BF16); oTt = tks(4, 16)
        pX = [st.ps([128, 1024], BF16)[:, 0:512] for _ in range(2)]; pXt = tks(2)
        for s in range(2):
            for tt in range(16):
                r0 = s * S + tt * 128
                k, kt = n3.next()
                P.dma("sp", n3.t[:, k, :, :], numg[:, r0:r0 + 128, :].rearrange("g t f -> t g f"), reads=[dt["numg"]], writes=[kt])
                k2, k2t = acc.next()
                P.op("dve", lambda e, k=k, k2=k2: e.tensor_tensor(out=acc.t[:, k2, :], in0=n3.t[:, k, 0, :], in1=n3.t[:, k, 1, :], op=ALU.add), reads=[kt], writes=[k2t])
                P.op("dve", lambda e, k=k, k2=k2: e.tensor_tensor(out=acc.t[:, k2, :], in0=acc.t[:, k2, :], in1=n3.t[:, k, 2, :], op=ALU.add), reads=[kt, k2t], writes=[k2t])
                k3, k3t = rr.next()
                a4 = acc.t[:, k2, :].rearrange("p (h f) -> p h f", f=129)
                P.op("dve", lambda e, a4=a4, k3=k3: e.reciprocal(out=rr.t[:, k3, :], in_=a4[:, :, 128]), reads=[k2t], writes=[k3t])
                k4, k4t = otok.next()
                for h in range(4):
                    P.op("dve", lambda e, a4=a4, k3=k3, k4=k4, h=h: e.tensor_scalar(out=otok.t[:, k4, h * 128:(h + 1) * 128], in0=a4[:, h, 0:128], scalar1=rr.t[:, k3, h:h + 1], scalar2=None, op0=ALU.mult),
                         reads=[k2t, k3t], writes=[k4t])
                px = tt % 2
                for h in range(4):
                    P.op("pe", lambda e, k4=k4, h=h, px=px: e.transpose(pX[px][:, h * 128:(h + 1) * 128], otok.t[:, k4, h * 128:(h + 1) * 128], identb[:]), reads=[k4t, identbt], writes=[pXt[px]])
                P.op("act", lambda e, px=px, tt=tt: e.activation(out=oTs[:, :, tt * 128:(tt + 1) * 128], in_=pX[px][:].rearrange("p (h t) -> p h t", h=4), func=AF.Copy),
                     reads=[pXt[px]], writes=[oTt[h_][tt] for h_ in range(4)])
            for h in range(4):
                P.dma("sp", oT[h * 128:(h + 1) * 128, s * S:(s + 1) * S], oTs[:, h, :], reads=oTt[h], writes=[dt["oT"]])


INPUT_NAMES = {"xT", "w_in", "w_out0", "w_gu0", "w_gu1", "w_down0", "w_down1", "w_qkv", "w_out1", "mixn", "ffnn", "finn",
               "fb", "ib", "hnb", "cw", "cb", "lg", "lb", "fcw", "fcb", "cos2", "sin2"}

ALL_STAGES = ["n_inproj_fm", "inproj_tok", "mlstm", "conf", "out0", "n_ffnup0", "ffndown0",
              "n_qkv", "attn", "merge", "on_ffnup1", "ffndown1", "normf"]


def run_stages(C, stages):
    if stages is None:
        stages = ALL_STAGES
    d = C.dram
    xT = d("xT", [D, T], F32)
    w_in = d("w_in", [D, 6152], F32)
    w_out0 = d("w_out0", [2048, D], F32)
    w_gu = [d("w_gu0", [D, 2 * DFF], F32), d("w_gu1", [D, 2 * DFF], F32)]
    w_down = [d("w_down0", [DFF, D], F32), d("w_down1", [DFF, D], F32)]
    w_qkv = d("w_qkv", [D, 4608], F32)
    w_out1 = d("w_out1", [512, D], F32)
    mixn = d("mixn", [2, 128, 8], F32)
    ffnn = d("ffnn", [2, 128, 8], F32)
    finn = d("finn", [128, 8], F32)
    fb = d("fb", [128, 64], F32)
    ib = d("ib", [128, 64], F32)
    hnb = d("hnb", [128, 1024], F32)
    cw = d("cw", [128, 8, 31], F32)
    cb = d("cb", [128, 8], F32)
    lg = d("lg", [128, 8], F32)
    lb = d("lb", [128, 8], F32)
    fcw = d("fcw", [2, 128, 22, 3], F32)
    fcb = d("fcb", [2, 128, 22], F32)
    cos2 = d("cos2", [128, S], F32)
    sin2 = d("sin2", [128, S], F32)
    hn = d("hn", [D, T], BF16)
    qT0 = d("qT0", [D, T], BF16)
    kT0 = d("kT0", [D, T], BF16)
    uT = d("uT", [D, T], BF16)
    ktok = d("ktok", [T, D], BF16)
    vtok = d("vtok", [T, D], F32)
    so = d("so", [T, D], F32)
    gates = d("gates", [T, 8], F32)
    catT = d("catT", [2048, T], BF16)
    r1T = d("r1T", [D, T], F32)
    aT = d("aT", [DFF, T], BF16)
    r2T = d("r2T", [D, T], F32)
    qT1 = d("qT1", [1536, T], BF16)
    kT1 = d("kT1", [1536, T], BF16)
    v1 = d("v1", [T, 1536], BF16)
    numg = d("numg", [3, T, 516], F32)
    oT = d("oT", [512, T], BF16)
    r3T = d("r3T", [D, T], F32)
    r4T = d("r4T", [D, T], F32)
    outT = d("outT", [D, T], F32)
    dt = C.dt
    for sname in stages:
        if sname == "n_inproj_fm":
            st_inproj_fm(C, hn, w_in, qT0, kT0, uT, norm=(xT, dt["xT"], mixn[0]))
        elif sname == "n_ffnup0":
            st_ffnup(C, hn, w_gu[0], aT, fcw[0], fcb[0], norm=(r1T, dt["r1T"], ffnn[0]))
        elif sname == "n_qk":
            st_qk(C, hn, w_qkv, qT1, kT1, cos2, sin2, norm=(r2T, dt["r2T"], mixn[1]))
        elif sname == "n_qkv":
            st_qk(C, hn, w_qkv, qT1, kT1, cos2, sin2, norm=(r2T, dt["r2T"], mixn[1]), v1=v1)
        elif sname == "n_ffnup1":
            st_ffnup(C, hn, w_gu[1], aT, fcw[1], fcb[1], norm=(r3T, dt["r3T"], ffnn[1]))
        elif sname == "on_ffnup1":
            st_ffnup(C, hn, w_gu[1], aT, fcw[1], fcb[1], norm=(r2T, dt["r2T"], ffnn[1]), outproj=(oT, dt["oT"], w_out1, r3T, dt["r3T"]))
        elif sname == "norm0":
            st_norm(C, xT, dt["xT"], mixn[0], hn, dt["hn"])
        elif sname == "inproj_fm":
            st_inproj_fm(C, hn, w_in, qT0, kT0, uT)
        elif sname == "inproj_tok":
            st_inproj_tok(C, hn, w_in, ktok, vtok, so, gates, hnb)
        elif sname == "mlstm":
            st_mlstm(C, qT0, kT0, ktok, vtok, so, gates, catT, fb, ib, hnb)
        elif sname == "conf":
            st_conf(C, uT, catT, cw, cb, lg, lb)
        elif sname == "out0":
            st_projres(C, catT, "catT", w_out0, 16, xT, "xT", r1T, "r1T")
        elif sname == "norm1":
            st_norm(C, r1T, dt["r1T"], ffnn[0], hn, dt["hn"])
        elif sname == "ffnup0":
            st_ffnup(C, hn, w_gu[0], aT, fcw[0], fcb[0])
        elif sname == "ffndown0":
            st_projres(C, aT, "aT", w_down[0], 22, r1T, "r1T", r2T, "r2T")
        elif sname == "norm2":
            st_norm(C, r2T, dt["r2T"], mixn[1], hn, dt["hn"])
        elif sname == "qk":
            st_qk(C, hn, w_qkv, qT1, kT1, cos2, sin2)
        elif sname == "v1":
            st_v1(C, hn, w_qkv, v1)
        elif sname == "attn":
            st_attn(C, qT1, kT1, v1, numg)
        elif sname == "merge":
            st_merge(C, numg, oT)
        elif sname == "out1":
            st_projres(C, oT, "oT", w_out1, 4, r2T, "r2T", r3T, "r3T")
        elif sname == "norm3":
            st_norm(C, r3T, dt["r3T"], ffnn[1], hn, dt["hn"])
        elif sname == "ffnup1":
            st_ffnup(C, hn, w_gu[1], aT, fcw[1], fcb[1])
        elif sname == "ffndown1":
            st_projres(C, aT, "aT", w_down[1], 22, r3T, "r3T", r4T, "r4T")
        elif sname == "normf":
            st_norm(C, r4T, dt["r4T"], finn, outT, dt["outT"], final=True)
        else:
            raise ValueError(sname)


def build_program(stages=None, ext_in=(), ext_out=("outT",)):
    nc = bass.Bass("TRN2", target_bir_lowering=False)
    top = contextlib.ExitStack()
    with top:
        P = Prog(nc, top)
        C = Ctx(nc, P, ext_in=set(ext_in) | INPUT_NAMES, ext_out=ext_out)
        run_stages(C, stages)
    return nc


def host_params(inputs):
    f = lambda a: np.ascontiguousarray(np.asarray(a, dtype=np.float32))
    p = {}
    p["w_in"] = f(inputs["ab_w_in"][0])
    p["w_out0"] = f(inputs["ab_w_out"][0])
    p["w_gu0"] = f(inputs["ffn_w_gu"][0]); p["w_gu1"] = f(inputs["ffn_w_gu"][1])
    p["w_down0"] = f(inputs["ffn_w_down"][0]); p["w_down1"] = f(inputs["ffn_w_down"][1])
    p["w_qkv"] = f(inputs["c_w_qkv"][0])
    p["w_out1"] = f(inputs["c_w_out"][0])
    pk = lambda v: f(np.asarray(v).reshape(-1, 128).T)
    p["mixn"] = f(np.stack([pk(inputs["mix_norm"][l]) for l in range(2)]))
    p["ffnn"] = f(np.stack([pk(inputs["ffn_norm"][l]) for l in range(2)]))
    p["finn"] = pk(inputs["final_norm"])
    p["fb"] = f(np.broadcast_to(np.tile(np.asarray(inputs["ab_f_bias"][0]), 16)[None, :], (128, 64)))
    p["ib"] = f(np.broadcast_to(np.tile(np.asarray(inputs["ab_i_bias"][0]), 16)[None, :], (128, 64)))
    p["hnb"] = f(np.broadcast_to(np.asarray(inputs["ab_head_norm"][0])[None, :], (128, 1024)))
    p["cw"] = f(np.asarray(inputs["ab_conv_w"][0]).T.reshape(8, 128, 31).transpose(1, 0, 2))
    p["cb"] = pk(inputs["ab_conv_b"][0]); p["lg"] = pk(inputs["ab_ln_g"][0]); p["lb"] = pk(inputs["ab_ln_b"][0])
    p["fcw"] = f(np.stack([np.asarray(inputs["ffn_conv_w"][l]).T.reshape(22, 128, 3).transpose(1, 0, 2) for l in range(2)]))
    p["fcb"] = f(np.stack([pk(inputs["ffn_conv_b"][l]) for l in range(2)]))
    pos = np.arange(S, dtype=np.float32)
    inv = (10000.0 ** (-np.arange(0, 128, 2, dtype=np.float32) / 128)).astype(np.float32)
    ang = (pos[None, :] * inv[:, None]).astype(np.float32)
    p["cos2"] = f(np.concatenate([np.cos(ang), np.cos(ang)], 0))
    p["sin2"] = f(np.concatenate([np.sin(ang), np.sin(ang)], 0))
    return p


_CACHE = {}


def kernel(**inputs):
    x = np.asarray(inputs["x"], dtype=np.float32)
    p = host_params(inputs)
    if "nc" not in _CACHE:
        _CACHE["nc"] = build_program()
    nc = _CACHE["nc"]
    in_maps = []
    for c in range(NCORES):
        m = dict(p)
        m["xT"] = np.ascontiguousarray(x[2 * c:2 * c + 2].reshape(T, D).T)
        in_maps.append(m)
    res = run_bass_kernel_spmd(nc, in_maps, core_ids=list(range(NCORES)))
    out = np.empty((16, S, D), dtype=np.float32)
    for c in range(NCORES):
        out[2 * c:2 * c + 2] = np.asarray(res.results[c]["outT"]).T.reshape(2, S, D)
    return out
```

```python
import contextlib
import numpy as np
import concourse.bass as bass
import concourse.mybir as mybir
from concourse.bass_utils import run_bass_kernel_spmd

F32 = mybir.dt.float32
BF16 = mybir.dt.bfloat16
ALU = mybir.AluOpType
AF = mybir.ActivationFunctionType
AX = mybir.AxisListType

NCORES = 8
T = 4096
S = 2048
D = 1024
DFF = 2816
N_DMA_SEMS = 24
EPS = 1e-6


class Tk:
    __slots__ = ("w", "r")

    def __init__(self):
        self.w = None
        self.r = {}


def tks(*shape):
    if len(shape) == 1:
        return [Tk() for _ in range(shape[0])]
    return [tks(*shape[1:]) for _ in range(shape[0])]


class Prog:
    ENGS = ("pe", "act", "dve", "pool", "sp")

    def __init__(self, nc, stack):
        self.nc = nc
        self.ops = []
        self.base = 0
        self.seq = {e: 0 for e in self.ENGS}
        self.dma_i = 0
        self.slot_last = {}
        self.known = {e: {} for e in self.ENGS}
        self.sems = {}
        self.stack = stack
        self.stage_i = 0
        for k in range(N_DMA_SEMS):
            self.sems[("dma", k)] = stack.enter_context(nc.semaphore("s_dma%d" % k))
        self.n_instr = 0

    def op(self, eng, fn, reads=(), writes=(), dma=False):
        deps = set()
        for t in reads:
            if t.w is not None:
                deps.add(t.w)
        for t in writes:
            if t.w is not None:
                deps.add(t.w)
            deps.update(t.r.values())
        gid = self.base + len(self.ops)
        self.ops.append([eng, fn, deps, dma, False])
        key = ("dma", gid) if dma else eng
        for t in reads:
            t.r[key] = gid
        for t in writes:
            t.w = gid
            t.r = {}
        return gid

    def dma(self, q, out, in_, reads=(), writes=()):
        return self.op(q, lambda e: e.dma_start(out=out, in_=in_), reads, writes, dma=True)

    def flush(self):
        nc = self.nc
        ops = self.ops
        base = self.base
        n = len(ops)
        if n == 0:
            return
        self.stage_i += 1
        for e in self.ENGS:
            if e != "pe" and e in self.sems:
                continue
            self.sems[e] = self.stack.enter_context(nc.semaphore("s_%s_%d" % (e, self.stage_i)))
            self.seq[e] = 0
            for e2 in self.ENGS:
                self.known[e2].pop(e, None)
        self.stage_counts = getattr(self, "stage_counts", [])
        for o in ops:
            for d in o[2]:
                if d >= base:
                    ops[d - base][4] = True
        last = {}
        for i, o in enumerate(ops):
            if not o[3]:
                last[o[0]] = i
        for e, i in last.items():
            ops[i][4] = True
        sig = {}
        dma_prev = {}
        for i, o in enumerate(ops):
            eng, fn, deps, is_dma, signals = o
            if is_dma:
                slot = self.dma_i % N_DMA_SEMS
                val = 16 * (self.dma_i // N_DMA_SEMS + 1)
                sig[i] = (("dma", slot), val)
                if slot in self.slot_last:
                    dma_prev[i] = self.slot_last[slot]
                self.slot_last[slot] = (("dma", slot), val)
                self.dma_i += 1
            elif signals:
                self.seq[eng] += 1
                sig[i] = (eng, self.seq[eng])
        final = {}
        for e in self.ENGS:
            if self.seq[e] > 0:
                final[e] = self.seq[e]
        for slot, kv in self.slot_last.items():
            final[kv[0]] = kv[1]
        sems = self.sems

        def run_engine(ename, e):
            known = self.known[ename]
            for i, o in enumerate(ops):
                eng, fn, deps, is_dma, signals = o
                if eng != ename:
                    continue
                need = {}
                for d in deps:
                    if d < base:
                        continue
                    dd = ops[d - base]
                    if (not dd[3]) and dd[0] == ename and ename == "pe":
                        continue
                    k, v = sig[d - base]
                    if need.get(k, 0) < v:
                        need[k] = v
                if i in dma_prev:
                    k, v = dma_prev[i]
                    if need.get(k, 0) < v:
                        need[k] = v
                for k, v in need.items():
                    if known.get(k, 0) < v:
                        e.wait_ge(sems[k], v)
                        known[k] = v
                        self.n_instr += 1
                ins = fn(e)
                self.n_instr += 1
                if i in sig:
                    k, v = sig[i]
                    ins.then_inc(sems[k], 16 if is_dma else 1)
            for k, v in final.items():
                if known.get(k, 0) < v:
                    e.wait_ge(sems[k], v)
                    known[k] = v

        with nc.Block() as block:
            @block.tensor
            def _(e):
                run_engine("pe", e)

            @block.scalar
            def _(e):
                run_engine("act", e)

            @block.vector
            def _(e):
                run_engine("dve", e)

            @block.gpsimd
            def _(e):
                run_engine("pool", e)

            @block.sync
            def _(e):
                run_engine("sp", e)
        self.stage_counts.append(dict(self.seq))
        self.base += n
        self.ops = []


class Ctx:
    def __init__(self, nc, P, ext_in=(), ext_out=()):
        self.nc = nc
        self.P = P
        self.ext_in = set(ext_in)
        self.ext_out = set(ext_out)
        self.d = {}
        self.dt = {}

    def dram(self, name, shape, dtype):
        if name in self.d:
            return self.d[name]
        kind = "Internal"
        if name in self.ext_in:
            kind = "ExternalInput"
        elif name in self.ext_out:
            kind = "ExternalOutput"
        self.d[name] = self.nc.dram_tensor(name, list(shape), dtype, kind=kind).ap()
        self.dt[name] = Tk()
        return self.d[name]


_UID = [0]


class Stage:
    def __init__(self, C):
        self.C = C
        self.st = contextlib.ExitStack()
        self.n = 0

    def __enter__(self):
        self.st.__enter__()
        return self

    def __exit__(self, *a):
        if a[0] is None:
            self.C.P.flush()
        return self.st.__exit__(*a)

    def sb(self, shape, dtype, name=None):
        _UID[0] += 1
        return self.st.enter_context(self.C.nc.sbuf_tensor("sb%d" % _UID[0], list(shape), dtype))

    def ps(self, shape, dtype=F32):
        _UID[0] += 1
        return self.st.enter_context(self.C.nc.psum_tensor("ps%d" % _UID[0], list(shape), dtype))

    def psbanks(self, n=8):
        return [self.ps([128, 512]) for _ in range(n)], tks(n)


def make_ident(P, t, tk, dtype_is_bf=False):
    P.op("pool", lambda e: e.memset(t[:], 0.0), writes=[tk])
    P.op("pool", lambda e: e.affine_select(out=t[:], in_=t[:], pattern=[[-1, 128]], compare_op=ALU.not_equal,
                                           fill=1.0, base=0, channel_multiplier=1), reads=[tk], writes=[tk])


def st_norm(C, src, srct, g_dram, dst, dstt, final=False):
    P = C.P
    srcv = src.rearrange("(kc p) t -> p kc t", p=128)
    dstv = dst.rearrange("(kc p) t -> p kc t", p=128)
    odt = F32 if final else BF16
    NT = T // 512
    LA = 2
    with Stage(C) as st:
        NR = LA + 1
        R = st.sb([128, NR, 8, 512], F32)
        Rt = tks(NR, 8)
        g = st.sb([128, 8], F32)
        gt = Tk()
        ones = st.sb([128, 128], BF16)
        onest = Tk()
        sq = st.sb([128, 2, 8, 512], BF16)
        sqt = tks(2, 8)
        rs = st.sb([128, 2, 512], F32)
        rst = tks(2)
        ho = st.sb([128, 2, 8, 512], odt)
        hot = tks(2, 8)
        pb, pbt = st.psbanks(2)
        P.dma("sp", g[:], g_dram, writes=[gt])
        P.op("pool", lambda e: e.memset(ones[:], 1.0), writes=[onest])

        def load(i):
            rb = i % NR
            P.dma("sp", R[:, rb, :, :], srcv[:, :, i * 512:(i + 1) * 512], reads=[srct], writes=Rt[rb])
        for i in range(min(LA, NT)):
            load(i)
        for i in range(NT):
            if i + LA < NT:
                load(i + LA)
            par = i % 2
            rb = i % NR
            for kc in range(8):
                P.op("act", lambda e, kc=kc, par=par, rb=rb: e.activation(out=sq[:, par, kc, :], in_=R[:, rb, kc, :], func=AF.Square),
                     reads=[Rt[rb][kc]], writes=[sqt[par][kc]])
            for kc in range(8):
                P.op("pe", lambda e, kc=kc, par=par: e.matmul(pb[par][:], lhsT=ones[:], rhs=sq[:, par, kc, :], start=(kc == 0), stop=(kc == 7)),
                     reads=[onest, sqt[par][kc]], writes=[pbt[par]])
            P.op("dve", lambda e, par=par: e.tensor_scalar(out=rs[:, par, :], in0=pb[par][:], scalar1=1.0 / D, scalar2=EPS, op0=ALU.mult, op1=ALU.add),
                 reads=[pbt[par]], writes=[rst[par]])
            P.op("act", lambda e, par=par: e.activation(out=rs[:, par, :], in_=rs[:, par, :], func=AF.Sqrt), reads=[rst[par]], writes=[rst[par]])
            P.op("dve", lambda e, par=par: e.reciprocal(out=rs[:, par, :], in_=rs[:, par, :]), reads=[rst[par]], writes=[rst[par]])
            for kc in range(8):
                P.op("dve", lambda e, kc=kc, par=par, rb=rb: e.scalar_tensor_tensor(out=ho[:, par, kc, :], in0=R[:, rb, kc, :], scalar=g[:, kc:kc + 1], in1=rs[:, par, :],
                                                                                   op0=ALU.mult, op1=ALU.mult),
                     reads=[Rt[rb][kc], gt, rst[par]], writes=[hot[par][kc]])
            P.dma("sp", dstv[:, :, i * 512:(i + 1) * 512], ho[:, par, :, :], reads=hot[par], writes=[dstt])


def norm_into(C, st, src, srct, g_dram, xin, xint, pb, pbt, banks, hn=None, hnt=None, outproj=None):
    P = C.P
    srcv = src.rearrange("(kc p) t -> p kc t", p=128)
    NT = T // 512
    LA = 2
    NR = LA + 1
    R = st.sb([128, NR, 8, 512], F32)
    Rt = tks(NR, 8)
    g = st.sb([128, 8], F32)
    gt = Tk()
    ones = st.sb([128, 128], BF16)
    onest = Tk()
    sq = st.sb([128, 2, 8, 512], BF16)
    sqt = tks(2, 8)
    rs = st.sb([128, 2, 512], F32)
    rst = tks(2)
    P.dma("sp", g[:], g_dram, writes=[gt])
    P.op("pool", lambda e: e.memset(ones[:], 1.0), writes=[onest])
    if hn is not None:
        hnv = hn.rearrange("(kc p) t -> p kc t", p=128)

    if outproj is not None:
        oT_d, oTt_d, wo_d, rout, routt = outproj
        wo = st.sb([128, 4, D], BF16)
        wot = Tk()
        P.dma("pool", wo[:], wo_d.rearrange("(kc p) n -> p kc n", p=128), writes=[wot])
        oTs = st.sb([128, NR, 4, 512], BF16)
        oTst = tks(NR)
        oTv = oT_d.rearrange("(kc p) t -> p kc t", p=128)
        routv = rout.rearrange("(kc p) t -> p kc t", p=128)

    def load(i):
        rb = i % NR
        P.dma("sp", R[:, rb, :, :], srcv[:, :, i * 512:(i + 1) * 512], reads=[srct], writes=Rt[rb])
        if outproj is not None:
            P.dma("sp", oTs[:, rb, :, :], oTv[:, :, i * 512:(i + 1) * 512], reads=[oTt_d], writes=[oTst[rb]])

    def produce(i):
        rb = i % NR
        if outproj is not None:
            for gi in range(8):
                bk = 4 + gi % 2
                for kc in range(4):
                    P.op("pe", lambda e, bk=bk, kc=kc, gi=gi, rb=rb: e.matmul(pb[bk][:], lhsT=wo[:, kc, gi * 128:(gi + 1) * 128], rhs=oTs[:, rb, kc, :], start=(kc == 0), stop=(kc == 3)),
                         reads=[wot, oTst[rb]], writes=[pbt[bk]])
                P.op("dve", lambda e, bk=bk, gi=gi, rb=rb: e.tensor_tensor(out=R[:, rb, gi, :], in0=R[:, rb, gi, :], in1=pb[bk][:], op=ALU.add),
                     reads=[pbt[bk], Rt[rb][gi]], writes=[Rt[rb][gi]])
            P.dma("sp", routv[:, :, i * 512:(i + 1) * 512], R[:, rb, :, :], reads=Rt[rb], writes=[routt])
    for i in range(min(LA, NT)):
        load(i)

    def tile(i):
        if i + LA < NT:
            load(i + LA)
        produce(i)
        par = i % 2
        rb = i % NR
        s, tb = i // 4, i % 4
        bk = banks[par]
        for kc in range(8):
            P.op("act", lambda e, kc=kc, par=par, rb=rb: e.activation(out=sq[:, par, kc, :], in_=R[:, rb, kc, :], func=AF.Square),
                 reads=[Rt[rb][kc]], writes=[sqt[par][kc]])
        for kc in range(8):
            P.op("pe", lambda e, kc=kc, par=par, bk=bk: e.matmul(pb[bk][:], lhsT=ones[:], rhs=sq[:, par, kc, :], start=(kc == 0), stop=(kc == 7)),
                 reads=[onest, sqt[par][kc]], writes=[pbt[bk]])
        P.op("dve", lambda e, par=par, bk=bk: e.tensor_scalar(out=rs[:, par, :], in0=pb[bk][:], scalar1=1.0 / D, scalar2=EPS, op0=ALU.mult, op1=ALU.add),
             reads=[pbt[bk]], writes=[rst[par]])
        P.op("act", lambda e, par=par: e.activation(out=rs[:, par, :], in_=rs[:, par, :], func=AF.Sqrt), reads=[rst[par]], writes=[rst[par]])
        P.op("dve", lambda e, par=par: e.reciprocal(out=rs[:, par, :], in_=rs[:, par, :]), reads=[rst[par]], writes=[rst[par]])
        for kc in range(8):
            P.op("dve", lambda e, kc=kc, par=par, rb=rb, s=s, tb=tb: e.scalar_tensor_tensor(out=xin[:, s, kc, tb * 512:(tb + 1) * 512], in0=R[:, rb, kc, :], scalar=g[:, kc:kc + 1], in1=rs[:, par, :],
                                                                                     op0=ALU.mult, op1=ALU.mult),
                 reads=[Rt[rb][kc], gt, rst[par]], writes=[xint[s][kc]])
        if hn is not None:
            P.dma("sp", hnv[:, :, i * 512:(i + 1) * 512], xin[:, s, :, tb * 512:(tb + 1) * 512], reads=xint[s], writes=[hnt])
    for i in range(4):
        tile(i)
    return [lambda i=i: tile(i) for i in range(4, NT)]


def alloc_fused(st):
    xin = st.sb([128, 2, 8, S], BF16)
    xint = tks(2, 8)
    pb, pbt = st.psbanks(8)
    return dict(xin=xin, xint=xint, pb=pb, pbt=pbt)


def proj_fm(C, st, src, srct, W, groups, epi, KC, wload=None, nslot=2, pre=None):
    P = C.P
    srcv = src.rearrange("(kc p) t -> p kc t", p=128)
    Wv = W.rearrange("(kc p) n -> p kc n", p=128)
    big = KC > 8
    if big:
        NH = 3
        xin = st.sb([128, NH, KC, 1024], BF16)
        xint = tks(NH, KC)
    elif pre is not None:
        xin, xint = pre["xin"], pre["xint"]
    else:
        xin = st.sb([128, 2, KC, S], BF16)
        xint = tks(2, KC)
    NW = 3
    wt = st.sb([128, NW, nslot, KC, 128], BF16)
    wtt = tks(NW, nslot)
    if pre is not None:
        pb, pbt = pre["pb"], pre["pbt"]
    else:
        pb, pbt = st.psbanks(8)
    bank_i = [0]

    def alloc():
        b = bank_i[0] % 8
        bank_i[0] += 1
        return b
    order = [(s, gi) for s in range(2) for gi in range(len(groups))]

    def load_w(idx):
        s, gi = order[idx]
        grp = groups[gi]
        wb = idx % NW
        if wload is None:
            for j, col in enumerate(grp):
                P.dma("pool", wt[:, wb, j, :, :], Wv[:, :, col:col + 128], writes=[wtt[wb][j]])
        else:
            wload(gi, grp, wt, wtt, wb, Wv)

    def load_x(s):
        for kc in range(KC):
            P.dma("sp", xin[:, s, kc, :], srcv[:, kc, s * S:(s + 1) * S], reads=[srct], writes=[xint[s][kc]])

    def load_xh(hi):
        s, half = hi // 2, hi % 2
        hb = hi % NH
        t0 = s * S + half * 1024
        for kc in range(KC):
            P.dma("sp", xin[:, hb, kc, :], srcv[:, kc, t0:t0 + 1024], reads=[srct], writes=[xint[hb][kc]])
    if big:
        load_xh(0)
        load_w(0)
        load_xh(1)
        if len(order) > 1:
            load_w(1)
        load_xh(2)
    else:
        if pre is None:
            load_x(0)
        load_w(0)
        if len(order) > 1:
            load_w(1)
        if pre is None:
            load_x(1)
    for idx, (s, gi) in enumerate(order):
        grp = groups[gi]
        wb = idx % NW
        if idx + 2 < len(order):
            load_w(idx + 2)
        if pre is not None and pre.get("hooks"):
            pre["hooks"].pop(0)()
        for half in range(2):
            if big:
                xb = (2 * s + half) % NH
                xoff = 0
            else:
                xb = s
                xoff = half * 1024
            banks = []
            for j in range(nslot if wload is not None else len(grp)):
                b0 = alloc()
                b1 = alloc()
                for kc in range(KC):
                    for tb, b in enumerate((b0, b1)):
                        t0 = xoff + tb * 512
                        P.op("pe", lambda e, b=b, wb=wb, j=j, kc=kc, t0=t0, xb=xb: e.matmul(pb[b][:], lhsT=wt[:, wb, j, kc, :], rhs=xin[:, xb, kc, t0:t0 + 512],
                                                                                             start=(kc == 0), stop=(kc == KC - 1)),
                             reads=[wtt[wb][j], xint[xb][kc]], writes=[pbt[b]])
                    if kc == 1 and j == 0 and pre is not None:
                        while pre.get("mid"):
                            pre["mid"].pop(0)()
                banks.append((b0, b1))
            if big and s == 0 and gi == len(groups) - 1 and half == 0:
                load_xh(3)
            epi(s, half, gi, grp, banks, pb, pbt, alloc)


class Rot:
    def __init__(self, st, n, shape, dtype):
        self.t = st.sb([128, n] + list(shape), dtype)
        self.tk = tks(n)
        self.n = n
        self.i = 0

    def next(self):
        k = self.i % self.n
        self.i += 1
        return k, self.tk[k]


def st_proj_tok(C, src, srct, W, col0, ncols, cgroups, epi, wranges=None):
    P = C.P
    srcv = src.rearrange("(kc p) t -> p kc t", p=128)
    Wv = W.rearrange("(kc p) n -> p kc n", p=128)
    if wranges is None:
        wranges = [(col0, ncols)]
    ncols = sum(n for _, n in wranges)
    with Stage(C) as st:
        xin = st.sb([128, 2, 8, S], BF16)
        xint = tks(2, 8)
        wt = st.sb([128, 8, ncols], BF16)
        wtt = tks(8)
        pb, pbt = st.psbanks(8)
        env = epi(st)
        for kc in range(8):
            o = 0
            for (c0, n) in wranges:
                P.dma("pool", wt[:, kc, o:o + n], Wv[:, kc, c0:c0 + n], writes=[wtt[kc]])
                o += n
        for s in range(2):
            for kc in range(8):
                P.dma("sp", xin[:, s, kc, :], srcv[:, kc, s * S:(s + 1) * S], reads=[srct], writes=[xint[s][kc]])
        bi = 0
        for s in range(2):
            for tt in range(16):
                for si, cgs in enumerate(cgroups):
                    banks = []
                    for _ in cgs:
                        banks.append(bi % 8)
                        bi += 1
                    for kc in range(8):
                        for (c0, n), b in zip(cgs, banks):
                            P.op("pe", lambda e, b=b, kc=kc, tt=tt, c0=c0, n=n, s=s: e.matmul(pb[b][:, 0:n], lhsT=xin[:, s, kc, tt * 128:(tt + 1) * 128], rhs=wt[:, kc, c0:c0 + n],
                                                                                               start=(kc == 0), stop=(kc == 7)),
                                 reads=[xint[s][kc], wtt[kc]], writes=[pbt[b]])
                    env(s, tt, si, cgs, banks, pb, pbt)


def st_inproj_fm(C, hn, w_in, qT0, kT0, uT, norm=None):
    P = C.P
    groups = [(c * 128,) for c in range(8)] + [(1024 + c * 128,) for c in range(8)] + [(4104 + c * 128, 5128 + c * 128) for c in range(8)]
    with Stage(C) as st:
        pre = None
        if norm is not None:
            pre = alloc_fused(st)
            pre["hooks"] = norm_into(C, st, norm[0], norm[1], norm[2], pre["xin"], pre["xint"], pre["pb"], pre["pbt"], (6, 7), hn=hn, hnt=C.dt["hn"])
        ob = Rot(st, 4, [512], BF16)
        sg = Rot(st, 3, [512], F32)

        def epi(s, half, gi, grp, banks, pb, pbt, alloc):
            for tb in range(2):
                t0 = s * S + half * 1024 + tb * 512
                k, kt = ob.next()
                if gi < 16:
                    b = banks[0][tb]
                    sc = 1.0 if gi < 8 else 1.0 / 16
                    P.op("act", lambda e, b=b, k=k, sc=sc: e.activation(out=ob.t[:, k, :], in_=pb[b][:], func=AF.Identity, scale=sc), reads=[pbt[b]], writes=[kt])
                    dst, dn = (qT0, "qT0") if gi < 8 else (kT0, "kT0")
                    r0 = (gi % 8) * 128
                else:
                    ba, bg = banks[0][tb], banks[1][tb]
                    k2, k2t = sg.next()
                    P.op("act", lambda e, bg=bg, k2=k2: e.activation(out=sg.t[:, k2, :], in_=pb[bg][:], func=AF.Sigmoid), reads=[pbt[bg]], writes=[k2t])
                    P.op("dve", lambda e, ba=ba, k=k, k2=k2: e.tensor_tensor(out=ob.t[:, k, :], in0=pb[ba][:], in1=sg.t[:, k2, :], op=ALU.mult), reads=[pbt[ba], k2t], writes=[kt])
                    dst, dn = uT, "uT"
                    r0 = (gi - 16) * 128
                P.dma("sp", dst[r0:r0 + 128, t0:t0 + 512], ob.t[:, k, :], reads=[kt], writes=[C.dt[dn]])
        proj_fm(C, st, hn, C.dt["hn"], w_in, groups, epi, 8, pre=pre)


def st_inproj_tok(C, hn, w_in, ktok, vtok, so, gates, hnb_d):
    P = C.P
    cg = [[(0, 512), (512, 512), (1024, 512), (1536, 512)], [(2048, 512), (2560, 512), (3072, 8)]]

    def mk(st):
        hnb = st.sb([128, 1024], F32); hnbt = Tk()
        P.dma("sp", hnb[:], hnb_d, writes=[hnbt])
        kst = Rot(st, 2, [1024], BF16)
        vst = Rot(st, 2, [1024], F32)
        ost = Rot(st, 2, [1024], F32)
        gst = Rot(st, 2, [8], F32)

        def env(s, tt, si, cgs, banks, pb, pbt):
            r0 = s * S + tt * 128
            if si == 0:
                k, kt = kst.next()
                for i in range(2):
                    b = banks[i]
                    P.op("act", lambda e, b=b, k=k, i=i: e.activation(out=kst.t[:, k, i * 512:(i + 1) * 512], in_=pb[b][:], func=AF.Identity, scale=1.0 / 16), reads=[pbt[b]], writes=[kt])
                P.dma("sp", ktok[r0:r0 + 128, :], kst.t[:, k, :], reads=[kt], writes=[C.dt["ktok"]])
                k, kt = vst.next()
                for i in range(2):
                    b = banks[2 + i]
                    P.op("dve", lambda e, b=b, k=k, i=i: e.tensor_copy(out=vst.t[:, k, i * 512:(i + 1) * 512], in_=pb[b][:]), reads=[pbt[b]], writes=[kt])
                P.dma("sp", vtok[r0:r0 + 128, :], vst.t[:, k, :], reads=[kt], writes=[C.dt["vtok"]])
            else:
                k, kt = ost.next()
                for i in range(2):
                    b = banks[i]
                    P.op("act", lambda e, b=b, k=k, i=i: e.activation(out=ost.t[:, k, i * 512:(i + 1) * 512], in_=pb[b][:], func=AF.Sigmoid), reads=[pbt[b]], writes=[kt])
                P.op("dve", lambda e, k=k: e.tensor_tensor(out=ost.t[:, k, :], in0=ost.t[:, k, :], in1=hnb[:], op=ALU.mult), reads=[kt, hnbt], writes=[kt])
                P.dma("sp", so[r0:r0 + 128, :], ost.t[:, k, :], reads=[kt], writes=[C.dt["so"]])
                k, kt = gst.next()
                b = banks[2]
                P.op("dve", lambda e, b=b, k=k: e.tensor_copy(out=gst.t[:, k, :], in_=pb[b][:, 0:8]), reads=[pbt[b]], writes=[kt])
                P.dma("sp", gates[r0:r0 + 128, :], gst.t[:, k, :], reads=[kt], writes=[C.dt["gates"]])
        return env
    st_proj_tok(C, hn, C.dt["hn"], w_in, 1024, 3080, cg, mk)


def st_mlstm(C, qT0, kT0, ktok, vtok, so, gates, catT, fb_d, ib_d, hnb_d):
    P = C.P
    dt = C.dt
    with Stage(C) as st:
        tri = st.sb([128, 128], F32); trit = Tk()
        onesf = st.sb([128, 128], F32); onest = Tk()
        ident = st.sb([128, 128], F32); identt = Tk()
        identb = st.sb([128, 128], BF16); identbt = Tk()
        maskT = st.sb([128, 128], F32); maskt = Tk()
        fb = st.sb([128, 64], F32); ib = st.sb([128, 64], F32); cbt = Tk()
        P.op("pool", lambda e: e.memset(onesf[:], 1.0), writes=[onest])
        make_ident(P, ident, identt)
        P.op("pool", lambda e: e.tensor_copy(out=identb[:], in_=ident[:]), reads=[identt], writes=[identbt])
        P.op("pool", lambda e: e.memset(tri[:], 1.0), writes=[trit])
        P.op("pool", lambda e: e.affine_select(out=tri[:], in_=tri[:], pattern=[[1, 128]], compare_op=ALU.is_ge, fill=0.0, base=0, channel_multiplier=-1), reads=[trit], writes=[trit])
        P.op("pool", lambda e: e.tensor_copy(out=maskT[:], in_=tri[:]), reads=[trit], writes=[maskt])
        P.dma("sp", fb[:], fb_d, writes=[cbt])
        P.dma("sp", ib[:], ib_d, writes=[cbt])
        GF = st.sb([128, 16, 8], F32); gft = Tk()
        IGb = st.sb([128, 16, 4], F32); FGb = st.sb([128, 16, 4], F32); Lt = st.sb([128, 16, 4], F32)
        A = st.sb([128, 16, 4], F32); Bn = st.sb([128, 16, 4], F32); BLn = st.sb([128, 16, 4], F32)
        Amax = st.sb([64, 1], F32); Arep = st.sb([64, 128], F32); Abc = st.sb([128, 16, 4], F32)
        M = st.sb([128, 17, 4], F32); MU = st.sb([128, 16, 4], F32)
        Wg = st.sb([128, 16, 4], F32); Gg = st.sb([128, 16, 4], F32); EN = st.sb([128, 16, 4], F32)
        tmp = st.sb([128, 16, 4], F32)
        gt = {n: Tk() for n in "IGb FGb L A Bn BLn Amax Arep Abc M MU W G EN tmp".split()}
        bk0 = st.ps([128, 512])
        pg = [bk0[:, 0:64], bk0[:, 64:128]]
        _pgt = Tk(); pgt = [_pgt, _pgt]
        qTh2 = st.sb([128, 2, 2, S], BF16); qTht2 = tks(2, 2)
        kTh2 = st.sb([128, 2, 2, S], BF16); kTht2 = tks(2, 2)
        kth2 = st.sb([128, 2, 16, 256], BF16); ktht2 = tks(2)
        vh2 = st.sb([128, 2, 16, 256], F32); vht2 = tks(2)
        soh2 = st.sb([128, 2, 16, 256], F32); soht2 = tks(2)
        vwa = st.sb([128, 16, 257], BF16); vwat = tks(16); vwaot = Tk()
        Cst2 = st.sb([128, 2, 2, 257], F32); Cstt2 = tks(2, 2)
        Cbf2 = st.sb([128, 2, 2, 257], BF16); Cbft2 = tks(2)
        WmT = Rot(st, 3, [128], BF16)
        numsb2 = st.sb([128, 2, 16, 257], F32); numt2 = tks(2, 16)
        sqs_t = st.sb([128, 16, 256], BF16); sqst_t = Tk()
        sm2 = [{n: (st.sb([128, 16], F32), Tk()) for n in "ssn d1 r t1 sc".split()} for _ in range(2)]
        hmt2 = st.sb([128, 2, 16, 256], BF16); hmtt2 = tks(2, 16)
        hmT = st.sb([128, 2, S], BF16); hmTt = tks(2, 4)
        pS = [st.ps([128, 512])[:, 0:128] for _ in range(2)]; pSt = tks(2)
        pN = [st.ps([128, 512])[:, 0:257] for _ in range(2)]; pNt = tks(2)
        pU = [st.ps([128, 512])[:, 0:257] for _ in range(2)]; pUt = tks(2)
        pX = st.ps([128, 1024], BF16)[:, 0:512]; pXt = Tk()
        gv = gates.rearrange("(s c p) g -> s p c g", p=128, c=16)

        def load_head(idx):
            s, h = idx // 4, idx % 4
            hb = idx % 2
            fm = lambda a: a[h * 256:(h + 1) * 256, s * S:(s + 1) * S].rearrange("(dc p) t -> p dc t", p=128)
            tokv = lambda a: a[s * S:(s + 1) * S, h * 256:(h + 1) * 256].rearrange("(c p) d -> p c d", p=128)
            P.dma("sp", qTh2[:, hb], fm(qT0), reads=[dt["qT0"]], writes=qTht2[hb])
            P.dma("sp", kTh2[:, hb], fm(kT0), reads=[dt["kT0"]], writes=kTht2[hb])
            P.dma("sp", kth2[:, hb], tokv(ktok), reads=[dt["ktok"]], writes=[ktht2[hb]])
            P.dma("sp", vh2[:, hb], tokv(vtok), reads=[dt["vtok"]], writes=[vht2[hb]])

        def load_so(idx):
            s, h = idx // 4, idx % 4
            hb = idx % 2
            tokv = lambda a: a[s * S:(s + 1) * S, h * 256:(h + 1) * 256].rearrange("(c p) d -> p c d", p=128)
            P.dma("sp", soh2[:, hb], tokv(so), reads=[dt["so"]], writes=[soht2[hb]])
        load_head(0)
        load_so(0)
        pending = []
        for s in range(2):
            while pending:
                pending.pop(0)()
            P.dma("sp", GF[:], gv[s], reads=[dt["gates"]], writes=[gft])
            fb3 = fb[:].rearrange("p (c h) -> p c h", h=4)
            ib3 = ib[:].rearrange("p (c h) -> p c h", h=4)
            P.op("dve", lambda e: e.tensor_tensor(out=FGb[:], in0=GF[:, :, 4:8], in1=fb3, op=ALU.add), reads=[gft, cbt], writes=[gt["FGb"]])
            P.op("dve", lambda e: e.tensor_tensor(out=IGb[:], in0=GF[:, :, 0:4], in1=ib3, op=ALU.add), reads=[gft, cbt], writes=[gt["IGb"]])
            P.op("act", lambda e: e.activation(out=Lt[:], in_=FGb[:], func=AF.Exp, scale=-1.0), reads=[gt["FGb"]], writes=[gt["L"]])
            P.op("dve", lambda e: e.tensor_scalar(out=Lt[:], in0=Lt[:], scalar1=1.0, scalar2=1.0, op0=ALU.add, op1=ALU.mult), reads=[gt["L"]], writes=[gt["L"]])
            P.op("act", lambda e: e.activation(out=Lt[:], in_=Lt[:], func=AF.Ln), reads=[gt["L"]], writes=[gt["L"]])
            L2 = Lt[:].rearrange("p c h -> p (c h)")
            P.op("pe", lambda e: e.matmul(pg[0][:], lhsT=tri[:], rhs=L2, start=True, stop=True), reads=[trit, gt["L"]], writes=[pgt[0]])
            P.op("pe", lambda e: e.matmul(pg[1][:], lhsT=onesf[:], rhs=L2, start=True, stop=True), reads=[onest, gt["L"]], writes=[pgt[1]])
            f2 = lambda t: t[:].rearrange("p c h -> p (c h)")
            P.op("dve", lambda e: e.tensor_copy(out=f2(Bn), in_=pg[0][:]), reads=[pgt[0]], writes=[gt["Bn"], pgt[0]])
            P.op("dve", lambda e: e.tensor_copy(out=f2(BLn), in_=pg[1][:]), reads=[pgt[1]], writes=[gt["BLn"], pgt[1]])
            P.op("dve", lambda e: e.tensor_tensor(out=A[:], in0=IGb[:], in1=Bn[:], op=ALU.add), reads=[gt["IGb"], gt["Bn"]], writes=[gt["A"]])
            P.op("act", lambda e: e.activation(out=tmp[:], in_=A[:], func=AF.Exp), reads=[gt["A"]], writes=[gt["tmp"]])
            P.op("pe", lambda e: e.matmul(pg[0][:], lhsT=onesf[:], rhs=f2(tmp), start=True, stop=True), reads=[onest, gt["tmp"]], writes=[pgt[0]])
            P.op("act", lambda e: e.activation(out=f2(Abc), in_=pg[0][:], func=AF.Ln), reads=[pgt[0]], writes=[gt["Abc"], pgt[0]])
            P.op("pool", lambda e: e.memset(M[:, 0, :], 0.0), writes=[gt["M"]])
            for c in range(16):
                P.op("dve", lambda e, c=c: e.tensor_tensor(out=MU[:, c, :], in0=M[:, c, :], in1=Abc[:, c, :], op=ALU.max), reads=[gt["M"], gt["Abc"]], writes=[gt["MU"]])
                P.op("dve", lambda e, c=c: e.tensor_tensor(out=M[:, c + 1, :], in0=MU[:, c, :], in1=BLn[:, c, :], op=ALU.subtract), reads=[gt["MU"], gt["BLn"]], writes=[gt["M"]])
            P.op("dve", lambda e: e.tensor_tensor(out=tmp[:], in0=A[:], in1=MU[:], op=ALU.subtract), reads=[gt["A"], gt["MU"]], writes=[gt["tmp"]])
            P.op("act", lambda e: e.activation(out=Wg[:], in_=tmp[:], func=AF.Exp), reads=[gt["tmp"]], writes=[gt["W"]])
            P.op("dve", lambda e: e.tensor_tensor(out=tmp[:], in0=M[:, 0:16, :], in1=MU[:], op=ALU.subtract), reads=[gt["M"], gt["MU"], gt["W"]], writes=[gt["tmp"]])
            P.op("act", lambda e: e.activation(out=Gg[:], in_=tmp[:], func=AF.Exp), reads=[gt["tmp"]], writes=[gt["G"]])
            P.op("dve", lambda e: e.tensor_tensor(out=tmp[:], in0=Bn[:], in1=MU[:], op=ALU.subtract), reads=[gt["Bn"], gt["MU"], gt["G"]], writes=[gt["tmp"]])
            P.op("act", lambda e: e.activation(out=EN[:], in_=tmp[:], func=AF.Exp), reads=[gt["tmp"]], writes=[gt["EN"]])
            def do_head(s, h, pending):
                idx = s * 4 + h
                hb = idx % 2
                if idx + 1 < 8:
                    load_head(idx + 1)
                qTh = qTh2[:, hb]; kTh = kTh2[:, hb]; kth = kth2[:, hb]; vh = vh2[:, hb]; soh = soh2[:, hb]
                qTht = qTht2[hb]; kTht = kTht2[hb]; ktht = ktht2[hb]; vht = vht2[hb]; soht = soht2[hb]
                numsb = numsb2[:, hb]; numt = numt2[hb]; hmt = hmt2[:, hb]; hmtt = hmtt2[hb]; sm = sm2[hb]
                so_done = []
                sqs = sqs_t; sqst = sqst_t
                for c in range(16):
                    P.op("act", lambda e, c=c, h=h: e.activation(out=vwa[:, c, 0:256], in_=vh[:, c, :], func=AF.Identity, scale=Wg[:, c, h:h + 1]),
                         reads=[vht, gt["W"]], writes=[vwat[c]])
                P.op("dve", lambda e, h=h: e.tensor_copy(out=vwa[:, :, 256:257], in_=Wg[:, :, h:h + 1]), reads=[gt["W"]], writes=[vwaot])
                for c in range(16):
                    cs = slice(c * 128, (c + 1) * 128)
                    pp = c % 2
                    sp_ = c % 2
                    for dc in range(2):
                        P.op("pe", lambda e, dc=dc, cs=cs, sp_=sp_: e.matmul(pS[sp_][:], lhsT=kTh[:, dc, cs], rhs=qTh[:, dc, cs], start=(dc == 0), stop=(dc == 1)),
                             reads=[kTht[dc], qTht[dc]], writes=[pSt[sp_]])
                    k, kt = WmT.next()
                    P.op("dve", lambda e, k=k, sp_=sp_: e.tensor_tensor(out=WmT.t[:, k, :], in0=pS[sp_][:], in1=maskT[:], op=ALU.mult), reads=[pSt[sp_], maskt], writes=[kt])
                    if c < 15:
                        for dc in range(2):
                            P.op("pe", lambda e, dc=dc, c=c: e.matmul(pU[dc][:], lhsT=kth[:, c, dc * 128:(dc + 1) * 128], rhs=vwa[:, c, :], start=True, stop=True),
                                 reads=[ktht, vwat[c], vwaot], writes=[pUt[dc]])
                            if c == 0:
                                P.op("dve", lambda e, dc=dc, pp=pp: e.tensor_copy(out=Cst2[:, pp, dc, :], in_=pU[dc][:]), reads=[pUt[dc]], writes=[Cstt2[pp][dc]])
                            else:
                                P.op("dve", lambda e, dc=dc, c=c, h=h, pp=pp: e.scalar_tensor_tensor(out=Cst2[:, pp, dc, :], in0=Cst2[:, 1 - pp, dc, :], scalar=Gg[:, c, h:h + 1], in1=pU[dc][:], op0=ALU.mult, op1=ALU.add),
                                     reads=[pUt[dc], Cstt2[1 - pp][dc], gt["G"]], writes=[Cstt2[pp][dc]])
                        P.op("act", lambda e, c=c, h=h, pp=pp: e.activation(out=Cbf2[:, 1 - pp], in_=Cst2[:, pp], func=AF.Identity, scale=Gg[:, c + 1, h:h + 1]),
                             reads=[gt["G"]] + Cstt2[pp], writes=[Cbft2[1 - pp]])
                    if c > 0:
                        for dc in range(2):
                            P.op("pe", lambda e, dc=dc, cs=cs, sp_=sp_, pp=pp: e.matmul(pN[sp_][:], lhsT=qTh[:, dc, cs], rhs=Cbf2[:, pp, dc, :], start=(dc == 0), stop=False),
                                 reads=[qTht[dc], Cbft2[pp]], writes=[pNt[sp_]])
                    P.op("pe", lambda e, k=k, c=c, sp_=sp_: e.matmul(pN[sp_][:], lhsT=WmT.t[:, k, :], rhs=vwa[:, c, :], start=(c == 0), stop=True),
                         reads=[kt, vwat[c], vwaot], writes=[pNt[sp_]])
                    P.op("act", lambda e, c=c, sp_=sp_: e.activation(out=numsb[:, c, :], in_=pN[sp_][:], func=AF.Copy), reads=[pNt[sp_]], writes=[numt[c]])
                    if c % 2 == 1 and pending:
                        pending.pop(0)()
                        if not pending and idx + 1 < 8 and not so_done:
                            load_so(idx + 1)
                            so_done.append(1)
                ssn, d1, rr, t1, sc = (sm[n][0] for n in "ssn d1 r t1 sc".split())
                ssnt, d1t, rt_, t1t, sct = (sm[n][1] for n in "ssn d1 r t1 sc".split())
                steps = []

                def s0():
                    P.op("act", lambda e: e.activation(out=sqs[:], in_=numsb[:, :, 0:256], func=AF.Square), reads=numt, writes=[sqst])
                    P.op("act", lambda e: e.activation(out=d1[:], in_=numsb[:, :, 256], func=AF.Abs), reads=numt, writes=[d1t])
                steps.append(s0)

                def s1():
                    P.op("dve", lambda e: e.tensor_reduce(out=ssn[:], in_=sqs[:], axis=AX.X, op=ALU.add), reads=[sqst], writes=[ssnt])
                    P.op("dve", lambda e: e.tensor_tensor(out=d1[:], in0=d1[:], in1=EN[:, :, h], op=ALU.max), reads=[d1t, gt["EN"]], writes=[d1t])
                    P.op("dve", lambda e: e.reciprocal(out=rr[:], in_=d1[:]), reads=[d1t], writes=[rt_])
                    P.op("dve", lambda e: e.tensor_tensor(out=t1[:], in0=ssn[:], in1=rr[:], op=ALU.mult), reads=[ssnt, rt_], writes=[t1t])
                    P.op("dve", lambda e: e.tensor_tensor(out=t1[:], in0=t1[:], in1=rr[:], op=ALU.mult), reads=[t1t, rt_], writes=[t1t])
                    P.op("dve", lambda e: e.tensor_scalar(out=t1[:], in0=t1[:], scalar1=1.0 / 256, scalar2=EPS, op0=ALU.mult, op1=ALU.add), reads=[t1t], writes=[t1t])
                    P.op("act", lambda e: e.activation(out=t1[:], in_=t1[:], func=AF.Sqrt), reads=[t1t], writes=[t1t])
                    P.op("dve", lambda e: e.reciprocal(out=t1[:], in_=t1[:]), reads=[t1t], writes=[t1t])
                    P.op("dve", lambda e: e.tensor_tensor(out=sc[:], in0=t1[:], in1=rr[:], op=ALU.mult), reads=[t1t, rt_], writes=[sct])
                steps.append(s1)

                def mk_stt(c0):
                    def f():
                        for c in range(c0, c0 + 4):
                            P.op("dve", lambda e, c=c: e.scalar_tensor_tensor(out=hmt[:, c, :], in0=numsb[:, c, 0:256], scalar=sc[:, c:c + 1], in1=soh[:, c, :], op0=ALU.mult, op1=ALU.mult),
                                 reads=[numt[c], sct, soht], writes=[hmtt[c]])
                    return f
                for c0 in range(0, 16, 4):
                    steps.append(mk_stt(c0))

                def mk_tr(vc):
                    def f():
                        for c4 in range(4):
                            for ci in range(4):
                                c = c4 * 4 + ci
                                P.op("pe", lambda e, c=c, ci=ci, vc=vc: e.transpose(pX[:, ci * 128:(ci + 1) * 128], hmt[:, c, vc * 128:(vc + 1) * 128], identb[:]),
                                     reads=[hmtt[c], identbt], writes=[pXt])
                            P.op("act", lambda e, c4=c4, vc=vc: e.activation(out=hmT[:, vc, c4 * 512:(c4 + 1) * 512], in_=pX[:], func=AF.Copy), reads=[pXt], writes=[hmTt[vc][c4]])
                        r0 = h * 256 + vc * 128
                        P.dma("sp", catT[r0:r0 + 128, s * S:(s + 1) * S], hmT[:, vc, :], reads=hmTt[vc], writes=[dt["catT"]])
                    return f
                steps.append(mk_tr(0))
                steps.append(mk_tr(1))
                while pending:
                    pending.pop(0)()
                if idx + 1 < 8 and not so_done:
                    load_so(idx + 1)
                return steps
            for h in range(4):
                pending = do_head(s, h, pending)
        while pending:
            pending.pop(0)()


def st_conf(C, uT, catT, cw_d, cb_d, lg_d, lb_d):
    P = C.P
    dt = C.dt
    uv = uT.rearrange("(kc p) t -> p kc t", p=128)
    with Stage(C) as st:
        ident = st.sb([128, 128], F32); identt = Tk()
        onesf = st.sb([128, 128], F32); onest = Tk()
        make_ident(P, ident, identt)
        P.op("pool", lambda e: e.memset(onesf[:], 1.0), writes=[onest])
        cw = st.sb([128, 8, 31], F32); cb = st.sb([128, 8], F32); lg = st.sb([128, 8], F32); lb = st.sb([128, 8], F32); ct = Tk()
        P.dma("sp", cw[:], cw_d, writes=[ct]); P.dma("sp", cb[:], cb_d, writes=[ct]); P.dma("sp", lg[:], lg_d, writes=[ct]); P.dma("sp", lb[:], lb_d, writes=[ct])
        U = st.sb([128, 2, 8, 30 + S], BF16); Ut = tks(2, 8); Upt = Tk()
        P.op("pool", lambda e: e.memset(U[:, :, :, 0:30], 0.0), writes=[Upt])
        Dg = st.sb([128, 8, 31, 128], BF16); Dgt = tks(8)
        Y = st.sb([128, 2, 8, 512], F32); Yt = tks(2, 8)
        Ysq = st.sb([128, 8, 512], F32); Ysqt = tks(8)
        mean = st.sb([128, 512], F32); meant = Tk()
        rstd = st.sb([128, 512], F32); rstdt = Tk()
        ysum = st.sb([128, 2, 512], F32); ysumt = tks(2)
        t1 = Rot(st, 2, [512], F32)
        ob = Rot(st, 3, [512], BF16)
        pb, pbt = st.psbanks(6)
        for s in range(2):
            for cc in range(8):
                P.dma("sp", U[:, s, cc, 30:], uv[:, cc, s * S:(s + 1) * S], reads=[dt["uT"]], writes=[Ut[s][cc]])
        for cc in range(8):
            for k in range(31):
                eng = ("act", "dve")[k % 2]
                if eng == "act":
                    P.op(eng, lambda e, k=k, cc=cc: e.activation(out=Dg[:, cc, k, :], in_=ident[:], func=AF.Identity, scale=cw[:, cc, k:k + 1]),
                         reads=[identt, ct], writes=[Dgt[cc]])
                else:
                    P.op(eng, lambda e, k=k, cc=cc: e.tensor_scalar(out=Dg[:, cc, k, :], in0=ident[:], scalar1=cw[:, cc, k:k + 1], scalar2=None, op0=ALU.mult),
                         reads=[identt, ct], writes=[Dgt[cc]])
        bi_ = [0]

        def conv_cc(it, cc):
            s, tb = it // 4, it % 4
            yp = it % 2
            b = bi_[0] % 4
            bi_[0] += 1
            for k in range(31):
                P.op("pe", lambda e, k=k, cc=cc, tb=tb, b=b, s=s: e.matmul(pb[b][:], lhsT=Dg[:, cc, k, :], rhs=U[:, s, cc, tb * 512 + k: tb * 512 + k + 512], start=(k == 0), stop=(k == 30)),
                     reads=[Dgt[cc], Ut[s][cc], Upt], writes=[pbt[b]])
            P.op("act", lambda e, cc=cc, b=b, yp=yp: e.activation(out=Y[:, yp, cc, :], in_=pb[b][:], func=AF.Identity, bias=cb[:, cc:cc + 1]),
                 reads=[pbt[b], ct], writes=[Yt[yp][cc]])

        def ln_a(it):
            yp = it % 2
            for cc in range(8):
                P.op("act", lambda e, cc=cc, yp=yp: e.activation(out=Ysq[:, cc, :], in_=Y[:, yp, cc, :], func=AF.Square), reads=[Yt[yp][cc]], writes=[Ysqt[cc]])
            P.op("dve", lambda e, yp=yp: e.tensor_tensor(out=ysum[:, 0, :], in0=Y[:, yp, 0, :], in1=Y[:, yp, 1, :], op=ALU.add), reads=[Yt[yp][0], Yt[yp][1]], writes=[ysumt[0]])
            for cc in range(2, 8):
                P.op("dve", lambda e, yp=yp, cc=cc: e.tensor_tensor(out=ysum[:, 0, :], in0=ysum[:, 0, :], in1=Y[:, yp, cc, :], op=ALU.add), reads=[Yt[yp][cc], ysumt[0]], writes=[ysumt[0]])
            P.op("dve", lambda e: e.tensor_tensor(out=ysum[:, 1, :], in0=Ysq[:, 0, :], in1=Ysq[:, 1, :], op=ALU.add), reads=[Ysqt[0], Ysqt[1]], writes=[ysumt[1]])
            for cc in range(2, 8):
                P.op("dve", lambda e, cc=cc: e.tensor_tensor(out=ysum[:, 1, :], in0=ysum[:, 1, :], in1=Ysq[:, cc, :], op=ALU.add), reads=[Ysqt[cc], ysumt[1]], writes=[ysumt[1]])

        def ln_b(it):
            P.op("pe", lambda e: e.matmul(pb[4][:], lhsT=onesf[:], rhs=ysum[:, 0, :], start=True, stop=True), reads=[onest, ysumt[0]], writes=[pbt[4]])
            P.op("pe", lambda e: e.matmul(pb[5][:], lhsT=onesf[:], rhs=ysum[:, 1, :], start=True, stop=True), reads=[onest, ysumt[1]], writes=[pbt[5]])
            P.op("dve", lambda e: e.tensor_scalar(out=mean[:], in0=pb[4][:], scalar1=1.0 / D, scalar2=None, op0=ALU.mult), reads=[pbt[4]], writes=[meant])
            P.op("dve", lambda e: e.tensor_tensor(out=rstd[:], in0=mean[:], in1=mean[:], op=ALU.mult), reads=[meant], writes=[rstdt])
            P.op("dve", lambda e: e.scalar_tensor_tensor(out=rstd[:], in0=pb[5][:], scalar=1.0 / D, in1=rstd[:], op0=ALU.mult, op1=ALU.subtract), reads=[pbt[5], rstdt], writes=[rstdt])
            P.op("dve", lambda e: e.tensor_scalar(out=rstd[:], in0=rstd[:], scalar1=EPS, scalar2=None, op0=ALU.add), reads=[rstdt], writes=[rstdt])
            P.op("act", lambda e: e.activation(out=rstd[:], in_=rstd[:], func=AF.Sqrt), reads=[rstdt], writes=[rstdt])
            P.op("dve", lambda e: e.reciprocal(out=rstd[:], in_=rstd[:]), reads=[rstdt], writes=[rstdt])

        def ln_c(it, ccs):
            s, tb = it // 4, it % 4
            yp = it % 2
            for cc in ccs:
                k, kt = t1.next()
                P.op("dve", lambda e, cc=cc, k=k, yp=yp: e.tensor_tensor(out=t1.t[:, k, :], in0=Y[:, yp, cc, :], in1=mean[:], op=ALU.subtract), reads=[Yt[yp][cc], meant], writes=[kt])
                P.op("dve", lambda e, k=k: e.tensor_tensor(out=t1.t[:, k, :], in0=t1.t[:, k, :], in1=rstd[:], op=ALU.mult), reads=[kt, rstdt], writes=[kt])
                k2, k2t = ob.next()
                P.op("act", lambda e, cc=cc, k=k, k2=k2: e.activation(out=ob.t[:, k2, :], in_=t1.t[:, k, :], func=AF.Silu, scale=lg[:, cc:cc + 1], bias=lb[:, cc:cc + 1]),
                     reads=[kt, ct], writes=[k2t])
                r0 = 1024 + cc * 128
                P.dma("sp", catT[r0:r0 + 128, s * S + tb * 512: s * S + (tb + 1) * 512], ob.t[:, k2, :], reads=[k2t], writes=[dt["catT"]])

        for cc in range(8):
            conv_cc(0, cc)
        for it in range(8):
            nxt = it + 1 if it + 1 < 8 else None
            parts = [lambda it=it: ln_a(it), lambda it=it: ln_b(it), lambda it=it: ln_c(it, range(0, 4)), lambda it=it: ln_c(it, range(4, 8))]
            for pi, part in enumerate(parts):
                if nxt is not None:
                    conv_cc(nxt, 2 * pi)
                    conv_cc(nxt, 2 * pi + 1)
                part()


def st_projres(C, src, srcn, W, KC, Rin, rinn, Rout, routn):
    P = C.P
    dt = C.dt
    groups = [(c * 128,) for c in range(8)]
    with Stage(C) as st:
        LA = 5
        rt = Rot(st, LA + 2, [512], F32)
        ot = Rot(st, 4, [512], F32)
        tiles = [(s, gi, half, tb) for s in range(2) for gi in range(8) for half in range(2) for tb in range(2)]
        issued = {}
        ptr = [0]

        def issue(i):
            s, gi, half, tb = tiles[i]
            t0 = s * S + half * 1024 + tb * 512
            k, kt = rt.next()
            P.dma("sp", rt.t[:, k, :], Rin[gi * 128:(gi + 1) * 128, t0:t0 + 512], reads=[dt[rinn]], writes=[kt])
            issued[i] = (k, kt)

        def epi(s, half, gi, grp, banks, pb, pbt, alloc):
            for tb in range(2):
                i = ptr[0]
                ptr[0] += 1
                assert tiles[i] == (s, gi, half, tb)
                if i == 0:
                    for j in range(min(LA, len(tiles))):
                        issue(j)
                if i + LA < len(tiles):
                    issue(i + LA)
                k, kt = issued.pop(i)
                t0 = s * S + half * 1024 + tb * 512
                b = banks[0][tb]
                k2, k2t = ot.next()
                P.op("dve", lambda e, b=b, k=k, k2=k2: e.tensor_tensor(out=ot.t[:, k2, :], in0=pb[b][:], in1=rt.t[:, k, :], op=ALU.add), reads=[pbt[b], kt], writes=[k2t])
                P.dma("sp", Rout[gi * 128:(gi + 1) * 128, t0:t0 + 512], ot.t[:, k2, :], reads=[k2t], writes=[dt[routn]])
        proj_fm(C, st, src, dt[srcn], W, groups, epi, KC)


def st_ffnup(C, hn, w_gu, aT, fcw_d, fcb_d, norm=None, outproj=None):
    P = C.P
    dt = C.dt
    NJ = DFF // 128
    groups = [(j * 128, DFF + j * 128) for j in range(NJ)]
    with Stage(C) as st:
        pre = None
        if norm is not None:
            pre = alloc_fused(st)
            pre["hooks"] = norm_into(C, st, norm[0], norm[1], norm[2], pre["xin"], pre["xint"], pre["pb"], pre["pbt"], (6, 7), outproj=outproj)
        ident = st.sb([128, 128], F32); identt = Tk()
        make_ident(P, ident, identt)
        fcw = st.sb([128, NJ, 3], F32); fcb = st.sb([128, NJ], F32); ct = Tk()
        P.dma("sp", fcw[:], fcw_d, writes=[ct]); P.dma("sp", fcb[:], fcb_d, writes=[ct])
        Gsb = st.sb([128, 2, 2 + S], BF16); Gt = tks(2, 4); Gpt = Tk()
        P.op("pool", lambda e: e.memset(Gsb[:, :, 0:2], 0.0), writes=[Gpt])
        cv = Rot(st, 3, [512], F32)
        sg = Rot(st, 3, [512], F32)
        ob = Rot(st, 3, [512], BF16)
        state = {"n": 0}

        def epi(s, half, gi, grp, banks, pb, pbt, alloc):
            if half == 0:
                state["n"] += 1
            gp = state["n"] % 2
            for tb in range(2):
                q = half * 2 + tb
                b = banks[0][tb]
                P.op("act", lambda e, b=b, gp=gp, q=q: e.activation(out=Gsb[:, gp, 2 + q * 512: 2 + (q + 1) * 512], in_=pb[b][:], func=AF.Copy), reads=[pbt[b]], writes=[Gt[gp][q]])
            tl = []
            for tb in range(2):
                q = half * 2 + tb
                rd = [Gt[gp][q], Gpt, ct] + ([Gt[gp][q - 1]] if q > 0 else [])
                kc_, kct = cv.next()
                k1, k1t = sg.next()
                k2, k2t = ob.next()
                tl.append((tb, q, rd, kc_, kct, k1, k1t, k2, k2t))
            for (tb, q, rd, kc_, kct, k1, k1t, k2, k2t) in tl:
                P.op("act", lambda e, kc_=kc_, gp=gp, q=q, gi=gi: e.activation(out=cv.t[:, kc_, :], in_=Gsb[:, gp, q * 512: q * 512 + 512], func=AF.Identity,
                                                                               scale=fcw[:, gi, 0:1], bias=fcb[:, gi:gi + 1]), reads=rd, writes=[kct])
            for (tb, q, rd, kc_, kct, k1, k1t, k2, k2t) in tl:
                for k in (1, 2):
                    P.op("dve", lambda e, kc_=kc_, gp=gp, q=q, gi=gi, k=k: e.scalar_tensor_tensor(out=cv.t[:, kc_, :], in0=Gsb[:, gp, q * 512 + k: q * 512 + k + 512], scalar=fcw[:, gi, k:k + 1],
                                                                                                  in1=cv.t[:, kc_, :], op0=ALU.mult, op1=ALU.add), reads=rd + [kct], writes=[kct])
            for (tb, q, rd, kc_, kct, k1, k1t, k2, k2t) in tl:
                P.op("act", lambda e, kc_=kc_, k1=k1: e.activation(out=sg.t[:, k1, :], in_=cv.t[:, kc_, :], func=AF.Silu), reads=[kct], writes=[k1t])
            for (tb, q, rd, kc_, kct, k1, k1t, k2, k2t) in tl:
                bu = banks[1][tb]
                t0 = s * S + q * 512
                P.op("dve", lambda e, bu=bu, k1=k1, k2=k2: e.tensor_tensor(out=ob.t[:, k2, :], in0=pb[bu][:], in1=sg.t[:, k1, :], op=ALU.mult), reads=[pbt[bu], k1t], writes=[k2t])
                P.dma("sp", aT[gi * 128:(gi + 1) * 128, t0:t0 + 512], ob.t[:, k2, :], reads=[k2t], writes=[dt["aT"]])
        proj_fm(C, st, hn, dt["hn"], w_gu, groups, epi, 8, pre=pre)


def st_qk(C, hn, w_qkv, qT1, kT1, cos_d, sin_d, norm=None, v1=None):
    P = C.P
    dt = C.dt
    blocks = [(g, j, h) for g in range(3) for j in range(2) for h in range(4)]
    groups = [(g * 1536 + j * 512 + h * 128,) for (g, j, h) in blocks]
    with Stage(C) as st:
        pre = None
        if norm is not None:
            pre = alloc_fused(st)
            if v1 is None:
                pre["hooks"] = norm_into(C, st, norm[0], norm[1], norm[2], pre["xin"], pre["xint"], pre["pb"], pre["pbt"], (6, 7), hn=hn, hnt=dt["hn"])
            else:
                pre["hooks"] = norm_into(C, st, norm[0], norm[1], norm[2], pre["xin"], pre["xint"], pre["pb"], pre["pbt"], (6, 7))
        if v1 is not None:
            Wv_ = w_qkv.rearrange("(kc p) n -> p kc n", p=128)
            wv = st.sb([128, 8, 1536], BF16); wvt = tks(8)
            for kc in range(8):
                for g in range(3):
                    P.dma("pool", wv[:, kc, g * 512:(g + 1) * 512], Wv_[:, kc, g * 1536 + 1024: g * 1536 + 1536], writes=[wvt[kc]])
            vst = Rot(st, 2, [1536], BF16)
        cos2 = st.sb([128, S], F32); sin2 = st.sb([128, S], F32); ct = Tk()
        P.dma("sp", cos2[:], cos_d, writes=[ct]); P.dma("sp", sin2[:], sin_d, writes=[ct])
        pif = st.sb([128, 128], F32); pib = st.sb([128, 128], BF16); pit = Tk()
        P.op("pool", lambda e: e.memset(pif[:], 0.0), writes=[pit])
        P.op("pool", lambda e: e.affine_select(out=pif[:], in_=pif[:], pattern=[[-1, 128]], compare_op=ALU.not_equal, fill=-1.0, base=-64, channel_multiplier=1), reads=[pit], writes=[pit])
        P.op("pool", lambda e: e.affine_select(out=pif[:], in_=pif[:], pattern=[[-1, 128]], compare_op=ALU.not_equal, fill=1.0, base=64, channel_multiplier=1), reads=[pit], writes=[pit])
        P.op("pool", lambda e: e.tensor_copy(out=pib[:], in_=pif[:]), reads=[pit], writes=[pit])
        qb = Rot(st, 4, [512], BF16)
        ta = Rot(st, 2, [512], F32)
        tb_ = Rot(st, 2, [512], F32)
        ob = Rot(st, 3, [512], BF16)

        def epi(s, half, gi, grp, banks, pb, pbt, alloc):
            g, j, h = blocks[gi]
            dst, dn = (qT1, "qT1") if j == 0 else (kT1, "kT1")
            r0 = (g * 4 + h) * 128
            items = []
            for tb in range(2):
                p0 = half * 1024 + tb * 512
                b0 = banks[0][tb]
                k0, k0t = qb.next()
                P.op("act", lambda e, b0=b0, k0=k0: e.activation(out=qb.t[:, k0, :], in_=pb[b0][:], func=AF.Copy), reads=[pbt[b0]], writes=[k0t])
                items.append((p0, b0, k0, k0t))

            def finish(items=items, dst=dst, dn=dn, r0=r0, s=s, pb=pb, pbt=pbt, alloc=alloc):
                for (p0, b0, k0, k0t) in items:
                    b1 = alloc()
                    P.op("pe", lambda e, b1=b1, k0=k0: e.matmul(pb[b1][:], lhsT=pib[:], rhs=qb.t[:, k0, :], start=True, stop=True), reads=[pit, k0t], writes=[pbt[b1]])
                    k1, k1t = ta.next()
                    k2, k2t = tb_.next()
                    k3, k3t = ob.next()
                    P.op("dve", lambda e, b0=b0, k1=k1, p0=p0: e.tensor_tensor(out=ta.t[:, k1, :], in0=pb[b0][:], in1=cos2[:, p0:p0 + 512], op=ALU.mult), reads=[pbt[b0], ct, k0t], writes=[k1t])
                    P.op("dve", lambda e, b1=b1, k2=k2, p0=p0: e.tensor_tensor(out=tb_.t[:, k2, :], in0=pb[b1][:], in1=sin2[:, p0:p0 + 512], op=ALU.mult), reads=[pbt[b1], ct], writes=[k2t])
                    P.op("dve", lambda e, k1=k1, k2=k2, k3=k3: e.tensor_tensor(out=ob.t[:, k3, :], in0=ta.t[:, k1, :], in1=tb_.t[:, k2, :], op=ALU.add), reads=[k1t, k2t], writes=[k3t])
                    P.dma("sp", dst[r0:r0 + 128, s * S + p0: s * S + p0 + 512], ob.t[:, k3, :], reads=[k3t], writes=[dt[dn]])
            pre.setdefault("mid", []).append(finish)
        proj_fm(C, st, hn, dt["hn"], w_qkv, groups, epi, 8, nslot=1, pre=pre)
        while pre.get("mid"):
            pre["mid"].pop(0)()
        if v1 is not None:
            xin, xint, pb, pbt = pre["xin"], pre["xint"], pre["pb"], pre["pbt"]
            bi = 0
            for s in range(2):
                for tt in range(16):
                    banks = [(bi + g) % 8 for g in range(3)]
                    bi += 3
                    for kc in range(8):
                        for g in range(3):
                            b = banks[g]
                            P.op("pe", lambda e, b=b, kc=kc, tt=tt, g=g, s=s: e.matmul(pb[b][:], lhsT=xin[:, s, kc, tt * 128:(tt + 1) * 128], rhs=wv[:, kc, g * 512:(g + 1) * 512],
                                                                                       start=(kc == 0), stop=(kc == 7)),
                                 reads=[xint[s][kc], wvt[kc]], writes=[pbt[b]])
                    r0 = s * S + tt * 128
                    k, kt = vst.next()
                    for g in range(3):
                        b = banks[g]
                        if g == 1:
                            P.op("dve", lambda e, b=b, k=k, g=g: e.tensor_copy(out=vst.t[:, k, g * 512:(g + 1) * 512], in_=pb[b][:]), reads=[pbt[b]], writes=[kt])
                        else:
                            P.op("act", lambda e, b=b, k=k, g=g: e.activation(out=vst.t[:, k, g * 512:(g + 1) * 512], in_=pb[b][:], func=AF.Copy), reads=[pbt[b]], writes=[kt])
                    P.dma("sp", v1[r0:r0 + 128, :], vst.t[:, k, :], reads=[kt], writes=[dt["v1"]])


def st_v1(C, hn, w_qkv, v1):
    P = C.P

    def mk(st):
        vst = Rot(st, 3, [1536], BF16)

        def env(s, tt, si, cgs, banks, pb, pbt):
            r0 = s * S + tt * 128
            k, kt = vst.next()
            for g in range(3):
                b = banks[g]
                if g == 1:
                    P.op("dve", lambda e, b=b, k=k, g=g: e.tensor_copy(out=vst.t[:, k, g * 512:(g + 1) * 512], in_=pb[b][:]), reads=[pbt[b]], writes=[kt])
                else:
                    P.op("act", lambda e, b=b, k=k, g=g: e.activation(out=vst.t[:, k, g * 512:(g + 1) * 512], in_=pb[b][:], func=AF.Copy), reads=[pbt[b]], writes=[kt])
            P.dma("sp", v1[r0:r0 + 128, :], vst.t[:, k, :], reads=[kt], writes=[C.dt["v1"]])
        return env
    st_proj_tok(C, hn, C.dt["hn"], w_qkv, 0, 0, [[(0, 512), (512, 512), (1024, 512)]], mk,
                wranges=[(g * 1536 + 1024, 512) for g in range(3)])


PATTERNS = ((128, 1), (512, 4), (2048, 16))


def st_attn(C, qT1, kT1, v1, numg):
    P = C.P
    dt = C.dt
    with Stage(C) as st:
        mask2 = st.sb([128, 256], BF16); mt = Tk()
        P.op("pool", lambda e: e.memset(mask2[:], 1.0), writes=[mt])
        P.op("pool", lambda e: e.affine_select(out=mask2[:, 0:128], in_=mask2[:, 0:128], pattern=[[1, 128]], compare_op=ALU.is_ge, fill=0.0, base=0, channel_multiplier=-1), reads=[mt], writes=[mt])
        P.op("pool", lambda e: e.affine_select(out=mask2[:, 128:256], in_=mask2[:, 128:256], pattern=[[-1, 128]], compare_op=ALU.is_ge, fill=0.0, base=0, channel_multiplier=1), reads=[mt], writes=[mt])
        qTg2 = st.sb([128, 2, 4, S], BF16); qt2 = tks(2, 4)
        kTg2 = st.sb([128, 2, 4, S], BF16); kt2 = tks(2, 4)
        vaug2 = st.sb([128, 2, 16, 4, 129], BF16); vt2 = tks(2, 16); vot = Tk()
        for bb_ in range(2):
            P.op("pool", lambda e, bb_=bb_: e.memset(vaug2[:, bb_, :, :, 128:129], 1.0), writes=[vot])
        Eb = st.sb([128, 4, 3, 256], BF16); Ebt = tks(4, 3)
        E3 = [0]
        si_ = [0]
        osb = Rot(st, 3, [516], F32)
        pS = [st.ps([128, 512])[:, 0:256] for _ in range(4)]; pSt = tks(4)
        pO = [[st.ps([128, 512])[:, 0:258] for _ in range(2)] for _ in range(2)]; pOt = tks(2, 4)
        si = 0
        sg_list = [(s, g) for s in range(2) for g in range(3)]

        def load_sg(idx):
            s, g = sg_list[idx]
            dil = PATTERNS[g][1]
            nb = 16 // dil
            bb = idx % 2
            qv = qT1[g * 512:(g + 1) * 512, s * S:(s + 1) * S].rearrange("(h p) t -> p h t", p=128)
            kv = kT1[g * 512:(g + 1) * 512, s * S:(s + 1) * S].rearrange("(h p) t -> p h t", p=128)
            P.dma("sp", qTg2[:, bb, :, :], qv, reads=[dt["qT1"]], writes=qt2[bb])
            P.dma("sp", kTg2[:, bb, :, :], kv, reads=[dt["kT1"]], writes=kt2[bb])
            vv = v1[s * S:(s + 1) * S, g * 512:(g + 1) * 512].rearrange("(b i r) (h d) -> r b i h d", i=128, r=dil, h=4)
            for r in range(dil):
                for b in range(nb):
                    P.dma("sp", vaug2[:, bb, r * nb + b, :, 0:128], vv[r, b], reads=[dt["v1"]], writes=[vt2[bb][r * nb + b]])
        load_sg(0)
        for idx, (s, g) in enumerate(sg_list):
            if True:
                window, dil = PATTERNS[g]
                nb = 16 // dil
                if idx + 1 < len(sg_list):
                    load_sg(idx + 1)
                bb = idx % 2
                qTg = qTg2[:, bb]; kTg = kTg2[:, bb]; vaug = vaug2[:, bb]
                qt = qt2[bb]; kt_ = kt2[bb]; vt = vt2[bb]
                nv = numg[g, s * S:(s + 1) * S, :].rearrange("(b i r) f -> r b i f", i=128, r=dil)
                blist = [(r, b) for r in range(dil) for b in range(nb)]

                def phase1(fi, kTg=kTg, qTg=qTg, qt=qt, kt_=kt_, dil=dil, nb=nb, blist=blist):
                    r, b = blist[fi]
                    nq = 256 if b < nb - 1 else 128
                    k0 = r + dil * b * 128
                    e3 = E3[0] % 3
                    E3[0] += 1
                    for h in range(4):
                        sp_ = si_[0] % 4
                        si_[0] += 1
                        ks = slice(k0, k0 + dil * 127 + 1, dil)
                        qs = slice(k0, k0 + dil * (nq - 1) + 1, dil)
                        P.op("pe", lambda e, sp_=sp_, h=h, ks=ks, qs=qs, nq=nq: e.matmul(pS[sp_][:, 0:nq], lhsT=kTg[:, h, ks], rhs=qTg[:, h, qs], start=True, stop=True),
                             reads=[kt_[h], qt[h]], writes=[pSt[sp_]])
                        kt = Ebt[h][e3]
                        P.op("act", lambda e, sp_=sp_, h=h, e3=e3, nq=nq: e.activation(out=Eb[:, h, e3, 0:nq], in_=pS[sp_][:, 0:nq], func=AF.Exp, scale=float(128 ** -0.5)), reads=[pSt[sp_]], writes=[kt])
                        P.op("dve", lambda e, h=h, e3=e3, nq=nq: e.tensor_tensor(out=Eb[:, h, e3, 0:nq], in0=Eb[:, h, e3, 0:nq], in1=mask2[:, 0:nq], op=ALU.mult), reads=[kt, mt], writes=[kt])
                    return e3

                def phase2(fi, e3, e3prev, vaug=vaug, vt=vt, nv=nv, nb=nb, blist=blist):
                    r, b = blist[fi]
                    rb = r * nb + b
                    for h in range(4):
                        hp, hh = h // 2, h % 2
                        if b > 0:
                            P.op("pe", lambda e, b=b, hp=hp, hh=hh, rb=rb, h=h: e.matmul(pO[b % 2][hp][:, hh * 129:(hh + 1) * 129], lhsT=Eb[:, h, e3prev, 128:256], rhs=vaug[:, rb - 1, h, :], start=True, stop=False),
                                 reads=[Ebt[h][e3prev], vt[rb - 1], vot], writes=[pOt[b % 2][h]])
                        P.op("pe", lambda e, b=b, hp=hp, hh=hh, rb=rb, h=h: e.matmul(pO[b % 2][hp][:, hh * 129:(hh + 1) * 129], lhsT=Eb[:, h, e3, 0:128], rhs=vaug[:, rb, h, :], start=(b == 0), stop=True),
                             reads=[Ebt[h][e3], vt[rb], vot], writes=[pOt[b % 2][h]])
                    k, kt = osb.next()
                    for hp in range(2):
                        if hp == 0:
                            P.op("act", lambda e, k=k, b=b, hp=hp: e.activation(out=osb.t[:, k, hp * 258:(hp + 1) * 258], in_=pO[b % 2][hp][:], func=AF.Copy),
                                 reads=[pOt[b % 2][2 * hp], pOt[b % 2][2 * hp + 1]], writes=[kt])
                        else:
                            P.op("dve", lambda e, k=k, b=b, hp=hp: e.tensor_copy(out=osb.t[:, k, hp * 258:(hp + 1) * 258], in_=pO[b % 2][hp][:]),
                                 reads=[pOt[b % 2][2 * hp], pOt[b % 2][2 * hp + 1]], writes=[kt])
                    P.dma("sp", nv[r, b], osb.t[:, k, :], reads=[kt], writes=[dt["numg"]])

                es = {0: phase1(0)}
                for fi in range(len(blist)):
                    if fi + 1 < len(blist):
                        es[fi + 1] = phase1(fi + 1)
                    phase2(fi, es[fi], es.get(fi - 1, 0))


def st_merge(C, numg, oT):
    P = C.P
    dt = C.dt
    with Stage(C) as st:
        identb = st.sb([128, 128], BF16); identbt = Tk()
        identf = st.sb([128, 128], F32); identft = Tk()
        make_ident(P, identf, identft)
        P.op("pool", lambda e: e.tensor_copy(out=identb[:], in_=identf[:]), reads=[identft], writes=[identbt])
        n3 = Rot(st, 4, [3, 516], F32)
        acc = Rot(st, 3, [516], F32)
        rr = Rot(st, 3, [4], F32)
        otok = Rot(st, 3, [512], BF16)
        oTs = st.sb([128, 4, S], BF16); oTt = tks(4, 16)
        pX = [st.ps([128, 1024], BF16)[:, 0:512] for _ in range(2)]; pXt = tks(2)
        for s in range(2):
            for tt in range(16):
                r0 = s * S + tt * 128
                k, kt = n3.next()
                P.dma("sp", n3.t[:, k, :, :], numg[:, r0:r0 + 128, :].rearrange("g t f -> t g f"), reads=[dt["numg"]], writes=[kt])
                k2, k2t = acc.next()
                P.op("dve", lambda e, k=k, k2=k2: e.tensor_tensor(out=acc.t[:, k2, :], in0=n3.t[:, k, 0, :], in1=n3.t[:, k, 1, :], op=ALU.add), reads=[kt], writes=[k2t])
                P.op("dve", lambda e, k=k, k2=k2: e.tensor_tensor(out=acc.t[:, k2, :], in0=acc.t[:, k2, :], in1=n3.t[:, k, 2, :], op=ALU.add), reads=[kt, k2t], writes=[k2t])
                k3, k3t = rr.next()
                a4 = acc.t[:, k2, :].rearrange("p (h f) -> p h f", f=129)
                P.op("dve", lambda e, a4=a4, k3=k3: e.reciprocal(out=rr.t[:, k3, :], in_=a4[:, :, 128]), reads=[k2t], writes=[k3t])
                k4, k4t = otok.next()
                for h in range(4):
                    P.op("dve", lambda e, a4=a4, k3=k3, k4=k4, h=h: e.tensor_scalar(out=otok.t[:, k4, h * 128:(h + 1) * 128], in0=a4[:, h, 0:128], scalar1=rr.t[:, k3, h:h + 1], scalar2=None, op0=ALU.mult),
                         reads=[k2t, k3t], writes=[k4t])
                px = tt % 2
                for h in range(4):
                    P.op("pe", lambda e, k4=k4, h=h, px=px: e.transpose(pX[px][:, h * 128:(h + 1) * 128], otok.t[:, k4, h * 128:(h + 1) * 128], identb[:]), reads=[k4t, identbt], writes=[pXt[px]])
                P.op("act", lambda e, px=px, tt=tt: e.activation(out=oTs[:, :, tt * 128:(tt + 1) * 128], in_=pX[px][:].rearrange("p (h t) -> p h t", h=4), func=AF.Copy),
                     reads=[pXt[px]], writes=[oTt[h_][tt] for h_ in range(4)])
            for h in range(4):
                P.dma("sp", oT[h * 128:(h + 1) * 128, s * S:(s + 1) * S], oTs[:, h, :], reads=oTt[h], writes=[dt["oT"]])


INPUT_NAMES = {"xT", "w_in", "w_out0", "w_gu0", "w_gu1", "w_down0", "w_down1", "w_qkv", "w_out1", "mixn", "ffnn", "finn",
               "fb", "ib", "hnb", "cw", "cb", "lg", "lb", "fcw", "fcb", "cos2", "sin2"}

ALL_STAGES = ["n_inproj_fm", "inproj_tok", "mlstm", "conf", "out0", "n_ffnup0", "ffndown0",
              "n_qkv", "attn", "merge", "on_ffnup1", "ffndown1", "normf"]


def run_stages(C, stages):
    if stages is None:
        stages = ALL_STAGES
    d = C.dram
    xT = d("xT", [D, T], F32)
    w_in = d("w_in", [D, 6152], F32)
    w_out0 = d("w_out0", [2048, D], F32)
    w_gu = [d("w_gu0", [D, 2 * DFF], F32), d("w_gu1", [D, 2 * DFF], F32)]
    w_down = [d("w_down0", [DFF, D], F32), d("w_down1", [DFF, D], F32)]
    w_qkv = d("w_qkv", [D, 4608], F32)
    w_out1 = d("w_out1", [512, D], F32)
    mixn = d("mixn", [2, 128, 8], F32)
    ffnn = d("ffnn", [2, 128, 8], F32)
    finn = d("finn", [128, 8], F32)
    fb = d("fb", [128, 64], F32)
    ib = d("ib", [128, 64], F32)
    hnb = d("hnb", [128, 1024], F32)
    cw = d("cw", [128, 8, 31], F32)
    cb = d("cb", [128, 8], F32)
    lg = d("lg", [128, 8], F32)
    lb = d("lb", [128, 8], F32)
    fcw = d("fcw", [2, 128, 22, 3], F32)
    fcb = d("fcb", [2, 128, 22], F32)
    cos2 = d("cos2", [128, S], F32)
    sin2 = d("sin2", [128, S], F32)
    hn = d("hn", [D, T], BF16)
    qT0 = d("qT0", [D, T], BF16)
    kT0 = d("kT0", [D, T], BF16)
    uT = d("uT", [D, T], BF16)
    ktok = d("ktok", [T, D], BF16)
    vtok = d("vtok", [T, D], F32)
    so = d("so", [T, D], F32)
    gates = d("gates", [T, 8], F32)
    catT = d("catT", [2048, T], BF16)
    r1T = d("r1T", [D, T], F32)
    aT = d("aT", [DFF, T], BF16)
    r2T = d("r2T", [D, T], F32)
    qT1 = d("qT1", [1536, T], BF16)
    kT1 = d("kT1", [1536, T], BF16)
    v1 = d("v1", [T, 1536], BF16)
    numg = d("numg", [3, T, 516], F32)
    oT = d("oT", [512, T], BF16)
    r3T = d("r3T", [D, T], F32)
    r4T = d("r4T", [D, T], F32)
    outT = d("outT", [D, T], F32)
    dt = C.dt
    for sname in stages:
        if sname == "n_inproj_fm":
            st_inproj_fm(C, hn, w_in, qT0, kT0, uT, norm=(xT, dt["xT"], mixn[0]))
        elif sname == "n_ffnup0":
            st_ffnup(C, hn, w_gu[0], aT, fcw[0], fcb[0], norm=(r1T, dt["r1T"], ffnn[0]))
        elif sname == "n_qk":
            st_qk(C, hn, w_qkv, qT1, kT1, cos2, sin2, norm=(r2T, dt["r2T"], mixn[1]))
        elif sname == "n_qkv":
            st_qk(C, hn, w_qkv, qT1, kT1, cos2, sin2, norm=(r2T, dt["r2T"], mixn[1]), v1=v1)
        elif sname == "n_ffnup1":
            st_ffnup(C, hn, w_gu[1], aT, fcw[1], fcb[1], norm=(r3T, dt["r3T"], ffnn[1]))
        elif sname == "on_ffnup1":
            st_ffnup(C, hn, w_gu[1], aT, fcw[1], fcb[1], norm=(r2T, dt["r2T"], ffnn[1]), outproj=(oT, dt["oT"], w_out1, r3T, dt["r3T"]))
        elif sname == "norm0":
            st_norm(C, xT, dt["xT"], mixn[0], hn, dt["hn"])
        elif sname == "inproj_fm":
            st_inproj_fm(C, hn, w_in, qT0, kT0, uT)
        elif sname == "inproj_tok":
            st_inproj_tok(C, hn, w_in, ktok, vtok, so, gates, hnb)
        elif sname == "mlstm":
            st_mlstm(C, qT0, kT0, ktok, vtok, so, gates, catT, fb, ib, hnb)
        elif sname == "conf":
            st_conf(C, uT, catT, cw, cb, lg, lb)
        elif sname == "out0":
            st_projres(C, catT, "catT", w_out0, 16, xT, "xT", r1T, "r1T")
        elif sname == "norm1":
            st_norm(C, r1T, dt["r1T"], ffnn[0], hn, dt["hn"])
        elif sname == "ffnup0":
            st_ffnup(C, hn, w_gu[0], aT, fcw[0], fcb[0])
        elif sname == "ffndown0":
            st_projres(C, aT, "aT", w_down[0], 22, r1T, "r1T", r2T, "r2T")
        elif sname == "norm2":
            st_norm(C, r2T, dt["r2T"], mixn[1], hn, dt["hn"])
        elif sname == "qk":
            st_qk(C, hn, w_qkv, qT1, kT1, cos2, sin2)
        elif sname == "v1":
            st_v1(C, hn, w_qkv, v1)
        elif sname == "attn":
            st_attn(C, qT1, kT1, v1, numg)
        elif sname == "merge":
            st_merge(C, numg, oT)
        elif sname == "out1":
            st_projres(C, oT, "oT", w_out1, 4, r2T, "r2T", r3T, "r3T")
        elif sname == "norm3":
            st_norm(C, r3T, dt["r3T"], ffnn[1], hn, dt["hn"])
        elif sname == "ffnup1":
            st_ffnup(C, hn, w_gu[1], aT, fcw[1], fcb[1])
        elif sname == "ffndown1":
            st_projres(C, aT, "aT", w_down[1], 22, r3T, "r3T", r4T, "r4T")
        elif sname == "normf":
            st_norm(C, r4T, dt["r4T"], finn, outT, dt["outT"], final=True)
        else:
            raise ValueError(sname)


def build_program(stages=None, ext_in=(), ext_out=("outT",)):
    nc = bass.Bass("TRN2", target_bir_lowering=False)
    top = contextlib.ExitStack()
    with top:
        P = Prog(nc, top)
        C = Ctx(nc, P, ext_in=set(ext_in) | INPUT_NAMES, ext_out=ext_out)
        run_stages(C, stages)
    return nc


def host_params(inputs):
    f = lambda a: np.ascontiguousarray(np.asarray(a, dtype=np.float32))
    p = {}
    p["w_in"] = f(inputs["ab_w_in"][0])
    p["w_out0"] = f(inputs["ab_w_out"][0])
    p["w_gu0"] = f(inputs["ffn_w_gu"][0]); p["w_gu1"] = f(inputs["ffn_w_gu"][1])
    p["w_down0"] = f(inputs["ffn_w_down"][0]); p["w_down1"] = f(inputs["ffn_w_down"][1])
    p["w_qkv"] = f(inputs["c_w_qkv"][0])
    p["w_out1"] = f(inputs["c_w_out"][0])
    pk = lambda v: f(np.asarray(v).reshape(-1, 128).T)
    p["mixn"] = f(np.stack([pk(inputs["mix_norm"][l]) for l in range(2)]))
    p["ffnn"] = f(np.stack([pk(inputs["ffn_norm"][l]) for l in range(2)]))
    p["finn"] = pk(inputs["final_norm"])
    p["fb"] = f(np.broadcast_to(np.tile(np.asarray(inputs["ab_f_bias"][0]), 16)[None, :], (128, 64)))
    p["ib"] = f(np.broadcast_to(np.tile(np.asarray(inputs["ab_i_bias"][0]), 16)[None, :], (128, 64)))
    p["hnb"] = f(np.broadcast_to(np.asarray(inputs["ab_head_norm"][0])[None, :], (128, 1024)))
    p["cw"] = f(np.asarray(inputs["ab_conv_w"][0]).T.reshape(8, 128, 31).transpose(1, 0, 2))
    p["cb"] = pk(inputs["ab_conv_b"][0]); p["lg"] = pk(inputs["ab_ln_g"][0]); p["lb"] = pk(inputs["ab_ln_b"][0])
    p["fcw"] = f(np.stack([np.asarray(inputs["ffn_conv_w"][l]).T.reshape(22, 128, 3).transpose(1, 0, 2) for l in range(2)]))
    p["fcb"] = f(np.stack([pk(inputs["ffn_conv_b"][l]) for l in range(2)]))
    pos = np.arange(S, dtype=np.float32)
    inv = (10000.0 ** (-np.arange(0, 128, 2, dtype=np.float32) / 128)).astype(np.float32)
    ang = (pos[None, :] * inv[:, None]).astype(np.float32)
    p["cos2"] = f(np.concatenate([np.cos(ang), np.cos(ang)], 0))
    p["sin2"] = f(np.concatenate([np.sin(ang), np.sin(ang)], 0))
    return p


_CACHE = {}


def kernel(**inputs):
    x = np.asarray(inputs["x"], dtype=np.float32)
    p = host_params(inputs)
    if "nc" not in _CACHE:
        _CACHE["nc"] = build_program()
    nc = _CACHE["nc"]
    in_maps = []
    for c in range(NCORES):
        m = dict(p)
        m["xT"] = np.ascontiguousarray(x[2 * c:2 * c + 2].reshape(T, D).T)
        in_maps.append(m)
    res = run_bass_kernel_spmd(nc, in_maps, core_ids=list(range(NCORES)))
    out = np.empty((16, S, D), dtype=np.float32)
    for c in range(NCORES):
        out[2 * c:2 * c + 2] = np.asarray(res.results[c]["outT"]).T.reshape(2, S, D)
    return out
```

```python
import contextlib
import numpy as np
import concourse.bass as bass
import concourse.mybir as mybir
from concourse.bass_utils import run_bass_kernel_spmd

F32 = mybir.dt.float32
BF16 = mybir.dt.bfloat16
ALU = mybir.AluOpType
AF = mybir.ActivationFunctionType
AX = mybir.AxisListType

NCORES = 8
T = 4096
S = 2048
D = 1024
DFF = 2816
N_DMA_SEMS = 24
EPS = 1e-6


class Tk:
    __slots__ = ("w", "r")

    def __init__(self):
        self.w = None
        self.r = {}


def tks(*shape):
    if len(shape) == 1:
        return [Tk() for _ in range(shape[0])]
    return [tks(*shape[1:]) for _ in range(shape[0])]


class Prog:
    ENGS = ("pe", "act", "dve", "pool", "sp")

    def __init__(self, nc, stack):
        self.nc = nc
        self.ops = []
        self.base = 0
        self.seq = {e: 0 for e in self.ENGS}
        self.dma_i = 0
        self.slot_last = {}
        self.known = {e: {} for e in self.ENGS}
        self.sems = {}
        self.stack = stack
        self.stage_i = 0
        for k in range(N_DMA_SEMS):
            self.sems[("dma", k)] = stack.enter_context(nc.semaphore("s_dma%d" % k))
        self.n_instr = 0

    def op(self, eng, fn, reads=(), writes=(), dma=False):
        deps = set()
        for t in reads:
            if t.w is not None:
                deps.add(t.w)
        for t in writes:
            if t.w is not None:
                deps.add(t.w)
            deps.update(t.r.values())
        gid = self.base + len(self.ops)
        self.ops.append([eng, fn, deps, dma, False])
        key = ("dma", gid) if dma else eng
        for t in reads:
            t.r[key] = gid
        for t in writes:
            t.w = gid
            t.r = {}
        return gid

    def dma(self, q, out, in_, reads=(), writes=()):
        return self.op(q, lambda e: e.dma_start(out=out, in_=in_), reads, writes, dma=True)

    def flush(self):
        nc = self.nc
        ops = self.ops
        base = self.base
        n = len(ops)
        if n == 0:
            return
        self.stage_i += 1
        for e in self.ENGS:
            if e != "pe" and e in self.sems:
                continue
            self.sems[e] = self.stack.enter_context(nc.semaphore("s_%s_%d" % (e, self.stage_i)))
            self.seq[e] = 0
            for e2 in self.ENGS:
                self.known[e2].pop(e, None)
        self.stage_counts = getattr(self, "stage_counts", [])
        for o in ops:
            for d in o[2]:
                if d >= base:
                    ops[d - base][4] = True
        last = {}
        for i, o in enumerate(ops):
            if not o[3]:
                last[o[0]] = i
        for e, i in last.items():
            ops[i][4] = True
        sig = {}
        dma_prev = {}
        for i, o in enumerate(ops):
            eng, fn, deps, is_dma, signals = o
            if is_dma:
                slot = self.dma_i % N_DMA_SEMS
                val = 16 * (self.dma_i // N_DMA_SEMS + 1)
                sig[i] = (("dma", slot), val)
                if slot in self.slot_last:
                    dma_prev[i] = self.slot_last[slot]
                self.slot_last[slot] = (("dma", slot), val)
                self.dma_i += 1
            elif signals:
                self.seq[eng] += 1
                sig[i] = (eng, self.seq[eng])
        final = {}
        for e in self.ENGS:
            if self.seq[e] > 0:
                final[e] = self.seq[e]
        for slot, kv in self.slot_last.items():
            final[kv[0]] = kv[1]
        sems = self.sems

        def run_engine(ename, e):
            known = self.known[ename]
            for i, o in enumerate(ops):
                eng, fn, deps, is_dma, signals = o
                if eng != ename:
                    continue
                need = {}
                for d in deps:
                    if d < base:
                        continue
                    dd = ops[d - base]
                    if (not dd[3]) and dd[0] == ename and ename == "pe":
                        continue
                    k, v = sig[d - base]
                    if need.get(k, 0) < v:
                        need[k] = v
                if i in dma_prev:
                    k, v = dma_prev[i]
                    if need.get(k, 0) < v:
                        need[k] = v
                for k, v in need.items():
                    if known.get(k, 0) < v:
                        e.wait_ge(sems[k], v)
                        known[k] = v
                        self.n_instr += 1
                ins = fn(e)
                self.n_instr += 1
                if i in sig:
                    k, v = sig[i]
                    ins.then_inc(sems[k], 16 if is_dma else 1)
            for k, v in final.items():
                if known.get(k, 0) < v:
                    e.wait_ge(sems[k], v)
                    known[k] = v

        with nc.Block() as block:
            @block.tensor
            def _(e):
                run_engine("pe", e)

            @block.scalar
            def _(e):
                run_engine("act", e)

            @block.vector
            def _(e):
                run_engine("dve", e)

            @block.gpsimd
            def _(e):
                run_engine("pool", e)

            @block.sync
            def _(e):
                run_engine("sp", e)
        self.stage_counts.append(dict(self.seq))
        self.base += n
        self.ops = []


class Ctx:
    def __init__(self, nc, P, ext_in=(), ext_out=()):
        self.nc = nc
        self.P = P
        self.ext_in = set(ext_in)
        self.ext_out = set(ext_out)
        self.d = {}
        self.dt = {}

    def dram(self, name, shape, dtype):
        if name in self.d:
            return self.d[name]
        kind = "Internal"
        if name in self.ext_in:
            kind = "ExternalInput"
        elif name in self.ext_out:
            kind = "ExternalOutput"
        self.d[name] = self.nc.dram_tensor(name, list(shape), dtype, kind=kind).ap()
        self.dt[name] = Tk()
        return self.d[name]


_UID = [0]


class Stage:
    def __init__(self, C):
        self.C = C
        self.st = contextlib.ExitStack()
        self.n = 0

    def __enter__(self):
        self.st.__enter__()
        return self

    def __exit__(self, *a):
        if a[0] is None:
            self.C.P.flush()
        return self.st.__exit__(*a)

    def sb(self, shape, dtype, name=None):
        _UID[0] += 1
        return self.st.enter_context(self.C.nc.sbuf_tensor("sb%d" % _UID[0], list(shape), dtype))

    def ps(self, shape, dtype=F32):
        _UID[0] += 1
        return self.st.enter_context(self.C.nc.psum_tensor("ps%d" % _UID[0], list(shape), dtype))

    def psbanks(self, n=8):
        return [self.ps([128, 512]) for _ in range(n)], tks(n)


def make_ident(P, t, tk, dtype_is_bf=False):
    P.op("pool", lambda e: e.memset(t[:], 0.0), writes=[tk])
    P.op("pool", lambda e: e.affine_select(out=t[:], in_=t[:], pattern=[[-1, 128]], compare_op=ALU.not_equal,
                                           fill=1.0, base=0, channel_multiplier=1), reads=[tk], writes=[tk])


def st_norm(C, src, srct, g_dram, dst, dstt, final=False):
    P = C.P
    srcv = src.rearrange("(kc p) t -> p kc t", p=128)
    dstv = dst.rearrange("(kc p) t -> p kc t", p=128)
    odt = F32 if final else BF16
    NT = T // 512
    LA = 2
    with Stage(C) as st:
        NR = LA + 1
        R = st.sb([128, NR, 8, 512], F32)
        Rt = tks(NR, 8)
        g = st.sb([128, 8], F32)
        gt = Tk()
        ones = st.sb([128, 128], BF16)
        onest = Tk()
        sq = st.sb([128, 2, 8, 512], BF16)
        sqt = tks(2, 8)
        rs = st.sb([128, 2, 512], F32)
        rst = tks(2)
        ho = st.sb([128, 2, 8, 512], odt)
        hot = tks(2, 8)
        pb, pbt = st.psbanks(2)
        P.dma("sp", g[:], g_dram, writes=[gt])
        P.op("pool", lambda e: e.memset(ones[:], 1.0), writes=[onest])

        def load(i):
            rb = i % NR
            P.dma("sp", R[:, rb, :, :], srcv[:, :, i * 512:(i + 1) * 512], reads=[srct], writes=Rt[rb])
        for i in range(min(LA, NT)):
            load(i)
        for i in range(NT):
            if i + LA < NT:
                load(i + LA)
            par = i % 2
            rb = i % NR
            for kc in range(8):
                P.op("act", lambda e, kc=kc, par=par, rb=rb: e.activation(out=sq[:, par, kc, :], in_=R[:, rb, kc, :], func=AF.Square),
                     reads=[Rt[rb][kc]], writes=[sqt[par][kc]])
            for kc in range(8):
                P.op("pe", lambda e, kc=kc, par=par: e.matmul(pb[par][:], lhsT=ones[:], rhs=sq[:, par, kc, :], start=(kc == 0), stop=(kc == 7)),
                     reads=[onest, sqt[par][kc]], writes=[pbt[par]])
            P.op("dve", lambda e, par=par: e.tensor_scalar(out=rs[:, par, :], in0=pb[par][:], scalar1=1.0 / D, scalar2=EPS, op0=ALU.mult, op1=ALU.add),
                 reads=[pbt[par]], writes=[rst[par]])
            P.op("act", lambda e, par=par: e.activation(out=rs[:, par, :], in_=rs[:, par, :], func=AF.Sqrt), reads=[rst[par]], writes=[rst[par]])
            P.op("dve", lambda e, par=par: e.reciprocal(out=rs[:, par, :], in_=rs[:, par, :]), reads=[rst[par]], writes=[rst[par]])
            for kc in range(8):
                P.op("dve", lambda e, kc=kc, par=par, rb=rb: e.scalar_tensor_tensor(out=ho[:, par, kc, :], in0=R[:, rb, kc, :], scalar=g[:, kc:kc + 1], in1=rs[:, par, :],
                                                                                   op0=ALU.mult, op1=ALU.mult),
                     reads=[Rt[rb][kc], gt, rst[par]], writes=[hot[par][kc]])
            P.dma("sp", dstv[:, :, i * 512:(i + 1) * 512], ho[:, par, :, :], reads=hot[par], writes=[dstt])


def norm_into(C, st, src, srct, g_dram, xin, xint, pb, pbt, banks, hn=None, hnt=None, outproj=None):
    P = C.P
    srcv = src.rearrange("(kc p) t -> p kc t", p=128)
    NT = T // 512
    LA = 2
    NR = LA + 1
    R = st.sb([128, NR, 8, 512], F32)
    Rt = tks(NR, 8)
    g = st.sb([128, 8], F32)
    gt = Tk()
    ones = st.sb([128, 128], BF16)
    onest = Tk()
    sq = st.sb([128, 2, 8, 512], BF16)
    sqt = tks(2, 8)
    rs = st.sb([128, 2, 512], F32)
    rst = tks(2)
    P.dma("sp", g[:], g_dram, writes=[gt])
    P.op("pool", lambda e: e.memset(ones[:], 1.0), writes=[onest])
    if hn is not None:
        hnv = hn.rearrange("(kc p) t -> p kc t", p=128)

    if outproj is not None:
        oT_d, oTt_d, wo_d, rout, routt = outproj
        wo = st.sb([128, 4, D], BF16)
        wot = Tk()
        P.dma("pool", wo[:], wo_d.rearrange("(kc p) n -> p kc n", p=128), writes=[wot])
        oTs = st.sb([128, NR, 4, 512], BF16)
        oTst = tks(NR)
        oTv = oT_d.rearrange("(kc p) t -> p kc t", p=128)
        routv = rout.rearrange("(kc p) t -> p kc t", p=128)

    def load(i):
        rb = i % NR
        P.dma("sp", R[:, rb, :, :], srcv[:, :, i * 512:(i + 1) * 512], reads=[srct], writes=Rt[rb])
        if outproj is not None:
            P.dma("sp", oTs[:, rb, :, :], oTv[:, :, i * 512:(i + 1) * 512], reads=[oTt_d], writes=[oTst[rb]])

    def produce(i):
        rb = i % NR
        if outproj is not None:
            for gi in range(8):
                bk = 4 + gi % 2
                for kc in range(4):
                    P.op("pe", lambda e, bk=bk, kc=kc, gi=gi, rb=rb: e.matmul(pb[bk][:], lhsT=wo[:, kc, gi * 128:(gi + 1) * 128], rhs=oTs[:, rb, kc, :], start=(kc == 0), stop=(kc == 3)),
                         reads=[wot, oTst[rb]], writes=[pbt[bk]])
                P.op("dve", lambda e, bk=bk, gi=gi, rb=rb: e.tensor_tensor(out=R[:, rb, gi, :], in0=R[:, rb, gi, :], in1=pb[bk][:], op=ALU.add),
                     reads=[pbt[bk], Rt[rb][gi]], writes=[Rt[rb][gi]])
            P.dma("sp", routv[:, :, i * 512:(i + 1) * 512], R[:, rb, :, :], reads=Rt[rb], writes=[routt])
    for i in range(min(LA, NT)):
        load(i)

    def tile(i):
        if i + LA < NT:
            load(i + LA)
        produce(i)
        par = i % 2
        rb = i % NR
        s, tb = i // 4, i % 4
        bk = banks[par]
        for kc in range(8):
            P.op("act", lambda e, kc=kc, par=par, rb=rb: e.activation(out=sq[:, par, kc, :], in_=R[:, rb, kc, :], func=AF.Square),
                 reads=[Rt[rb][kc]], writes=[sqt[par][kc]])
        for kc in range(8):
            P.op("pe", lambda e, kc=kc, par=par, bk=bk: e.matmul(pb[bk][:], lhsT=ones[:], rhs=sq[:, par, kc, :], start=(kc == 0), stop=(kc == 7)),
                 reads=[onest, sqt[par][kc]], writes=[pbt[bk]])
        P.op("dve", lambda e, par=par, bk=bk: e.tensor_scalar(out=rs[:, par, :], in0=pb[bk][:], scalar1=1.0 / D, scalar2=EPS, op0=ALU.mult, op1=ALU.add),
             reads=[pbt[bk]], writes=[rst[par]])
        P.op("act", lambda e, par=par: e.activation(out=rs[:, par, :], in_=rs[:, par, :], func=AF.Sqrt), reads=[rst[par]], writes=[rst[par]])
        P.op("dve", lambda e, par=par: e.reciprocal(out=rs[:, par, :], in_=rs[:, par, :]), reads=[rst[par]], writes=[rst[par]])
        for kc in range(8):
            P.op("dve", lambda e, kc=kc, par=par, rb=rb, s=s, tb=tb: e.scalar_tensor_tensor(out=xin[:, s, kc, tb * 512:(tb + 1) * 512], in0=R[:, rb, kc, :], scalar=g[:, kc:kc + 1], in1=rs[:, par, :],
                                                                                     op0=ALU.mult, op1=ALU.mult),
                 reads=[Rt[rb][kc], gt, rst[par]], writes=[xint[s][kc]])
        if hn is not None:
            P.dma("sp", hnv[:, :, i * 512:(i + 1) * 512], xin[:, s, :, tb * 512:(tb + 1) * 512], reads=xint[s], writes=[hnt])
    for i in range(4):
        tile(i)
    return [lambda i=i: tile(i) for i in range(4, NT)]


def alloc_fused(st):
    xin = st.sb([128, 2, 8, S], BF16)
    xint = tks(2, 8)
    pb, pbt = st.psbanks(8)
    return dict(xin=xin, xint=xint, pb=pb, pbt=pbt)


def proj_fm(C, st, src, srct, W, groups, epi, KC, wload=None, nslot=2, pre=None):
    P = C.P
    srcv = src.rearrange("(kc p) t -> p kc t", p=128)
    Wv = W.rearrange("(kc p) n -> p kc n", p=128)
    big = KC > 8
    if big:
        NH = 3
        xin = st.sb([128, NH, KC, 1024], BF16)
        xint = tks(NH, KC)
    elif pre is not None:
        xin, xint = pre["xin"], pre["xint"]
    else:
        xin = st.sb([128, 2, KC, S], BF16)
        xint = tks(2, KC)
    NW = 3
    wt = st.sb([128, NW, nslot, KC, 128], BF16)
    wtt = tks(NW, nslot)
    if pre is not None:
        pb, pbt = pre["pb"], pre["pbt"]
    else:
        pb, pbt = st.psbanks(8)
    bank_i = [0]

    def alloc():
        b = bank_i[0] % 8
        bank_i[0] += 1
        return b
    order = [(s, gi) for s in range(2) for gi in range(len(groups))]

    def load_w(idx):
        s, gi = order[idx]
        grp = groups[gi]
        wb = idx % NW
        if wload is None:
            for j, col in enumerate(grp):
                P.dma("pool", wt[:, wb, j, :, :], Wv[:, :, col:col + 128], writes=[wtt[wb][j]])
        else:
            wload(gi, grp, wt, wtt, wb, Wv)

    def load_x(s):
        for kc in range(KC):
            P.dma("sp", xin[:, s, kc, :], srcv[:, kc, s * S:(s + 1) * S], reads=[srct], writes=[xint[s][kc]])

    def load_xh(hi):
        s, half = hi // 2, hi % 2
        hb = hi % NH
        t0 = s * S + half * 1024
        for kc in range(KC):
            P.dma("sp", xin[:, hb, kc, :], srcv[:, kc, t0:t0 + 1024], reads=[srct], writes=[xint[hb][kc]])
    if big:
        load_xh(0)
        load_w(0)
        load_xh(1)
        if len(order) > 1:
            load_w(1)
        load_xh(2)
    else:
        if pre is None:
            load_x(0)
        load_w(0)
        if len(order) > 1:
            load_w(1)
        if pre is None:
            load_x(1)
    for idx, (s, gi) in enumerate(order):
        grp = groups[gi]
        wb = idx % NW
        if idx + 2 < len(order):
            load_w(idx + 2)
        if pre is not None and pre.get("hooks"):
            pre["hooks"].pop(0)()
        for half in range(2):
            if big:
                xb = (2 * s + half) % NH
                xoff = 0
            else:
                xb = s
                xoff = half * 1024
            banks = []
            for j in range(nslot if wload is not None else len(grp)):
                b0 = alloc()
                b1 = alloc()
                for kc in range(KC):
                    for tb, b in enumerate((b0, b1)):
                        t0 = xoff + tb * 512
                        P.op("pe", lambda e, b=b, wb=wb, j=j, kc=kc, t0=t0, xb=xb: e.matmul(pb[b][:], lhsT=wt[:, wb, j, kc, :], rhs=xin[:, xb, kc, t0:t0 + 512],
                                                                                             start=(kc == 0), stop=(kc == KC - 1)),
                             reads=[wtt[wb][j], xint[xb][kc]], writes=[pbt[b]])
                    if kc == 1 and j == 0 and pre is not None:
                        while pre.get("mid"):
                            pre["mid"].pop(0)()
                banks.append((b0, b1))
            if big and s == 0 and gi == len(groups) - 1 and half == 0:
                load_xh(3)
            epi(s, half, gi, grp, banks, pb, pbt, alloc)


class Rot:
    def __init__(self, st, n, shape, dtype):
        self.t = st.sb([128, n] + list(shape), dtype)
        self.tk = tks(n)
        self.n = n
        self.i = 0

    def next(self):
        k = self.i % self.n
        self.i += 1
        return k, self.tk[k]


def st_proj_tok(C, src, srct, W, col0, ncols, cgroups, epi, wranges=None):
    P = C.P
    srcv = src.rearrange("(kc p) t -> p kc t", p=128)
    Wv = W.rearrange("(kc p) n -> p kc n", p=128)
    if wranges is None:
        wranges = [(col0, ncols)]
    ncols = sum(n for _, n in wranges)
    with Stage(C) as st:
        xin = st.sb([128, 2, 8, S], BF16)
        xint = tks(2, 8)
        wt = st.sb([128, 8, ncols], BF16)
        wtt = tks(8)
        pb, pbt = st.psbanks(8)
        env = epi(st)
        for kc in range(8):
            o = 0
            for (c0, n) in wranges:
                P.dma("pool", wt[:, kc, o:o + n], Wv[:, kc, c0:c0 + n], writes=[wtt[kc]])
                o += n
        for s in range(2):
            for kc in range(8):
                P.dma("sp", xin[:, s, kc, :], srcv[:, kc, s * S:(s + 1) * S], reads=[srct], writes=[xint[s][kc]])
        bi = 0
        for s in range(2):
            for tt in range(16):
                for si, cgs in enumerate(cgroups):
                    banks = []
                    for _ in cgs:
                        banks.append(bi % 8)
                        bi += 1
                    for kc in range(8):
                        for (c0, n), b in zip(cgs, banks):
                            P.op("pe", lambda e, b=b, kc=kc, tt=tt, c0=c0, n=n, s=s: e.matmul(pb[b][:, 0:n], lhsT=xin[:, s, kc, tt * 128:(tt + 1) * 128], rhs=wt[:, kc, c0:c0 + n],
                                                                                               start=(kc == 0), stop=(kc == 7)),
                                 reads=[xint[s][kc], wtt[kc]], writes=[pbt[b]])
                    env(s, tt, si, cgs, banks, pb, pbt)


def st_inproj_fm(C, hn, w_in, qT0, kT0, uT, norm=None):
    P = C.P
    groups = [(c * 128,) for c in range(8)] + [(1024 + c * 128,) for c in range(8)] + [(4104 + c * 128, 5128 + c * 128) for c in range(8)]
    with Stage(C) as st:
        pre = None
        if norm is not None:
            pre = alloc_fused(st)
            pre["hooks"] = norm_into(C, st, norm[0], norm[1], norm[2], pre["xin"], pre["xint"], pre["pb"], pre["pbt"], (6, 7), hn=hn, hnt=C.dt["hn"])
        ob = Rot(st, 4, [512], BF16)
        sg = Rot(st, 3, [512], F32)

        def epi(s, half, gi, grp, banks, pb, pbt, alloc):
            for tb in range(2):
                t0 = s * S + half * 1024 + tb * 512
                k, kt = ob.next()
                if gi < 16:
                    b = banks[0][tb]
                    sc = 1.0 if gi < 8 else 1.0 / 16
                    P.op("act", lambda e, b=b, k=k, sc=sc: e.activation(out=ob.t[:, k, :], in_=pb[b][:], func=AF.Identity, scale=sc), reads=[pbt[b]], writes=[kt])
                    dst, dn = (qT0, "qT0") if gi < 8 else (kT0, "kT0")
                    r0 = (gi % 8) * 128
                else:
                    ba, bg = banks[0][tb], banks[1][tb]
                    k2, k2t = sg.next()
                    P.op("act", lambda e, bg=bg, k2=k2: e.activation(out=sg.t[:, k2, :], in_=pb[bg][:], func=AF.Sigmoid), reads=[pbt[bg]], writes=[k2t])
                    P.op("dve", lambda e, ba=ba, k=k, k2=k2: e.tensor_tensor(out=ob.t[:, k, :], in0=pb[ba][:], in1=sg.t[:, k2, :], op=ALU.mult), reads=[pbt[ba], k2t], writes=[kt])
                    dst, dn = uT, "uT"
                    r0 = (gi - 16) * 128
                P.dma("sp", dst[r0:r0 + 128, t0:t0 + 512], ob.t[:, k, :], reads=[kt], writes=[C.dt[dn]])
        proj_fm(C, st, hn, C.dt["hn"], w_in, groups, epi, 8, pre=pre)


def st_inproj_tok(C, hn, w_in, ktok, vtok, so, gates, hnb_d):
    P = C.P
    cg = [[(0, 512), (512, 512), (1024, 512), (1536, 512)], [(2048, 512), (2560, 512), (3072, 8)]]

    def mk(st):
        hnb = st.sb([128, 1024], F32); hnbt = Tk()
        P.dma("sp", hnb[:], hnb_d, writes=[hnbt])
        kst = Rot(st, 2, [1024], BF16)
        vst = Rot(st, 2, [1024], F32)
        ost = Rot(st, 2, [1024], F32)
        gst = Rot(st, 2, [8], F32)

        def env(s, tt, si, cgs, banks, pb, pbt):
            r0 = s * S + tt * 128
            if si == 0:
                k, kt = kst.next()
                for i in range(2):
                    b = banks[i]
                    P.op("act", lambda e, b=b, k=k, i=i: e.activation(out=kst.t[:, k, i * 512:(i + 1) * 512], in_=pb[b][:], func=AF.Identity, scale=1.0 / 16), reads=[pbt[b]], writes=[kt])
                P.dma("sp", ktok[r0:r0 + 128, :], kst.t[:, k, :], reads=[kt], writes=[C.dt["ktok"]])
                k, kt = vst.next()
                for i in range(2):
                    b = banks[2 + i]
                    P.op("dve", lambda e, b=b, k=k, i=i: e.tensor_copy(out=vst.t[:, k, i * 512:(i + 1) * 512], in_=pb[b][:]), reads=[pbt[b]], writes=[kt])
                P.dma("sp", vtok[r0:r0 + 128, :], vst.t[:, k, :], reads=[kt], writes=[C.dt["vtok"]])
            else:
                k, kt = ost.next()
                for i in range(2):
                    b = banks[i]
                    P.op("act", lambda e, b=b, k=k, i=i: e.activation(out=ost.t[:, k, i * 512:(i + 1) * 512], in_=pb[b][:], func=AF.Sigmoid), reads=[pbt[b]], writes=[kt])
                P.op("dve", lambda e, k=k: e.tensor_tensor(out=ost.t[:, k, :], in0=ost.t[:, k, :], in1=hnb[:], op=ALU.mult), reads=[kt, hnbt], writes=[kt])
                P.dma("sp", so[r0:r0 + 128, :], ost.t[:, k, :], reads=[kt], writes=[C.dt["so"]])
                k, kt = gst.next()
                b = banks[2]
                P.op("dve", lambda e, b=b, k=k: e.tensor_copy(out=gst.t[:, k, :], in_=pb[b][:, 0:8]), reads=[pbt[b]], writes=[kt])
                P.dma("sp", gates[r0:r0 + 128, :], gst.t[:, k, :], reads=[kt], writes=[C.dt["gates"]])
        return env
    st_proj_tok(C, hn, C.dt["hn"], w_in, 1024, 3080, cg, mk)


def st_mlstm(C, qT0, kT0, ktok, vtok, so, gates, catT, fb_d, ib_d, hnb_d):
    P = C.P
    dt = C.dt
    with Stage(C) as st:
        tri = st.sb([128, 128], F32); trit = Tk()
        onesf = st.sb([128, 128], F32); onest = Tk()
        ident = st.sb([128, 128], F32); identt = Tk()
        identb = st.sb([128, 128], BF16); identbt = Tk()
        maskT = st.sb([128, 128], F32); maskt = Tk()
        fb = st.sb([128, 64], F32); ib = st.sb([128, 64], F32); cbt = Tk()
        P.op("pool", lambda e: e.memset(onesf[:], 1.0), writes=[onest])
        make_ident(P, ident, identt)
        P.op("pool", lambda e: e.tensor_copy(out=identb[:], in_=ident[:]), reads=[identt], writes=[identbt])
        P.op("pool", lambda e: e.memset(tri[:], 1.0), writes=[trit])
        P.op("pool", lambda e: e.affine_select(out=tri[:], in_=tri[:], pattern=[[1, 128]], compare_op=ALU.is_ge, fill=0.0, base=0, channel_multiplier=-1), reads=[trit], writes=[trit])
        P.op("pool", lambda e: e.tensor_copy(out=maskT[:], in_=tri[:]), reads=[trit], writes=[maskt])
        P.dma("sp", fb[:], fb_d, writes=[cbt])
        P.dma("sp", ib[:], ib_d, writes=[cbt])
        GF = st.sb([128, 16, 8], F32); gft = Tk()
        IGb = st.sb([128, 16, 4], F32); FGb = st.sb([128, 16, 4], F32); Lt = st.sb([128, 16, 4], F32)
        A = st.sb([128, 16, 4], F32); Bn = st.sb([128, 16, 4], F32); BLn = st.sb([128, 16, 4], F32)
        Amax = st.sb([64, 1], F32); Arep = st.sb([64, 128], F32); Abc = st.sb([128, 16, 4], F32)
        M = st.sb([128, 17, 4], F32); MU = st.sb([128, 16, 4], F32)
        Wg = st.sb([128, 16, 4], F32); Gg = st.sb([128, 16, 4], F32); EN = st.sb([128, 16, 4], F32)
        tmp = st.sb([128, 16, 4], F32)
        gt = {n: Tk() for n in "IGb FGb L A Bn BLn Amax Arep Abc M MU W G EN tmp".split()}
        bk0 = st.ps([128, 512])
        pg = [bk0[:, 0:64], bk0[:, 64:128]]
        _pgt = Tk(); pgt = [_pgt, _pgt]
        qTh2 = st.sb([128, 2, 2, S], BF16); qTht2 = tks(2, 2)
        kTh2 = st.sb([128, 2, 2, S], BF16); kTht2 = tks(2, 2)
        kth2 = st.sb([128, 2, 16, 256], BF16); ktht2 = tks(2)
        vh2 = st.sb([128, 2, 16, 256], F32); vht2 = tks(2)
        soh2 = st.sb([128, 2, 16, 256], F32); soht2 = tks(2)
        vwa = st.sb([128, 16, 257], BF16); vwat = tks(16); vwaot = Tk()
        Cst2 = st.sb([128, 2, 2, 257], F32); Cstt2 = tks(2, 2)
        Cbf2 = st.sb([128, 2, 2, 257], BF16); Cbft2 = tks(2)
        WmT = Rot(st, 3, [128], BF16)
        numsb2 = st.sb([128, 2, 16, 257], F32); numt2 = tks(2, 16)
        sqs_t = st.sb([128, 16, 256], BF16); sqst_t = Tk()
        sm2 = [{n: (st.sb([128, 16], F32), Tk()) for n in "ssn d1 r t1 sc".split()} for _ in range(2)]
        hmt2 = st.sb([128, 2, 16, 256], BF16); hmtt2 = tks(2, 16)
        hmT = st.sb([128, 2, S], BF16); hmTt = tks(2, 4)
        pS = [st.ps([128, 512])[:, 0:128] for _ in range(2)]; pSt = tks(2)
        pN = [st.ps([128, 512])[:, 0:257] for _ in range(2)]; pNt = tks(2)
        pU = [st.ps([128, 512])[:, 0:257] for _ in range(2)]; pUt = tks(2)
        pX = st.ps([128, 1024], BF16)[:, 0:512]; pXt = Tk()
        gv = gates.rearrange("(s c p) g -> s p c g", p=128, c=16)

        def load_head(idx):
            s, h = idx // 4, idx % 4
            hb = idx % 2
            fm = lambda a: a[h * 256:(h + 1) * 256, s * S:(s + 1) * S].rearrange("(dc p) t -> p dc t", p=128)
            tokv = lambda a: a[s * S:(s + 1) * S, h * 256:(h + 1) * 256].rearrange("(c p) d -> p c d", p=128)
            P.dma("sp", qTh2[:, hb], fm(qT0), reads=[dt["qT0"]], writes=qTht2[hb])
            P.dma("sp", kTh2[:, hb], fm(kT0), reads=[dt["kT0"]], writes=kTht2[hb])
            P.dma("sp", kth2[:, hb], tokv(ktok), reads=[dt["ktok"]], writes=[ktht2[hb]])
            P.dma("sp", vh2[:, hb], tokv(vtok), reads=[dt["vtok"]], writes=[vht2[hb]])

        def load_so(idx):
            s, h = idx // 4, idx % 4
            hb = idx % 2
            tokv = lambda a: a[s * S:(s + 1) * S, h * 256:(h + 1) * 256].rearrange("(c p) d -> p c d", p=128)
            P.dma("sp", soh2[:, hb], tokv(so), reads=[dt["so"]], writes=[soht2[hb]])
        load_head(0)
        load_so(0)
        pending = []
        for s in range(2):
            while pending:
                pending.pop(0)()
            P.dma("sp", GF[:], gv[s], reads=[dt["gates"]], writes=[gft])
            fb3 = fb[:].rearrange("p (c h) -> p c h", h=4)
            ib3 = ib[:].rearrange("p (c h) -> p c h", h=4)
            P.op("dve", lambda e: e.tensor_tensor(out=FGb[:], in0=GF[:, :, 4:8], in1=fb3, op=ALU.add), reads=[gft, cbt], writes=[gt["FGb"]])
            P.op("dve", lambda e: e.tensor_tensor(out=IGb[:], in0=GF[:, :, 0:4], in1=ib3, op=ALU.add), reads=[gft, cbt], writes=[gt["IGb"]])
            P.op("act", lambda e: e.activation(out=Lt[:], in_=FGb[:], func=AF.Exp, scale=-1.0), reads=[gt["FGb"]], writes=[gt["L"]])
            P.op("dve", lambda e: e.tensor_scalar(out=Lt[:], in0=Lt[:], scalar1=1.0, scalar2=1.0, op0=ALU.add, op1=ALU.mult), reads=[gt["L"]], writes=[gt["L"]])
            P.op("act", lambda e: e.activation(out=Lt[:], in_=Lt[:], func=AF.Ln), reads=[gt["L"]], writes=[gt["L"]])
            L2 = Lt[:].rearrange("p c h -> p (c h)")
            P.op("pe", lambda e: e.matmul(pg[0][:], lhsT=tri[:], rhs=L2, start=True, stop=True), reads=[trit, gt["L"]], writes=[pgt[0]])
            P.op("pe", lambda e: e.matmul(pg[1][:], lhsT=onesf[:], rhs=L2, start=True, stop=True), reads=[onest, gt["L"]], writes=[pgt[1]])
            f2 = lambda t: t[:].rearrange("p c h -> p (c h)")
            P.op("dve", lambda e: e.tensor_copy(out=f2(Bn), in_=pg[0][:]), reads=[pgt[0]], writes=[gt["Bn"], pgt[0]])
            P.op("dve", lambda e: e.tensor_copy(out=f2(BLn), in_=pg[1][:]), reads=[pgt[1]], writes=[gt["BLn"], pgt[1]])
            P.op("dve", lambda e: e.tensor_tensor(out=A[:], in0=IGb[:], in1=Bn[:], op=ALU.add), reads=[gt["IGb"], gt["Bn"]], writes=[gt["A"]])
            P.op("act", lambda e: e.activation(out=tmp[:], in_=A[:], func=AF.Exp), reads=[gt["A"]], writes=[gt["tmp"]])
            P.op("pe", lambda e: e.matmul(pg[0][:], lhsT=onesf[:], rhs=f2(tmp), start=True, stop=True), reads=[onest, gt["tmp"]], writes=[pgt[0]])
            P.op("act", lambda e: e.activation(out=f2(Abc), in_=pg[0][:], func=AF.Ln), reads=[pgt[0]], writes=[gt["Abc"], pgt[0]])
            P.op("pool", lambda e: e.memset(M[:, 0, :], 0.0), writes=[gt["M"]])
            for c in range(16):
                P.op("dve", lambda e, c=c: e.tensor_tensor(out=MU[:, c, :], in0=M[:, c, :], in1=Abc[:, c, :], op=ALU.max), reads=[gt["M"], gt["Abc"]], writes=[gt["MU"]])
                P.op("dve", lambda e, c=c: e.tensor_tensor(out=M[:, c + 1, :], in0=MU[:, c, :], in1=BLn[:, c, :], op=ALU.subtract), reads=[gt["MU"], gt["BLn"]], writes=[gt["M"]])
            P.op("dve", lambda e: e.tensor_tensor(out=tmp[:], in0=A[:], in1=MU[:], op=ALU.subtract), reads=[gt["A"], gt["MU"]], writes=[gt["tmp"]])
            P.op("act", lambda e: e.activation(out=Wg[:], in_=tmp[:], func=AF.Exp), reads=[gt["tmp"]], writes=[gt["W"]])
            P.op("dve", lambda e: e.tensor_tensor(out=tmp[:], in0=M[:, 0:16, :], in1=MU[:], op=ALU.subtract), reads=[gt["M"], gt["MU"], gt["W"]], writes=[gt["tmp"]])
            P.op("act", lambda e: e.activation(out=Gg[:], in_=tmp[:], func=AF.Exp), reads=[gt["tmp"]], writes=[gt["G"]])
            P.op("dve", lambda e: e.tensor_tensor(out=tmp[:], in0=Bn[:], in1=MU[:], op=ALU.subtract), reads=[gt["Bn"], gt["MU"], gt["G"]], writes=[gt["tmp"]])
            P.op("act", lambda e: e.activation(out=EN[:], in_=tmp[:], func=AF.Exp), reads=[gt["tmp"]], writes=[gt["EN"]])
            def do_head(s, h, pending):
                idx = s * 4 + h
                hb = idx % 2
                if idx + 1 < 8:
                    load_head(idx + 1)
                qTh = qTh2[:, hb]; kTh = kTh2[:, hb]; kth = kth2[:, hb]; vh = vh2[:, hb]; soh = soh2[:, hb]
                qTht = qTht2[hb]; kTht = kTht2[hb]; ktht = ktht2[hb]; vht = vht2[hb]; soht = soht2[hb]
                numsb = numsb2[:, hb]; numt = numt2[hb]; hmt = hmt2[:, hb]; hmtt = hmtt2[hb]; sm = sm2[hb]
                so_done = []
                sqs = sqs_t; sqst = sqst_t
                for c in range(16):
                    P.op("act", lambda e, c=c, h=h: e.activation(out=vwa[:, c, 0:256], in_=vh[:, c, :], func=AF.Identity, scale=Wg[:, c, h:h + 1]),
                         reads=[vht, gt["W"]], writes=[vwat[c]])
                P.op("dve", lambda e, h=h: e.tensor_copy(out=vwa[:, :, 256:257], in_=Wg[:, :, h:h + 1]), reads=[gt["W"]], writes=[vwaot])
                for c in range(16):
                    cs = slice(c * 128, (c + 1) * 128)
                    pp = c % 2
                    sp_ = c % 2
                    for dc in range(2):
                        P.op("pe", lambda e, dc=dc, cs=cs, sp_=sp_: e.matmul(pS[sp_][:], lhsT=kTh[:, dc, cs], rhs=qTh[:, dc, cs], start=(dc == 0), stop=(dc == 1)),
                             reads=[kTht[dc], qTht[dc]], writes=[pSt[sp_]])
                    k, kt = WmT.next()
                    P.op("dve", lambda e, k=k, sp_=sp_: e.tensor_tensor(out=WmT.t[:, k, :], in0=pS[sp_][:], in1=maskT[:], op=ALU.mult), reads=[pSt[sp_], maskt], writes=[kt])
                    if c < 15:
                        for dc in range(2):
                            P.op("pe", lambda e, dc=dc, c=c: e.matmul(pU[dc][:], lhsT=kth[:, c, dc * 128:(dc + 1) * 128], rhs=vwa[:, c, :], start=True, stop=True),
                                 reads=[ktht, vwat[c], vwaot], writes=[pUt[dc]])
                            if c == 0:
                                P.op("dve", lambda e, dc=dc, pp=pp: e.tensor_copy(out=Cst2[:, pp, dc, :], in_=pU[dc][:]), reads=[pUt[dc]], writes=[Cstt2[pp][dc]])
                            else:
                                P.op("dve", lambda e, dc=dc, c=c, h=h, pp=pp: e.scalar_tensor_tensor(out=Cst2[:, pp, dc, :], in0=Cst2[:, 1 - pp, dc, :], scalar=Gg[:, c, h:h + 1], in1=pU[dc][:], op0=ALU.mult, op1=ALU.add),
                                     reads=[pUt[dc], Cstt2[1 - pp][dc], gt["G"]], writes=[Cstt2[pp][dc]])
                        P.op("act", lambda e, c=c, h=h, pp=pp: e.activation(out=Cbf2[:, 1 - pp], in_=Cst2[:, pp], func=AF.Identity, scale=Gg[:, c + 1, h:h + 1]),
                             reads=[gt["G"]] + Cstt2[pp], writes=[Cbft2[1 - pp]])
                    if c > 0:
                        for dc in range(2):
                            P.op("pe", lambda e, dc=dc, cs=cs, sp_=sp_, pp=pp: e.matmul(pN[sp_][:], lhsT=qTh[:, dc, cs], rhs=Cbf2[:, pp, dc, :], start=(dc == 0), stop=False),
                                 reads=[qTht[dc], Cbft2[pp]], writes=[pNt[sp_]])
                    P.op("pe", lambda e, k=k, c=c, sp_=sp_: e.matmul(pN[sp_][:], lhsT=WmT.t[:, k, :], rhs=vwa[:, c, :], start=(c == 0), stop=True),
                         reads=[kt, vwat[c], vwaot], writes=[pNt[sp_]])
                    P.op("act", lambda e, c=c, sp_=sp_: e.activation(out=numsb[:, c, :], in_=pN[sp_][:], func=AF.Copy), reads=[pNt[sp_]], writes=[numt[c]])
                    if c % 2 == 1 and pending:
                        pending.pop(0)()
                        if not pending and idx + 1 < 8 and not so_done:
                            load_so(idx + 1)
                            so_done.append(1)
                ssn, d1, rr, t1, sc = (sm[n][0] for n in "ssn d1 r t1 sc".split())
                ssnt, d1t, rt_, t1t, sct = (sm[n][1] for n in "ssn d1 r t1 sc".split())
                steps = []

                def s0():
                    P.op("act", lambda e: e.activation(out=sqs[:], in_=numsb[:, :, 0:256], func=AF.Square), reads=numt, writes=[sqst])
                    P.op("act", lambda e: e.activation(out=d1[:], in_=numsb[:, :, 256], func=AF.Abs), reads=numt, writes=[d1t])
                steps.append(s0)

                def s1():
                    P.op("dve", lambda e: e.tensor_reduce(out=ssn[:], in_=sqs[:], axis=AX.X, op=ALU.add), reads=[sqst], writes=[ssnt])
                    P.op("dve", lambda e: e.tensor_tensor(out=d1[:], in0=d1[:], in1=EN[:, :, h], op=ALU.max), reads=[d1t, gt["EN"]], writes=[d1t])
                    P.op("dve", lambda e: e.reciprocal(out=rr[:], in_=d1[:]), reads=[d1t], writes=[rt_])
                    P.op("dve", lambda e: e.tensor_tensor(out=t1[:], in0=ssn[:], in1=rr[:], op=ALU.mult), reads=[ssnt, rt_], writes=[t1t])
                    P.op("dve", lambda e: e.tensor_tensor(out=t1[:], in0=t1[:], in1=rr[:], op=ALU.mult), reads=[t1t, rt_], writes=[t1t])
                    P.op("dve", lambda e: e.tensor_scalar(out=t1[:], in0=t1[:], scalar1=1.0 / 256, scalar2=EPS, op0=ALU.mult, op1=ALU.add), reads=[t1t], writes=[t1t])
                    P.op("act", lambda e: e.activation(out=t1[:], in_=t1[:], func=AF.Sqrt), reads=[t1t], writes=[t1t])
                    P.op("dve", lambda e: e.reciprocal(out=t1[:], in_=t1[:]), reads=[t1t], writes=[t1t])
                    P.op("dve", lambda e: e.tensor_tensor(out=sc[:], in0=t1[:], in1=rr[:], op=ALU.mult), reads=[t1t, rt_], writes=[sct])
                steps.append(s1)

                def mk_stt(c0):
                    def f():
                        for c in range(c0, c0 + 4):
                            P.op("dve", lambda e, c=c: e.scalar_tensor_tensor(out=hmt[:, c, :], in0=numsb[:, c, 0:256], scalar=sc[:, c:c + 1], in1=soh[:, c, :], op0=ALU.mult, op1=ALU.mult),
                                 reads=[numt[c], sct, soht], writes=[hmtt[c]])
                    return f
                for c0 in range(0, 16, 4):
                    steps.append(mk_stt(c0))

                def mk_tr(vc):
                    def f():
                        for c4 in range(4):
                            for ci in range(4):
                                c = c4 * 4 + ci
                                P.op("pe", lambda e, c=c, ci=ci, vc=vc: e.transpose(pX[:, ci * 128:(ci + 1) * 128], hmt[:, c, vc * 128:(vc + 1) * 128], identb[:]),
                                     reads=[hmtt[c], identbt], writes=[pXt])
                            P.op("act", lambda e, c4=c4, vc=vc: e.activation(out=hmT[:, vc, c4 * 512:(c4 + 1) * 512], in_=pX[:], func=AF.Copy), reads=[pXt], writes=[hmTt[vc][c4]])
                        r0 = h * 256 + vc * 128
                        P.dma("sp", catT[r0:r0 + 128, s * S:(s + 1) * S], hmT[:, vc, :], reads=hmTt[vc], writes=[dt["catT"]])
                    return f
                steps.append(mk_tr(0))
                steps.append(mk_tr(1))
                while pending:
                    pending.pop(0)()
                if idx + 1 < 8 and not so_done:
                    load_so(idx + 1)
                return steps
            for h in range(4):
                pending = do_head(s, h, pending)
        while pending:
            pending.pop(0)()


def st_conf(C, uT, catT, cw_d, cb_d, lg_d, lb_d):
    P = C.P
    dt = C.dt
    uv = uT.rearrange("(kc p) t -> p kc t", p=128)
    with Stage(C) as st:
        ident = st.sb([128, 128], F32); identt = Tk()
        onesf = st.sb([128, 128], F32); onest = Tk()
        make_ident(P, ident, identt)
        P.op("pool", lambda e: e.memset(onesf[:], 1.0), writes=[onest])
        cw = st.sb([128, 8, 31], F32); cb = st.sb([128, 8], F32); lg = st.sb([128, 8], F32); lb = st.sb([128, 8], F32); ct = Tk()
        P.dma("sp", cw[:], cw_d, writes=[ct]); P.dma("sp", cb[:], cb_d, writes=[ct]); P.dma("sp", lg[:], lg_d, writes=[ct]); P.dma("sp", lb[:], lb_d, writes=[ct])
        U = st.sb([128, 2, 8, 30 + S], BF16); Ut = tks(2, 8); Upt = Tk()
        P.op("pool", lambda e: e.memset(U[:, :, :, 0:30], 0.0), writes=[Upt])
        Dg = st.sb([128, 8, 31, 128], BF16); Dgt = tks(8)
        Y = st.sb([128, 2, 8, 512], F32); Yt = tks(2, 8)
        Ysq = st.sb([128, 8, 512], F32); Ysqt = tks(8)
        mean = st.sb([128, 512], F32); meant = Tk()
        rstd = st.sb([128, 512], F32); rstdt = Tk()
        ysum = st.sb([128, 2, 512], F32); ysumt = tks(2)
        t1 = Rot(st, 2, [512], F32)
        ob = Rot(st, 3, [512], BF16)
        pb, pbt = st.psbanks(6)
        for s in range(2):
            for cc in range(8):
                P.dma("sp", U[:, s, cc, 30:], uv[:, cc, s * S:(s + 1) * S], reads=[dt["uT"]], writes=[Ut[s][cc]])
        for cc in range(8):
            for k in range(31):
                eng = ("act", "dve")[k % 2]
                if eng == "act":
                    P.op(eng, lambda e, k=k, cc=cc: e.activation(out=Dg[:, cc, k, :], in_=ident[:], func=AF.Identity, scale=cw[:, cc, k:k + 1]),
                         reads=[identt, ct], writes=[Dgt[cc]])
                else:
                    P.op(eng, lambda e, k=k, cc=cc: e.tensor_scalar(out=Dg[:, cc, k, :], in0=ident[:], scalar1=cw[:, cc, k:k + 1], scalar2=None, op0=ALU.mult),
                         reads=[identt, ct], writes=[Dgt[cc]])
        bi_ = [0]

        def conv_cc(it, cc):
            s, tb = it // 4, it % 4
            yp = it % 2
            b = bi_[0] % 4
            bi_[0] += 1
            for k in range(31):
                P.op("pe", lambda e, k=k, cc=cc, tb=tb, b=b, s=s: e.matmul(pb[b][:], lhsT=Dg[:, cc, k, :], rhs=U[:, s, cc, tb * 512 + k: tb * 512 + k + 512], start=(k == 0), stop=(k == 30)),
                     reads=[Dgt[cc], Ut[s][cc], Upt], writes=[pbt[b]])
            P.op("act", lambda e, cc=cc, b=b, yp=yp: e.activation(out=Y[:, yp, cc, :], in_=pb[b][:], func=AF.Identity, bias=cb[:, cc:cc + 1]),
                 reads=[pbt[b], ct], writes=[Yt[yp][cc]])

        def ln_a(it):
            yp = it % 2
            for cc in range(8):
                P.op("act", lambda e, cc=cc, yp=yp: e.activation(out=Ysq[:, cc, :], in_=Y[:, yp, cc, :], func=AF.Square), reads=[Yt[yp][cc]], writes=[Ysqt[cc]])
            P.op("dve", lambda e, yp=yp: e.tensor_tensor(out=ysum[:, 0, :], in0=Y[:, yp, 0, :], in1=Y[:, yp, 1, :], op=ALU.add), reads=[Yt[yp][0], Yt[yp][1]], writes=[ysumt[0]])
            for cc in range(2, 8):
                P.op("dve", lambda e, yp=yp, cc=cc: e.tensor_tensor(out=ysum[:, 0, :], in0=ysum[:, 0, :], in1=Y[:, yp, cc, :], op=ALU.add), reads=[Yt[yp][cc], ysumt[0]], writes=[ysumt[0]])
            P.op("dve", lambda e: e.tensor_tensor(out=ysum[:, 1, :], in0=Ysq[:, 0, :], in1=Ysq[:, 1, :], op=ALU.add), reads=[Ysqt[0], Ysqt[1]], writes=[ysumt[1]])
            for cc in range(2, 8):
                P.op("dve", lambda e, cc=cc: e.tensor_tensor(out=ysum[:, 1, :], in0=ysum[:, 1, :], in1=Ysq[:, cc, :], op=ALU.add), reads=[Ysqt[cc], ysumt[1]], writes=[ysumt[1]])

        def ln_b(it):
            P.op("pe", lambda e: e.matmul(pb[4][:], lhsT=onesf[:], rhs=ysum[:, 0, :], start=True, stop=True), reads=[onest, ysumt[0]], writes=[pbt[4]])
            P.op("pe", lambda e: e.matmul(pb[5][:], lhsT=onesf[:], rhs=ysum[:, 1, :], start=True, stop=True), reads=[onest, ysumt[1]], writes=[pbt[5]])
            P.op("dve", lambda e: e.tensor_scalar(out=mean[:], in0=pb[4][:], scalar1=1.0 / D, scalar2=None, op0=ALU.mult), reads=[pbt[4]], writes=[meant])
            P.op("dve", lambda e: e.tensor_tensor(out=rstd[:], in0=mean[:], in1=mean[:], op=ALU.mult), reads=[meant], writes=[rstdt])
            P.op("dve", lambda e: e.scalar_tensor_tensor(out=rstd[:], in0=pb[5][:], scalar=1.0 / D, in1=rstd[:], op0=ALU.mult, op1=ALU.subtract), reads=[pbt[5], rstdt], writes=[rstdt])
            P.op("dve", lambda e: e.tensor_scalar(out=rstd[:], in0=rstd[:], scalar1=EPS, scalar2=None, op0=ALU.add), reads=[rstdt], writes=[rstdt])
            P.op("act", lambda e: e.activation(out=rstd[:], in_=rstd[:], func=AF.Sqrt), reads=[rstdt], writes=[rstdt])
            P.op("dve", lambda e: e.reciprocal(out=rstd[:], in_=rstd[:]), reads=[rstdt], writes=[rstdt])

        def ln_c(it, ccs):
            s, tb = it // 4, it % 4
            yp = it % 2
            for cc in ccs:
                k, kt = t1.next()
                P.op("dve", lambda e, cc=cc, k=k, yp=yp: e.tensor_tensor(out=t1.t[:, k, :], in0=Y[:, yp, cc, :], in1=mean[:], op=ALU.subtract), reads=[Yt[yp][cc], meant], writes=[kt])
                P.op("dve", lambda e, k=k: e.tensor_tensor(out=t1.t[:, k, :], in0=t1.t[:, k, :], in1=rstd[:], op=ALU.mult), reads=[kt, rstdt], writes=[kt])
                k2, k2t = ob.next()
                P.op("act", lambda e, cc=cc, k=k, k2=k2: e.activation(out=ob.t[:, k2, :], in_=t1.t[:, k, :], func=AF.Silu, scale=lg[:, cc:cc + 1], bias=lb[:, cc:cc + 1]),
                     reads=[kt, ct], writes=[k2t])
                r0 = 1024 + cc * 128
                P.dma("sp", catT[r0:r0 + 128, s * S + tb * 512: s * S + (tb + 1) * 512], ob.t[:, k2, :], reads=[k2t], writes=[dt["catT"]])

        for cc in range(8):
            conv_cc(0, cc)
        for it in range(8):
            nxt = it + 1 if it + 1 < 8 else None
            parts = [lambda it=it: ln_a(it), lambda it=it: ln_b(it), lambda it=it: ln_c(it, range(0, 4)), lambda it=it: ln_c(it, range(4, 8))]
            for pi, part in enumerate(parts):
                if nxt is not None:
                    conv_cc(nxt, 2 * pi)
                    conv_cc(nxt, 2 * pi + 1)
                part()


def st_projres(C, src, srcn, W, KC, Rin, rinn, Rout, routn):
    P = C.P
    dt = C.dt
    groups = [(c * 128,) for c in range(8)]
    with Stage(C) as st:
        LA = 5
        rt = Rot(st, LA + 2, [512], F32)
        ot = Rot(st, 4, [512], F32)
        tiles = [(s, gi, half, tb) for s in range(2) for gi in range(8) for half in range(2) for tb in range(2)]
        issued = {}
        ptr = [0]

        def issue(i):
            s, gi, half, tb = tiles[i]
            t0 = s * S + half * 1024 + tb * 512
            k, kt = rt.next()
            P.dma("sp", rt.t[:, k, :], Rin[gi * 128:(gi + 1) * 128, t0:t0 + 512], reads=[dt[rinn]], writes=[kt])
            issued[i] = (k, kt)

        def epi(s, half, gi, grp, banks, pb, pbt, alloc):
            for tb in range(2):
                i = ptr[0]
                ptr[0] += 1
                assert tiles[i] == (s, gi, half, tb)
                if i == 0:
                    for j in range(min(LA, len(tiles))):
                        issue(j)
                if i + LA < len(tiles):
                    issue(i + LA)
                k, kt = issued.pop(i)
                t0 = s * S + half * 1024 + tb * 512
                b = banks[0][tb]
                k2, k2t = ot.next()
                P.op("dve", lambda e, b=b, k=k, k2=k2: e.tensor_tensor(out=ot.t[:, k2, :], in0=pb[b][:], in1=rt.t[:, k, :], op=ALU.add), reads=[pbt[b], kt], writes=[k2t])
                P.dma("act", Rout[gi * 128:(gi + 1) * 128, t0:t0 + 512], ot.t[:, k2, :], reads=[k2t], writes=[dt[routn]])
        proj_fm(C, st, src, dt[srcn], W, groups, epi, KC)


def st_ffnup(C, hn, w_gu, aT, fcw_d, fcb_d, norm=None, outproj=None):
    P = C.P
    dt = C.dt
    NJ = DFF // 128
    groups = [(j * 128, DFF + j * 128) for j in range(NJ)]
    with Stage(C) as st:
        pre = None
        if norm is not None:
            pre = alloc_fused(st)
            pre["hooks"] = norm_into(C, st, norm[0], norm[1], norm[2], pre["xin"], pre["xint"], pre["pb"], pre["pbt"], (6, 7), outproj=outproj)
        ident = st.sb([128, 128], F32); identt = Tk()
        make_ident(P, ident, identt)
        fcw = st.sb([128, NJ, 3], F32); fcb = st.sb([128, NJ], F32); ct = Tk()
        P.dma("sp", fcw[:], fcw_d, writes=[ct]); P.dma("sp", fcb[:], fcb_d, writes=[ct])
        Gsb = st.sb([128, 2, 2 + S], BF16); Gt = tks(2, 4); Gpt = Tk()
        P.op("pool", lambda e: e.memset(Gsb[:, :, 0:2], 0.0), writes=[Gpt])
        cv = Rot(st, 3, [512], F32)
        sg = Rot(st, 3, [512], F32)
        ob = Rot(st, 3, [512], BF16)
        state = {"n": 0}

        def epi(s, half, gi, grp, banks, pb, pbt, alloc):
            if half == 0:
                state["n"] += 1
            gp = state["n"] % 2
            for tb in range(2):
                q = half * 2 + tb
                b = banks[0][tb]
                P.op("act", lambda e, b=b, gp=gp, q=q: e.activation(out=Gsb[:, gp, 2 + q * 512: 2 + (q + 1) * 512], in_=pb[b][:], func=AF.Copy), reads=[pbt[b]], writes=[Gt[gp][q]])
            tl = []
            for tb in range(2):
                q = half * 2 + tb
                rd = [Gt[gp][q], Gpt, ct] + ([Gt[gp][q - 1]] if q > 0 else [])
                kc_, kct = cv.next()
                k1, k1t = sg.next()
                k2, k2t = ob.next()
                tl.append((tb, q, rd, kc_, kct, k1, k1t, k2, k2t))
            for (tb, q, rd, kc_, kct, k1, k1t, k2, k2t) in tl:
                P.op("act", lambda e, kc_=kc_, gp=gp, q=q, gi=gi: e.activation(out=cv.t[:, kc_, :], in_=Gsb[:, gp, q * 512: q * 512 + 512], func=AF.Identity,
                                                                               scale=fcw[:, gi, 0:1], bias=fcb[:, gi:gi + 1]), reads=rd, writes=[kct])
            for (tb, q, rd, kc_, kct, k1, k1t, k2, k2t) in tl:
                for k in (1, 2):
                    P.op("dve", lambda e, kc_=kc_, gp=gp, q=q, gi=gi, k=k: e.scalar_tensor_tensor(out=cv.t[:, kc_, :], in0=Gsb[:, gp, q * 512 + k: q * 512 + k + 512], scalar=fcw[:, gi, k:k + 1],
                                                                                                  in1=cv.t[:, kc_, :], op0=ALU.mult, op1=ALU.add), reads=rd + [kct], writes=[kct])
            for (tb, q, rd, kc_, kct, k1, k1t, k2, k2t) in tl:
                P.op("act", lambda e, kc_=kc_, k1=k1: e.activation(out=sg.t[:, k1, :], in_=cv.t[:, kc_, :], func=AF.Silu), reads=[kct], writes=[k1t])
            for (tb, q, rd, kc_, kct, k1, k1t, k2, k2t) in tl:
                bu = banks[1][tb]
                t0 = s * S + q * 512
                P.op("dve", lambda e, bu=bu, k1=k1, k2=k2: e.tensor_tensor(out=ob.t[:, k2, :], in0=pb[bu][:], in1=sg.t[:, k1, :], op=ALU.mult), reads=[pbt[bu], k1t], writes=[k2t])
                P.dma("sp", aT[gi * 128:(gi + 1) * 128, t0:t0 + 512], ob.t[:, k2, :], reads=[k2t], writes=[dt["aT"]])
        proj_fm(C, st, hn, dt["hn"], w_gu, groups, epi, 8, pre=pre)


def st_qk(C, hn, w_qkv, qT1, kT1, cos_d, sin_d, norm=None, v1=None):
    P = C.P
    dt = C.dt
    blocks = [(g, j, h) for g in range(3) for j in range(2) for h in range(4)]
    groups = [(g * 1536 + j * 512 + h * 128,) for (g, j, h) in blocks]
    with Stage(C) as st:
        pre = None
        if norm is not None:
            pre = alloc_fused(st)
            if v1 is None:
                pre["hooks"] = norm_into(C, st, norm[0], norm[1], norm[2], pre["xin"], pre["xint"], pre["pb"], pre["pbt"], (6, 7), hn=hn, hnt=dt["hn"])
            else:
                pre["hooks"] = norm_into(C, st, norm[0], norm[1], norm[2], pre["xin"], pre["xint"], pre["pb"], pre["pbt"], (6, 7))
        if v1 is not None:
            Wv_ = w_qkv.rearrange("(kc p) n -> p kc n", p=128)
            wv = st.sb([128, 8, 1536], BF16); wvt = tks(8)
            for kc in range(8):
                for g in range(3):
                    P.dma("pool", wv[:, kc, g * 512:(g + 1) * 512], Wv_[:, kc, g * 1536 + 1024: g * 1536 + 1536], writes=[wvt[kc]])
            vst = Rot(st, 2, [1536], BF16)
        cos2 = st.sb([128, S], F32); sin2 = st.sb([128, S], F32); ct = Tk()
        P.dma("sp", cos2[:], cos_d, writes=[ct]); P.dma("sp", sin2[:], sin_d, writes=[ct])
        pif = st.sb([128, 128], F32); pib = st.sb([128, 128], BF16); pit = Tk()
        P.op("pool", lambda e: e.memset(pif[:], 0.0), writes=[pit])
        P.op("pool", lambda e: e.affine_select(out=pif[:], in_=pif[:], pattern=[[-1, 128]], compare_op=ALU.not_equal, fill=-1.0, base=-64, channel_multiplier=1), reads=[pit], writes=[pit])
        P.op("pool", lambda e: e.affine_select(out=pif[:], in_=pif[:], pattern=[[-1, 128]], compare_op=ALU.not_equal, fill=1.0, base=64, channel_multiplier=1), reads=[pit], writes=[pit])
        P.op("pool", lambda e: e.tensor_copy(out=pib[:], in_=pif[:]), reads=[pit], writes=[pit])
        qb = Rot(st, 4, [512], BF16)
        ta = Rot(st, 2, [512], F32)
        tb_ = Rot(st, 2, [512], F32)
        ob = Rot(st, 3, [512], BF16)

        def epi(s, half, gi, grp, banks, pb, pbt, alloc):
            g, j, h = blocks[gi]
            dst, dn = (qT1, "qT1") if j == 0 else (kT1, "kT1")
            r0 = (g * 4 + h) * 128
            items = []
            for tb in range(2):
                p0 = half * 1024 + tb * 512
                b0 = banks[0][tb]
                k0, k0t = qb.next()
                P.op("act", lambda e, b0=b0, k0=k0: e.activation(out=qb.t[:, k0, :], in_=pb[b0][:], func=AF.Copy), reads=[pbt[b0]], writes=[k0t])
                items.append((p0, b0, k0, k0t))

            def finish(items=items, dst=dst, dn=dn, r0=r0, s=s, pb=pb, pbt=pbt, alloc=alloc):
                for (p0, b0, k0, k0t) in items:
                    b1 = alloc()
                    P.op("pe", lambda e, b1=b1, k0=k0: e.matmul(pb[b1][:], lhsT=pib[:], rhs=qb.t[:, k0, :], start=True, stop=True), reads=[pit, k0t], writes=[pbt[b1]])
                    k1, k1t = ta.next()
                    k2, k2t = tb_.next()
                    k3, k3t = ob.next()
                    P.op("dve", lambda e, b0=b0, k1=k1, p0=p0: e.tensor_tensor(out=ta.t[:, k1, :], in0=pb[b0][:], in1=cos2[:, p0:p0 + 512], op=ALU.mult), reads=[pbt[b0], ct, k0t], writes=[k1t])
                    P.op("dve", lambda e, b1=b1, k2=k2, p0=p0: e.tensor_tensor(out=tb_.t[:, k2, :], in0=pb[b1][:], in1=sin2[:, p0:p0 + 512], op=ALU.mult), reads=[pbt[b1], ct], writes=[k2t])
                    P.op("dve", lambda e, k1=k1, k2=k2, k3=k3: e.tensor_tensor(out=ob.t[:, k3, :], in0=ta.t[:, k1, :], in1=tb_.t[:, k2, :], op=ALU.add), reads=[k1t, k2t], writes=[k3t])
                    P.dma("sp", dst[r0:r0 + 128, s * S + p0: s * S + p0 + 512], ob.t[:, k3, :], reads=[k3t], writes=[dt[dn]])
            pre.setdefault("mid", []).append(finish)
        proj_fm(C, st, hn, dt["hn"], w_qkv, groups, epi, 8, nslot=1, pre=pre)
        while pre.get("mid"):
            pre["mid"].pop(0)()
        if v1 is not None:
            xin, xint, pb, pbt = pre["xin"], pre["xint"], pre["pb"], pre["pbt"]
            bi = 0
            for s in range(2):
                for tt in range(16):
                    banks = [(bi + g) % 8 for g in range(3)]
                    bi += 3
                    for kc in range(8):
                        for g in range(3):
                            b = banks[g]
                            P.op("pe", lambda e, b=b, kc=kc, tt=tt, g=g, s=s: e.matmul(pb[b][:], lhsT=xin[:, s, kc, tt * 128:(tt + 1) * 128], rhs=wv[:, kc, g * 512:(g + 1) * 512],
                                                                                       start=(kc == 0), stop=(kc == 7)),
                                 reads=[xint[s][kc], wvt[kc]], writes=[pbt[b]])
                    r0 = s * S + tt * 128
                    k, kt = vst.next()
                    for g in range(3):
                        b = banks[g]
                        if g == 1:
                            P.op("dve", lambda e, b=b, k=k, g=g: e.tensor_copy(out=vst.t[:, k, g * 512:(g + 1) * 512], in_=pb[b][:]), reads=[pbt[b]], writes=[kt])
                        else:
                            P.op("act", lambda e, b=b, k=k, g=g: e.activation(out=vst.t[:, k, g * 512:(g + 1) * 512], in_=pb[b][:], func=AF.Copy), reads=[pbt[b]], writes=[kt])
                    P.dma("sp", v1[r0:r0 + 128, :], vst.t[:, k, :], reads=[kt], writes=[dt["v1"]])


def st_v1(C, hn, w_qkv, v1):
    P = C.P

    def mk(st):
        vst = Rot(st, 3, [1536], BF16)

        def env(s, tt, si, cgs, banks, pb, pbt):
            r0 = s * S + tt * 128
            k, kt = vst.next()
            for g in range(3):
                b = banks[g]
                if g == 1:
                    P.op("dve", lambda e, b=b, k=k, g=g: e.tensor_copy(out=vst.t[:, k, g * 512:(g + 1) * 512], in_=pb[b][:]), reads=[pbt[b]], writes=[kt])
                else:
                    P.op("act", lambda e, b=b, k=k, g=g: e.activation(out=vst.t[:, k, g * 512:(g + 1) * 512], in_=pb[b][:], func=AF.Copy), reads=[pbt[b]], writes=[kt])
            P.dma("sp", v1[r0:r0 + 128, :], vst.t[:, k, :], reads=[kt], writes=[C.dt["v1"]])
        return env
    st_proj_tok(C, hn, C.dt["hn"], w_qkv, 0, 0, [[(0, 512), (512, 512), (1024, 512)]], mk,
                wranges=[(g * 1536 + 1024, 512) for g in range(3)])


PATTERNS = ((128, 1), (512, 4), (2048, 16))


def st_attn(C, qT1, kT1, v1, numg):
    P = C.P
    dt = C.dt
    with Stage(C) as st:
        mask2 = st.sb([128, 256], BF16); mt = Tk()
        P.op("pool", lambda e: e.memset(mask2[:], 1.0), writes=[mt])
        P.op("pool", lambda e: e.affine_select(out=mask2[:, 0:128], in_=mask2[:, 0:128], pattern=[[1, 128]], compare_op=ALU.is_ge, fill=0.0, base=0, channel_multiplier=-1), reads=[mt], writes=[mt])
        P.op("pool", lambda e: e.affine_select(out=mask2[:, 128:256], in_=mask2[:, 128:256], pattern=[[-1, 128]], compare_op=ALU.is_ge, fill=0.0, base=0, channel_multiplier=1), reads=[mt], writes=[mt])
        qTg2 = st.sb([128, 2, 4, S], BF16); qt2 = tks(2, 4)
        kTg2 = st.sb([128, 2, 4, S], BF16); kt2 = tks(2, 4)
        vaug2 = st.sb([128, 2, 16, 4, 129], BF16); vt2 = tks(2, 16); vot = Tk()
        for bb_ in range(2):
            P.op("pool", lambda e, bb_=bb_: e.memset(vaug2[:, bb_, :, :, 128:129], 1.0), writes=[vot])
        Eb = st.sb([128, 4, 3, 256], BF16); Ebt = tks(4, 3)
        E3 = [0]
        si_ = [0]
        osb = Rot(st, 3, [516], F32)
        pS = [st.ps([128, 512])[:, 0:256] for _ in range(4)]; pSt = tks(4)
        pO = [[st.ps([128, 512])[:, 0:258] for _ in range(2)] for _ in range(2)]; pOt = tks(2, 4)
        si = 0
        sg_list = [(s, g) for s in range(2) for g in range(3)]

        def load_sg(idx):
            s, g = sg_list[idx]
            dil = PATTERNS[g][1]
            nb = 16 // dil
            bb = idx % 2
            qv = qT1[g * 512:(g + 1) * 512, s * S:(s + 1) * S].rearrange("(h p) t -> p h t", p=128)
            kv = kT1[g * 512:(g + 1) * 512, s * S:(s + 1) * S].rearrange("(h p) t -> p h t", p=128)
            P.dma("sp", qTg2[:, bb, :, :], qv, reads=[dt["qT1"]], writes=qt2[bb])
            P.dma("sp", kTg2[:, bb, :, :], kv, reads=[dt["kT1"]], writes=kt2[bb])
            vv = v1[s * S:(s + 1) * S, g * 512:(g + 1) * 512].rearrange("(b i r) (h d) -> r b i h d", i=128, r=dil, h=4)
            for r in range(dil):
                for b in range(nb):
                    P.dma("sp", vaug2[:, bb, r * nb + b, :, 0:128], vv[r, b], reads=[dt["v1"]], writes=[vt2[bb][r * nb + b]])
        load_sg(0)
        for idx, (s, g) in enumerate(sg_list):
            if True:
                window, dil = PATTERNS[g]
                nb = 16 // dil
                if idx + 1 < len(sg_list):
                    load_sg(idx + 1)
                bb = idx % 2
                qTg = qTg2[:, bb]; kTg = kTg2[:, bb]; vaug = vaug2[:, bb]
                qt = qt2[bb]; kt_ = kt2[bb]; vt = vt2[bb]
                nv = numg[g, s * S:(s + 1) * S, :].rearrange("(b i r) f -> r b i f", i=128, r=dil)
                blist = [(r, b) for r in range(dil) for b in range(nb)]

                def phase1(fi, kTg=kTg, qTg=qTg, qt=qt, kt_=kt_, dil=dil, nb=nb, blist=blist):
                    r, b = blist[fi]
                    nq = 256 if b < nb - 1 else 128
                    k0 = r + dil * b * 128
                    e3 = E3[0] % 3
                    E3[0] += 1
                    for h in range(4):
                        sp_ = si_[0] % 4
                        si_[0] += 1
                        ks = slice(k0, k0 + dil * 127 + 1, dil)
                        qs = slice(k0, k0 + dil * (nq - 1) + 1, dil)
                        P.op("pe", lambda e, sp_=sp_, h=h, ks=ks, qs=qs, nq=nq: e.matmul(pS[sp_][:, 0:nq], lhsT=kTg[:, h, ks], rhs=qTg[:, h, qs], start=True, stop=True),
                             reads=[kt_[h], qt[h]], writes=[pSt[sp_]])
                        kt = Ebt[h][e3]
                        P.op("act", lambda e, sp_=sp_, h=h, e3=e3, nq=nq: e.activation(out=Eb[:, h, e3, 0:nq], in_=pS[sp_][:, 0:nq], func=AF.Exp, scale=float(128 ** -0.5)), reads=[pSt[sp_]], writes=[kt])
                        P.op("dve", lambda e, h=h, e3=e3, nq=nq: e.tensor_tensor(out=Eb[:, h, e3, 0:nq], in0=Eb[:, h, e3, 0:nq], in1=mask2[:, 0:nq], op=ALU.mult), reads=[kt, mt], writes=[kt])
                    return e3

                def phase2(fi, e3, e3prev, vaug=vaug, vt=vt, nv=nv, nb=nb, blist=blist):
                    r, b = blist[fi]
                    rb = r * nb + b
                    for h in range(4):
                        hp, hh = h // 2, h % 2
                        if b > 0:
                            P.op("pe", lambda e, b=b, hp=hp, hh=hh, rb=rb, h=h: e.matmul(pO[b % 2][hp][:, hh * 129:(hh + 1) * 129], lhsT=Eb[:, h, e3prev, 128:256], rhs=vaug[:, rb - 1, h, :], start=True, stop=False),
                                 reads=[Ebt[h][e3prev], vt[rb - 1], vot], writes=[pOt[b % 2][h]])
                        P.op("pe", lambda e, b=b, hp=hp, hh=hh, rb=rb, h=h: e.matmul(pO[b % 2][hp][:, hh * 129:(hh + 1) * 129], lhsT=Eb[:, h, e3, 0:128], rhs=vaug[:, rb, h, :], start=(b == 0), stop=True),
                             reads=[Ebt[h][e3], vt[rb], vot], writes=[pOt[b % 2][h]])
                    k, kt = osb.next()
                    for hp in range(2):
                        if hp == 0:
                            P.op("act", lambda e, k=k, b=b, hp=hp: e.activation(out=osb.t[:, k, hp * 258:(hp + 1) * 258], in_=pO[b % 2][hp][:], func=AF.Copy),
                                 reads=[pOt[b % 2][2 * hp], pOt[b % 2][2 * hp + 1]], writes=[kt])
                        else:
                            P.op("dve", lambda e, k=k, b=b, hp=hp: e.tensor_copy(out=osb.t[:, k, hp * 258:(hp + 1) * 258], in_=pO[b % 2][hp][:]),
                                 reads=[pOt[b % 2][2 * hp], pOt[b % 2][2 * hp + 1]], writes=[kt])
                    P.dma("sp", nv[r, b], osb.t[:, k, :], reads=[kt], writes=[dt["numg"]])

                es = {0: phase1(0)}
                for fi in range(len(blist)):
                    if fi + 1 < len(blist):
                        es[fi + 1] = phase1(fi + 1)
                    phase2(fi, es[fi], es.get(fi - 1, 0))


def st_merge(C, numg, oT):
    P = C.P
    dt = C.dt
    with Stage(C) as st:
        identb = st.sb([128, 128], BF16); identbt = Tk()
        identf = st.sb([128, 128], F32); identft = Tk()
        make_ident(P, identf, identft)
        P.op("pool", lambda e: e.tensor_copy(out=identb[:], in_=identf[:]), reads=[identft], writes=[identbt])
        n3 = Rot(st, 4, [3, 516], F32)
        acc = Rot(st, 3, [516], F32)
        rr = Rot(st, 3, [4], F32)
        otok = Rot(st, 3, [512], BF16)
        oTs = st.sb([128, 4, S], BF16); oTt = tks(4, 16)
        pX = [st.ps([128, 1024], BF16)[:, 0:512] for _ in range(2)]; pXt = tks(2)
        for s in range(2):
            for tt in range(16):
                r0 = s * S + tt * 128
                k, kt = n3.next()
                P.dma("sp", n3.t[:, k, :, :], numg[:, r0:r0 + 128, :].rearrange("g t f -> t g f"), reads=[dt["numg"]], writes=[kt])
                k2, k2t = acc.next()
                P.op("dve", lambda e, k=k, k2=k2: e.tensor_tensor(out=acc.t[:, k2, :], in0=n3.t[:, k, 0, :], in1=n3.t[:, k, 1, :], op=ALU.add), reads=[kt], writes=[k2t])
                P.op("dve", lambda e, k=k, k2=k2: e.tensor_tensor(out=acc.t[:, k2, :], in0=acc.t[:, k2, :], in1=n3.t[:, k, 2, :], op=ALU.add), reads=[kt, k2t], writes=[k2t])
                k3, k3t = rr.next()
                a4 = acc.t[:, k2, :].rearrange("p (h f) -> p h f", f=129)
                P.op("dve", lambda e, a4=a4, k3=k3: e.reciprocal(out=rr.t[:, k3, :], in_=a4[:, :, 128]), reads=[k2t], writes=[k3t])
                k4, k4t = otok.next()
                for h in range(4):
                    P.op("dve", lambda e, a4=a4, k3=k3, k4=k4, h=h: e.tensor_scalar(out=otok.t[:, k4, h * 128:(h + 1) * 128], in0=a4[:, h, 0:128], scalar1=rr.t[:, k3, h:h + 1], scalar2=None, op0=ALU.mult),
                         reads=[k2t, k3t], writes=[k4t])
                px = tt % 2
                for h in range(4):
                    P.op("pe", lambda e, k4=k4, h=h, px=px: e.transpose(pX[px][:, h * 128:(h + 1) * 128], otok.t[:, k4, h * 128:(h + 1) * 128], identb[:]), reads=[k4t, identbt], writes=[pXt[px]])
                P.op("act", lambda e, px=px, tt=tt: e.activation(out=oTs[:, :, tt * 128:(tt + 1) * 128], in_=pX[px][:].rearrange("p (h t) -> p h t", h=4), func=AF.Copy),
                     reads=[pXt[px]], writes=[oTt[h_][tt] for h_ in range(4)])
            for h in range(4):
                P.dma("sp", oT[h * 128:(h + 1) * 128, s * S:(s + 1) * S], oTs[:, h, :], reads=oTt[h], writes=[dt["oT"]])


INPUT_NAMES = {"xT", "w_in", "w_out0", "w_gu0", "w_gu1", "w_down0", "w_down1", "w_qkv", "w_out1", "mixn", "ffnn", "finn",
               "fb", "ib", "hnb", "cw", "cb", "lg", "lb", "fcw", "fcb", "cos2", "sin2"}

ALL_STAGES = ["n_inproj_fm", "inproj_tok", "mlstm", "conf", "out0", "n_ffnup0", "ffndown0",
              "n_qkv", "attn", "merge", "on_ffnup1", "ffndown1", "normf"]


def run_stages(C, stages):
    if stages is None:
        stages = ALL_STAGES
    d = C.dram
    xT = d("xT", [D, T], F32)
    w_in = d("w_in", [D, 6152], F32)
    w_out0 = d("w_out0", [2048, D], F32)
    w_gu = [d("w_gu0", [D, 2 * DFF], F32), d("w_gu1", [D, 2 * DFF], F32)]
    w_down = [d("w_down0", [DFF, D], F32), d("w_down1", [DFF, D], F32)]
    w_qkv = d("w_qkv", [D, 4608], F32)
    w_out1 = d("w_out1", [512, D], F32)
    mixn = d("mixn", [2, 128, 8], F32)
    ffnn = d("ffnn", [2, 128, 8], F32)
    finn = d("finn", [128, 8], F32)
    fb = d("fb", [128, 64], F32)
    ib = d("ib", [128, 64], F32)
    hnb = d("hnb", [128, 1024], F32)
    cw = d("cw", [128, 8, 31], F32)
    cb = d("cb", [128, 8], F32)
    lg = d("lg", [128, 8], F32)
    lb = d("lb", [128, 8], F32)
    fcw = d("fcw", [2, 128, 22, 3], F32)
    fcb = d("fcb", [2, 128, 22], F32)
    cos2 = d("cos2", [128, S], F32)
    sin2 = d("sin2", [128, S], F32)
    hn = d("hn", [D, T], BF16)
    qT0 = d("qT0", [D, T], BF16)
    kT0 = d("kT0", [D, T], BF16)
    uT = d("uT", [D, T], BF16)
    ktok = d("ktok", [T, D], BF16)
    vtok = d("vtok", [T, D], F32)
    so = d("so", [T, D], F32)
    gates = d("gates", [T, 8], F32)
    catT = d("catT", [2048, T], BF16)
    r1T = d("r1T", [D, T], F32)
    aT = d("aT", [DFF, T], BF16)
    r2T = d("r2T", [D, T], F32)
    qT1 = d("qT1", [1536, T], BF16)
    kT1 = d("kT1", [1536, T], BF16)
    v1 = d("v1", [T, 1536], BF16)
    numg = d("numg", [3, T, 516], F32)
    oT = d("oT", [512, T], BF16)
    r3T = d("r3T", [D, T], F32)
    r4T = d("r4T", [D, T], F32)
    outT = d("outT", [D, T], F32)
    dt = C.dt
    for sname in stages:
        if sname == "n_inproj_fm":
            st_inproj_fm(C, hn, w_in, qT0, kT0, uT, norm=(xT, dt["xT"], mixn[0]))
        elif sname == "n_ffnup0":
            st_ffnup(C, hn, w_gu[0], aT, fcw[0], fcb[0], norm=(r1T, dt["r1T"], ffnn[0]))
        elif sname == "n_qk":
            st_qk(C, hn, w_qkv, qT1, kT1, cos2, sin2, norm=(r2T, dt["r2T"], mixn[1]))
        elif sname == "n_qkv":
            st_qk(C, hn, w_qkv, qT1, kT1, cos2, sin2, norm=(r2T, dt["r2T"], mixn[1]), v1=v1)
        elif sname == "n_ffnup1":
            st_ffnup(C, hn, w_gu[1], aT, fcw[1], fcb[1], norm=(r3T, dt["r3T"], ffnn[1]))
        elif sname == "on_ffnup1":
            st_ffnup(C, hn, w_gu[1], aT, fcw[1], fcb[1], norm=(r2T, dt["r2T"], ffnn[1]), outproj=(oT, dt["oT"], w_out1, r3T, dt["r3T"]))
        elif sname == "norm0":
            st_norm(C, xT, dt["xT"], mixn[0], hn, dt["hn"])
        elif sname == "inproj_fm":
            st_inproj_fm(C, hn, w_in, qT0, kT0, uT)
        elif sname == "inproj_tok":
            st_inproj_tok(C, hn, w_in, ktok, vtok, so, gates, hnb)
        elif sname == "mlstm":
            st_mlstm(C, qT0, kT0, ktok, vtok, so, gates, catT, fb, ib, hnb)
        elif sname == "conf":
            st_conf(C, uT, catT, cw, cb, lg, lb)
        elif sname == "out0":
            st_projres(C, catT, "catT", w_out0, 16, xT, "xT", r1T, "r1T")
        elif sname == "norm1":
            st_norm(C, r1T, dt["r1T"], ffnn[0], hn, dt["hn"])
        elif sname == "ffnup0":
            st_ffnup(C, hn, w_gu[0], aT, fcw[0], fcb[0])
        elif sname == "ffndown0":
            st_projres(C, aT, "aT", w_down[0], 22, r1T, "r1T", r2T, "r2T")
        elif sname == "norm2":
            st_norm(C, r2T, dt["r2T"], mixn[1], hn, dt["hn"])
        elif sname == "qk":
            st_qk(C, hn, w_qkv, qT1, kT1, cos2, sin2)
        elif sname == "v1":
            st_v1(C, hn, w_qkv, v1)
        elif sname == "attn":
            st_attn(C, qT1, kT1, v1, numg)
        elif sname == "merge":
            st_merge(C, numg, oT)
        elif sname == "out1":
            st_projres(C, oT, "oT", w_out1, 4, r2T, "r2T", r3T, "r3T")
        elif sname == "norm3":
            st_norm(C, r3T, dt["r3T"], ffnn[1], hn, dt["hn"])
        elif sname == "ffnup1":
            st_ffnup(C, hn, w_gu[1], aT, fcw[1], fcb[1])
        elif sname == "ffndown1":
            st_projres(C, aT, "aT", w_down[1], 22, r3T, "r3T", r4T, "r4T")
        elif sname == "normf":
            st_norm(C, r4T, dt["r4T"], finn, outT, dt["outT"], final=True)
        else:
            raise ValueError(sname)


def build_program(stages=None, ext_in=(), ext_out=("outT",)):
    nc = bass.Bass("TRN2", target_bir_lowering=False)
    top = contextlib.ExitStack()
    with top:
        P = Prog(nc, top)
        C = Ctx(nc, P, ext_in=set(ext_in) | INPUT_NAMES, ext_out=ext_out)
        run_stages(C, stages)
    return nc


def host_params(inputs):
    f = lambda a: np.ascontiguousarray(np.asarray(a, dtype=np.float32))
    p = {}
    p["w_in"] = f(inputs["ab_w_in"][0])
    p["w_out0"] = f(inputs["ab_w_out"][0])
    p["w_gu0"] = f(inputs["ffn_w_gu"][0]); p["w_gu1"] = f(inputs["ffn_w_gu"][1])
    p["w_down0"] = f(inputs["ffn_w_down"][0]); p["w_down1"] = f(inputs["ffn_w_down"][1])
    p["w_qkv"] = f(inputs["c_w_qkv"][0])
    p["w_out1"] = f(inputs["c_w_out"][0])
    pk = lambda v: f(np.asarray(v).reshape(-1, 128).T)
    p["mixn"] = f(np.stack([pk(inputs["mix_norm"][l]) for l in range(2)]))
    p["ffnn"] = f(np.stack([pk(inputs["ffn_norm"][l]) for l in range(2)]))
    p["finn"] = pk(inputs["final_norm"])
    p["fb"] = f(np.broadcast_to(np.tile(np.asarray(inputs["ab_f_bias"][0]), 16)[None, :], (128, 64)))
    p["ib"] = f(np.broadcast_to(np.tile(np.asarray(inputs["ab_i_bias"][0]), 16)[None, :], (128, 64)))
    p["hnb"] = f(np.broadcast_to(np.asarray(inputs["ab_head_norm"][0])[None, :], (128, 1024)))
    p["cw"] = f(np.asarray(inputs["ab_conv_w"][0]).T.reshape(8, 128, 31).transpose(1, 0, 2))
    p["cb"] = pk(inputs["ab_conv_b"][0]); p["lg"] = pk(inputs["ab_ln_g"][0]); p["lb"] = pk(inputs["ab_ln_b"][0])
    p["fcw"] = f(np.stack([np.asarray(inputs["ffn_conv_w"][l]).T.reshape(22, 128, 3).transpose(1, 0, 2) for l in range(2)]))
    p["fcb"] = f(np.stack([pk(inputs["ffn_conv_b"][l]) for l in range(2)]))
    pos = np.arange(S, dtype=np.float32)
    inv = (10000.0 ** (-np.arange(0, 128, 2, dtype=np.float32) / 128)).astype(np.float32)
    ang = (pos[None, :] * inv[:, None]).astype(np.float32)
    p["cos2"] = f(np.concatenate([np.cos(ang), np.cos(ang)], 0))
    p["sin2"] = f(np.concatenate([np.sin(ang), np.sin(ang)], 0))
    return p


_CACHE = {}


def kernel(**inputs):
    x = np.asarray(inputs["x"], dtype=np.float32)
    p = host_params(inputs)
    if "nc" not in _CACHE:
        _CACHE["nc"] = build_program()
    nc = _CACHE["nc"]
    in_maps = []
    for c in range(NCORES):
        m = dict(p)
        m["xT"] = np.ascontiguousarray(x[2 * c:2 * c + 2].reshape(T, D).T)
        in_maps.append(m)
    res = run_bass_kernel_spmd(nc, in_maps, core_ids=list(range(NCORES)))
    out = np.empty((16, S, D), dtype=np.float32)
    for c in range(NCORES):
        out[2 * c:2 * c + 2] = np.asarray(res.results[c]["outT"]).T.reshape(2, S, D)
    return out
```
